# Optimizing a Trainium2 kernel written in Bass

```python
import math
import jax, jax.numpy as jnp
from jax import lax
import numpy as np

D_MODEL = 1024
BATCH = 4
SEQ = 4096
DEPTH = 2
DEC_BATCH = 32
DEC_SEQ = 8
PAST_LEN = 8192
PAGE_SIZE = 128

N_EVEN = (DEPTH + 1) // 2
N_ODD = DEPTH // 2
A_WIDTH = D_MODEL // 2
CHUNK = 128
A_GROUPS = 4
A_GROUP_DIM = A_WIDTH // A_GROUPS
B_HEADS = 4
B_QK_DIM = 64
B_V_DIM = 2 * B_QK_DIM
B_WIDTH = B_HEADS * B_V_DIM
IN_EVEN_WIDTH = 2 * A_WIDTH + 2 * B_HEADS * 2 * B_QK_DIM + B_WIDTH
Q_BLOCK = 128
SCALE = B_QK_DIM ** -0.5
SSM_GROUP = 16
SSM_GROUPS = D_MODEL // SSM_GROUP
SSM_STATE = 64
D_FF = ((8 * D_MODEL // 3 + 255) // 256) * 256
EPS = 1e-6

kernel_name = 'hybrid_sgu_diffattn_s5_step'


def rmsnorm(x, g):
    xf = x.astype(jnp.float32)
    y = xf * lax.rsqrt(jnp.mean(xf * xf, axis=-1, keepdims=True) + EPS)
    return (y * g.astype(jnp.float32)).astype(x.dtype)


def swiglu(h, w_in, w_out):
    gate, up = jnp.split(h @ w_in, 2, axis=-1)
    return (jax.nn.silu(gate) * up) @ w_out


def even_project(h, w_in, sgu_norm):
    b, t, _ = h.shape
    p = h @ w_in
    uv = jax.nn.gelu(p[..., :2 * A_WIDTH])
    u = uv[..., :A_WIDTH]
    v = rmsnorm(uv[..., A_WIDTH:], sgu_norm)
    o = 2 * A_WIDTH
    qk = B_HEADS * 2 * B_QK_DIM
    q = p[..., o:o + qk].reshape(b, t, B_HEADS, 2 * B_QK_DIM)
    k = p[..., o + qk:o + 2 * qk].reshape(b, t, B_HEADS, 2 * B_QK_DIM)
    vb = p[..., o + 2 * qk:].reshape(b, t, B_HEADS, B_V_DIM)
    return u, v, q, k, vb


def spatial_gate(v, sgu_w, sgu_b):
    t = v.shape[2]
    w = jnp.where(jnp.tril(jnp.ones((t, t), dtype=bool)), sgu_w[:, :t, :t], 0)
    vg = v.reshape(v.shape[:3] + (A_GROUPS, A_GROUP_DIM))
    out = jnp.einsum('gts,bnsgd->bntgd', w, vg) + sgu_b[:, :t].T[:, :, None]
    return out.reshape(v.shape)


def diff_lambda(lq1, lk1, lq2, lk2, lam_init):
    f32 = jnp.float32
    e1 = jnp.exp(jnp.sum(lq1.astype(f32) * lk1.astype(f32)))
    e2 = jnp.exp(jnp.sum(lq2.astype(f32) * lk2.astype(f32)))
    return e1 - e2 + lam_init


def diff_scores(q, k):
    s1 = jnp.einsum('bqhd,bkhd->bhqk', q[..., :B_QK_DIM], k[..., :B_QK_DIM])
    s2 = jnp.einsum('bqhd,bkhd->bhqk', q[..., B_QK_DIM:], k[..., B_QK_DIM:])
    return s1.astype(jnp.float32), s2.astype(jnp.float32)


def diff_attn_prompt(q, k, v, lam):
    b, s = q.shape[:2]
    q = q * SCALE
    k_pos = jnp.arange(s)

    def block(i):
        start = i * Q_BLOCK
        qb = lax.dynamic_slice_in_dim(q, start, Q_BLOCK, axis=1)
        s1, s2 = diff_scores(qb, k)
        mask = (start + jnp.arange(Q_BLOCK))[:, None] >= k_pos[None, :]
        p1 = jax.nn.softmax(jnp.where(mask, s1, -jnp.inf), axis=-1)
        p2 = jax.nn.softmax(jnp.where(mask, s2, -jnp.inf), axis=-1)
        return jnp.einsum('bhqk,bkhd->bqhd', (p1 - lam * p2).astype(v.dtype), v)

    o = lax.map(block, jnp.arange(s // Q_BLOCK))
    return o.transpose(1, 0, 2, 3, 4).reshape(b, s, B_HEADS, B_V_DIM)


def diff_attn_sample(q, k_new, v_new, k_past, v_past, lam):
    t = q.shape[1]
    n_past = k_past.shape[1]
    q = q * SCALE
    p1s, p2s = diff_scores(q, k_past)
    n1, n2 = diff_scores(q, k_new)
    mask = jnp.tril(jnp.ones((t, t), dtype=bool))
    p1 = jax.nn.softmax(jnp.concatenate([p1s, jnp.where(mask, n1, -jnp.inf)], axis=-1), axis=-1)
    p2 = jax.nn.softmax(jnp.concatenate([p2s, jnp.where(mask, n2, -jnp.inf)], axis=-1), axis=-1)
    a = (p1 - lam * p2).astype(v_new.dtype)
    return (jnp.einsum('bhqk,bkhd->bqhd', a[..., :n_past], v_past)
            + jnp.einsum('bhqk,bkhd->bqhd', a[..., n_past:], v_new))


def diff_head_out(o, subln, lam_init):
    b, t = o.shape[:2]
    return (rmsnorm(o, subln) * (1.0 - lam_init)).reshape(b, t, B_WIDTH)


def s5_scan(u, s0_re, s0_im, a_re, a_im, log_dt, b_re, b_im, c_re, c_im, d_skip):
    bsz, t, _ = u.shape
    f32 = jnp.float32
    a = lax.complex(a_re.astype(f32), a_im.astype(f32))
    dt = jnp.exp(log_dt.astype(f32))[:, None]
    a_bar = jnp.exp(a * dt)
    b_bar = ((a_bar - 1.0) / a)[:, :, None] * lax.complex(b_re.astype(f32), b_im.astype(f32))
    c = lax.complex(c_re.astype(f32), c_im.astype(f32))
    uc = u.astype(f32).reshape(bsz, t, SSM_GROUPS, SSM_GROUP)
    bu = jnp.einsum('gpc,btgc->btgp', b_bar, uc.astype(jnp.complex64))
    a_seq = jnp.broadcast_to(a_bar, bu.shape)

    def combine(left, right):
        a_l, b_l = left
        a_r, b_r = right
        return a_r * a_l, a_r * b_l + b_r

    a_cum, s = lax.associative_scan(combine, (a_seq, bu), axis=1)
    s = s + a_cum * lax.complex(s0_re.astype(f32), s0_im.astype(f32))[:, None]
    y = jnp.real(jnp.einsum('gcp,btgp->btgc', c, s)) + d_skip.astype(f32).reshape(SSM_GROUPS, SSM_GROUP) * uc
    s_last = s[:, -1]
    return y.reshape(bsz, t, D_MODEL).astype(u.dtype), jnp.real(s_last), jnp.imag(s_last)


def s5_glu(y, w_glu, b_glu):
    g = jax.nn.gelu(y)
    z = g @ w_glu + b_glu
    return z[..., :D_MODEL] * jax.nn.sigmoid(z[..., D_MODEL:])


def setup_inputs(seed: int = 0) -> dict:
    key = jax.random.key(seed)
    keys = jax.random.split(key, 48)

    def nrm(i, shape, scale=1.0):
        return jax.random.normal(keys[i], shape, jnp.float32) * scale

    n_pages = PAST_LEN // PAGE_SIZE
    n_used = DEC_BATCH * n_pages
    n_phys = n_used + max(1, n_used // 4)
    page_table = jax.random.permutation(keys[0], n_phys)[:n_used].reshape(DEC_BATCH, n_pages).astype(jnp.int32)
    a_im_init = math.pi * jnp.arange(SSM_STATE, dtype=jnp.float32)
    return {
        'x_prompt': nrm(1, (BATCH, SEQ, D_MODEL)),
        'x_sample': nrm(2, (DEC_BATCH, DEC_SEQ, D_MODEL)),
        'cache_k': nrm(3, (N_EVEN, n_phys, PAGE_SIZE, B_HEADS, 2 * B_QK_DIM)),
        'cache_v': nrm(4, (N_EVEN, n_phys, PAGE_SIZE, B_HEADS, B_V_DIM)),
        'page_table': page_table,
        'state_ssm_re': nrm(5, (N_ODD, DEC_BATCH, SSM_GROUPS, SSM_STATE), 0.1),
        'state_ssm_im': nrm(6, (N_ODD, DEC_BATCH, SSM_GROUPS, SSM_STATE), 0.1),
        'norm_mix': 1.0 + nrm(7, (DEPTH, D_MODEL), 0.02),
        'norm_ffn': 1.0 + nrm(8, (DEPTH, D_MODEL), 0.02),
        'norm_final': 1.0 + nrm(9, (D_MODEL,), 0.02),
        'w_in_even': nrm(10, (N_EVEN, D_MODEL, IN_EVEN_WIDTH), D_MODEL ** -0.5),
        'w_out_even': nrm(11, (N_EVEN, A_WIDTH + B_WIDTH, D_MODEL), (A_WIDTH + B_WIDTH) ** -0.5),
        'sgu_norm': 1.0 + nrm(12, (N_EVEN, A_WIDTH), 0.02),
        'sgu_w': nrm(13, (N_EVEN, A_GROUPS, CHUNK, CHUNK), CHUNK ** -0.5),
        'sgu_b': 1.0 + nrm(14, (N_EVEN, A_GROUPS, CHUNK), 0.02),
        'lambda_q1': nrm(15, (N_EVEN, B_QK_DIM), 0.1),
        'lambda_k1': nrm(16, (N_EVEN, B_QK_DIM), 0.1),
        'lambda_q2': nrm(17, (N_EVEN, B_QK_DIM), 0.1),
        'lambda_k2': nrm(18, (N_EVEN, B_QK_DIM), 0.1),
        'attn_subln': 1.0 + nrm(19, (N_EVEN, B_V_DIM), 0.02),
        'ssm_a_re': -0.5 + nrm(20, (N_ODD, SSM_GROUPS, SSM_STATE), 0.01),
        'ssm_a_im': a_im_init + nrm(21, (N_ODD, SSM_GROUPS, SSM_STATE), 0.01),
        'ssm_log_dt': jax.random.uniform(keys[22], (N_ODD, SSM_GROUPS), jnp.float32, math.log(1e-3), math.log(1e-1)),
        'ssm_b_re': nrm(23, (N_ODD, SSM_GROUPS, SSM_STATE, SSM_GROUP), (2 * SSM_GROUP) ** -0.5),
        'ssm_b_im': nrm(24, (N_ODD, SSM_GROUPS, SSM_STATE, SSM_GROUP), (2 * SSM_GROUP) ** -0.5),
        'ssm_c_re': nrm(25, (N_ODD, SSM_GROUPS, SSM_GROUP, SSM_STATE), SSM_STATE ** -0.5),
        'ssm_c_im': nrm(26, (N_ODD, SSM_GROUPS, SSM_GROUP, SSM_STATE), SSM_STATE ** -0.5),
        'ssm_d': nrm(27, (N_ODD, D_MODEL)),
        'w_glu': nrm(28, (N_ODD, D_MODEL, 2 * D_MODEL), D_MODEL ** -0.5),
        'b_glu': nrm(29, (N_ODD, 2 * D_MODEL), 0.02),
        'w_ffn_in': nrm(30, (DEPTH, D_MODEL, 2 * D_FF), D_MODEL ** -0.5),
        'w_ffn_out': nrm(31, (DEPTH, D_FF, D_MODEL), D_FF ** -0.5),
    }


def reference(x_prompt, x_sample, cache_k, cache_v, page_table, state_ssm_re, state_ssm_im,
              norm_mix, norm_ffn, norm_final, w_in_even, w_out_even, sgu_norm, sgu_w, sgu_b,
              lambda_q1, lambda_k1, lambda_q2, lambda_k2, attn_subln,
              ssm_a_re, ssm_a_im, ssm_log_dt, ssm_b_re, ssm_b_im, ssm_c_re, ssm_c_im, ssm_d,
              w_glu, b_glu, w_ffn_in, w_ffn_out):
    xp, xs = x_prompt, x_sample
    kp_rows, vp_rows, ks_rows, vs_rows, chunk_rows = [], [], [], [], []
    sp_re, sp_im, ss_re, ss_im = [], [], [], []
    for l in range(DEPTH):
        hp = rmsnorm(xp, norm_mix[l])
        hs = rmsnorm(xs, norm_mix[l])
        if l % 2 == 0:
            i = l // 2
            lam_init = 0.8 - 0.6 * math.exp(-0.3 * l)
            lam = diff_lambda(lambda_q1[i], lambda_k1[i], lambda_q2[i], lambda_k2[i], lam_init)
            up, vp, qp, kp, vbp = even_project(hp, w_in_even[i], sgu_norm[i])
            us, vs, qs, k_s, vbs = even_project(hs, w_in_even[i], sgu_norm[i])
            b, s = xp.shape[:2]
            a_p = up * spatial_gate(vp.reshape(b, s // CHUNK, CHUNK, A_WIDTH), sgu_w[i], sgu_b[i]).reshape(b, s, A_WIDTH)
            a_s = us * spatial_gate(vs[:, None], sgu_w[i], sgu_b[i])[:, 0]
            b_p = diff_head_out(diff_attn_prompt(qp, kp, vbp, lam), attn_subln[i], lam_init)
            db = xs.shape[0]
            k_past = cache_k[i][page_table].reshape(db, -1, B_HEADS, 2 * B_QK_DIM)
            v_past = cache_v[i][page_table].reshape(db, -1, B_HEADS, B_V_DIM)
            b_s = diff_head_out(diff_attn_sample(qs, k_s, vbs, k_past, v_past, lam), attn_subln[i], lam_init)
            mp = jnp.concatenate([a_p, b_p], axis=-1) @ w_out_even[i]
            ms = jnp.concatenate([a_s, b_s], axis=-1) @ w_out_even[i]
            kp_rows.append(kp)
            vp_rows.append(vbp)
            ks_rows.append(k_s)
            vs_rows.append(vbs)
            chunk_rows.append(vs)
        else:
            j = l // 2
            zeros = jnp.zeros((xp.shape[0], SSM_GROUPS, SSM_STATE), jnp.float32)
            ssm_params = (ssm_a_re[j], ssm_a_im[j], ssm_log_dt[j], ssm_b_re[j], ssm_b_im[j],
                          ssm_c_re[j], ssm_c_im[j], ssm_d[j])
            y_p, p_re, p_im = s5_scan(hp, zeros, zeros, *ssm_params)
            y_s, s_re, s_im = s5_scan(hs, state_ssm_re[j], state_ssm_im[j], *ssm_params)
            mp = s5_glu(y_p, w_glu[j], b_glu[j])
            ms = s5_glu(y_s, w_glu[j], b_glu[j])
            sp_re.append(p_re)
            sp_im.append(p_im)
            ss_re.append(s_re)
            ss_im.append(s_im)
        xp = xp + mp
        xs = xs + ms
        xp = xp + swiglu(rmsnorm(xp, norm_ffn[l]), w_ffn_in[l], w_ffn_out[l])
        xs = xs + swiglu(rmsnorm(xs, norm_ffn[l]), w_ffn_in[l], w_ffn_out[l])
    y_prompt = rmsnorm(xp, norm_final)
    y_sample = rmsnorm(xs, norm_final)
    return (y_prompt, y_sample, jnp.stack(kp_rows), jnp.stack(vp_rows), jnp.stack(ks_rows),
            jnp.stack(vs_rows), jnp.stack(chunk_rows), jnp.stack(sp_re), jnp.stack(sp_im),
            jnp.stack(ss_re), jnp.stack(ss_im))
```

```python
import contextlib
import math
import numpy as np
import concourse.bass as bass
import concourse.mybir as mybir
from concourse.bass_utils import run_bass_kernel_spmd

F32 = mybir.dt.float32
BF16 = mybir.dt.bfloat16
I32 = mybir.dt.int32
AF = mybir.ActivationFunctionType
ALU = mybir.AluOpType

PE, ACT, DVE, POOL, SP = "pe", "act", "dve", "pool", "sp"
COMPUTE = (PE, ACT, DVE, POOL)
NRING = {SP: 12, POOL: 8}

D = 1024
NT = 2048
NS = 32
NTOT = NT + NS
NPH = 2560
DFF = 2816
EPS = 1e-6
LAM_INIT = 0.8 - 0.6 * math.exp(-0.3 * 0)
SCALE = 64 ** -0.5
NPAGE = 64
FLAGS = dict(sgu=True, attn=True, outp=True, smp_attn=True, pred=True, ffn=True, ssm=True, glu=True)


class T:
    __slots__ = ("ap", "lw", "rd", "name", "sw")

    def __init__(self, ap, name=""):
        self.ap = ap
        self.lw = None
        self.rd = []
        self.name = name

    def __getitem__(self, idx):
        return self.ap[idx]


class Prog:
    def __init__(self, nc):
        self.nc = nc
        self.streams = {e: [] for e in (PE, ACT, DVE, POOL, SP)}
        self.cnt = {e: 0 for e in COMPUTE}
        self.seen = {e: {} for e in (PE, ACT, DVE, POOL, SP)}
        self.ring_pos = {q: 0 for q in NRING}
        self.ring_cnt = {q: [0] * NRING[q] for q in NRING}
        self.n = 0
        self.limit = FLAGS.get('limit', 10 ** 9)

    def _need(self, eng, events):
        seen = self.seen[eng]
        best = {}
        for ev in events:
            if ev is None:
                continue
            k, v = ev
            if best.get(k, 0) < v:
                best[k] = v
        out = []
        for k, v in best.items():
            if seen.get(k, 0) >= v:
                continue
            seen[k] = v
            out.append((k, v))
        return out

    @staticmethod
    def _deps(reads, writes):
        evs = []
        for t in reads:
            evs.append(t.lw)
        for t in writes:
            evs.append(t.lw)
            evs.extend(t.rd)
        return evs

    @staticmethod
    def _commit(ev, reads, writes):
        for t in reads:
            t.rd.append(ev)
            if len(t.rd) > 16:
                best = {}
                for k, v in t.rd:
                    if best.get(k, 0) < v:
                        best[k] = v
                t.rd = list(best.items())
        for t in writes:
            t.lw = ev
            t.rd = []

    def op(self, eng, fn, reads=(), writes=()):
        self.n += 1
        if self.n > self.limit:
            return None
        waits = self._need(eng, self._deps(reads, writes))
        self.cnt[eng] += 1
        ev = (eng, self.cnt[eng])
        self.streams[eng].append((waits, fn, ev, 1))
        self._commit(ev, reads, writes)
        return ev

    def dma(self, q, fn, reads=(), writes=()):
        self.n += 1
        if self.n > self.limit:
            return None
        n = NRING[q]
        slot = self.ring_pos[q] % n
        self.ring_pos[q] += 1
        key = ("ring", q, slot)
        prev = self.ring_cnt[q][slot]
        evs = self._deps(reads, writes)
        if prev > 0:
            evs.append((key, prev))
        waits = self._need(q, evs)
        self.ring_cnt[q][slot] = prev + 16
        ev = (key, prev + 16)
        self.streams[q].append((waits, fn, ev, 16))
        self._commit(ev, reads, writes)
        return ev

    def mark(self, name):
        if FLAGS.get('marks'):
            print('MARK', name, self.n)

    def barrier_all(self):
        evs = [(e, self.cnt[e]) for e in COMPUTE if self.cnt[e] > 0]
        for q in NRING:
            for s in range(NRING[q]):
                if self.ring_cnt[q][s] > 0:
                    evs.append((("ring", q, s), self.ring_cnt[q][s]))
        for e in (PE, ACT, DVE, POOL, SP):
            waits = self._need(e, evs)
            if waits:
                self.streams[e].append((waits, None, None, 0))

    def emit(self):
        nc = self.nc
        keys = set()
        for e in self.streams:
            for waits, fn, ev, inc in self.streams[e]:
                for k, v in waits:
                    keys.add(k)
                if ev is not None:
                    keys.add(ev[0])
        keys = sorted(keys, key=str)
        with contextlib.ExitStack() as st:
            sems = {}
            for k in keys:
                nm = "s_" + "_".join(str(x) for x in (k if isinstance(k, tuple) else (k,)))
                sems[k] = st.enter_context(nc.semaphore(nm))
            block = st.enter_context(nc.Block())

            def run(stream, eh):
                for waits, fn, ev, inc in stream:
                    for k, v in waits:
                        eh.wait_ge(sems[k], v)
                    if fn is None:
                        continue
                    ins = fn(eh)
                    ins.then_inc(sems[ev[0]], inc)

            @block.tensor
            def _(e):
                run(self.streams[PE], e)

            @block.scalar
            def _(e):
                run(self.streams[ACT], e)

            @block.vector
            def _(e):
                run(self.streams[DVE], e)

            @block.gpsimd
            def _(e):
                run(self.streams[POOL], e)

            @block.sync
            def _(e):
                run(self.streams[SP], e)


class Arena:
    def __init__(self, nc, st, name, nbytes):
        self.base = st.enter_context(nc.sbuf_tensor(name, [128, nbytes // 4], F32))
        self.cap = nbytes
        self.off = 0
        self.name = name

    def reset(self):
        self.off = 0

    def alloc(self, name, free_shape, dt):
        esz = 2 if dt == BF16 else 4
        n = 1
        for s in free_shape:
            n *= s
        nb = (n * esz + 31) // 32 * 32
        assert self.off + nb <= self.cap, f"arena {self.name} overflow at {name}: {self.off}+{nb}>{self.cap}"
        ap = self.base[:, self.off // 4:(self.off + nb) // 4]
        if dt == BF16:
            ap = ap.bitcast(BF16)
        elif dt == I32:
            ap = ap.bitcast(I32)
        ap = ap[:, 0:n]
        if len(free_shape) == 2:
            ap = ap.rearrange("p (a b) -> p a b", a=free_shape[0])
        elif len(free_shape) == 3:
            ap = ap.rearrange("p (a b c) -> p a b c", a=free_shape[0], b=free_shape[1])
        self.off += nb
        return T(ap, name)


def build_program(n_rows_cache):
    nc = bass.Bass("TRN2", target_bir_lowering=False)

    def din(name, shape, dt=F32):
        return nc.dram_tensor(name, list(shape), dt, kind="ExternalInput").ap()

    def dout(name, shape, dt=F32):
        return nc.dram_tensor(name, list(shape), dt, kind="ExternalOutput").ap()

    x_own = din("x_own", [NT, D])
    x_pred = din("x_pred", [NT, D])
    x_smp = din("x_smp", [NS, D])
    pbias_d = din("pbias", [128, 1])
    cache_k = din("cache_k", [n_rows_cache, 512])
    cache_v = din("cache_v", [n_rows_cache, 512])
    ptab = din("ptab", [4, NPAGE], I32)
    st_s = din("st_s", [128, 4, 64])
    st_sw = din("st_sw", [128, 4, 64])
    gam_mix = din("gam_mix", [2, 128, 8])
    gam_ffn = din("gam_ffn", [2, 128, 8])
    gam_fin = din("gam_fin", [128, 8])
    w_in = din("w_in", [D, NPH])
    w_out = din("w_out", [D, D])
    sgu_norm_b = din("sgu_norm", [1, 512])
    sgu_wT = din("sgu_wT", [4, 128, 128])
    sgu_wTs = din("sgu_wTs", [4, 32, 32])
    sgu_b = din("sgu_b", [4, 128])
    sgu_bs = din("sgu_bs", [4, 32])
    lam_d = din("lam", [4, 64])
    subln_d = din("subln", [1, 128])
    a_re = din("a_re", [128, 64])
    a_im = din("a_im", [128, 64])
    logdt = din("logdt", [128, 64])
    b_A = din("b_A", [128, 64, 16])
    b_B = din("b_B", [128, 64, 16])
    c_A = din("c_A", [128, 64, 16])
    c_B = din("c_B", [128, 64, 16])
    d_skip = din("d_skip", [128, 64])
    w_glu = din("w_glu", [D, 2 * D])
    b_glu = din("b_glu", [128, 16])
    w_f1 = din("w_f1", [2, D, 2 * DFF])
    w_f2 = din("w_f2", [2, DFF, D])

    y_p = dout("y_p", [NT, D])
    y_s = dout("y_s", [NS, D])
    kr_p = dout("kr_p", [NT, 512])
    vr_p = dout("vr_p", [NT, 512])
    kr_s = dout("kr_s", [NS, 512])
    vr_s = dout("vr_s", [NS, 512])
    cv_s = dout("cv_s", [NS, 512])
    so_p = dout("so_p", [128, 64])
    so_s = dout("so_s", [128, 4, 64])
    dbg_x = dout("dbg_x", [128, 8, NTOT]) if FLAGS.get("dbg") else None

    P = Prog(nc)
    with contextlib.ExitStack() as st:
        R1 = Arena(nc, st, "R1", 8 * NTOT * 4)
        R2 = Arena(nc, st, "R2", 66560)
        R3 = Arena(nc, st, "R3", 8 * NTOT * 2)
        RW = Arena(nc, st, "RW", 30720)
        RC = Arena(nc, st, "RC", 14336)
        psf = [T(st.enter_context(nc.psum_tensor(f"psf{i}", [128, 512], F32)), f"psf{i}") for i in range(6)]
        psb = [T(st.enter_context(nc.psum_tensor(f"psb{i}", [128, 1024], BF16)), f"psb{i}") for i in range(2)]
        rot = {"f": 0, "b": 0, "n": 4}

        def pf():
            rot["f"] = (rot["f"] + 1) % rot["n"]
            return psf[rot["f"]]

        def pbk():
            rot["b"] = (rot["b"] + 1) % 2
            return psb[rot["b"]]

        identf = RC.alloc("identf", [128], F32)
        identb = RC.alloc("identb", [128], BF16)
        trimask = RC.alloc("trimask", [128], F32)
        trimb = RC.alloc("trimb", [128], BF16)
        identsw = RC.alloc("identsw", [128], BF16)
        onesb = RC.alloc("onesb", [128], BF16)
        epsT = RC.alloc("epsT", [1], F32)
        gmix = RC.alloc("gmix", [2, 8], F32)
        gffn = RC.alloc("gffn", [2, 8], F32)
        gfin = RC.alloc("gfin", [8], F32)
        pbias = RC.alloc("pbias", [1], F32)
        zbias = RC.alloc("zbias", [1], F32)
        lamneg = RC.alloc("lamneg", [1], F32)
        sublnb = RC.alloc("sublnb", [128], F32)
        sgunb = RC.alloc("sgunb", [512], F32)
        bglu = RC.alloc("bglu", [16], F32)
        sgub = RC.alloc("sgub", [4, 128], BF16)
        sgubs = RC.alloc("sgubs", [4, 32], BF16)
        wmT = RC.alloc("wmT", [4, 128], BF16)
        wmTs = RC.alloc("wmTs", [4, 32], BF16)
        tmpc = RC.alloc("tmpc", [512], F32)
        tmpc2 = RC.alloc("tmpc2", [512], F32)

        P.op(POOL, lambda e: e.iota(tmpc[:, 0:128], pattern=[[1, 128]], base=0, channel_multiplier=-1,
                                    allow_small_or_imprecise_dtypes=True), writes=[tmpc])
        P.op(DVE, lambda e: e.tensor_single_scalar(out=identf[:], in_=tmpc[:, 0:128], scalar=0.0, op=ALU.is_equal),
             reads=[tmpc], writes=[identf])
        P.op(DVE, lambda e: e.tensor_single_scalar(out=trimask[:], in_=tmpc[:, 0:128], scalar=0.0, op=ALU.is_ge),
             reads=[tmpc], writes=[trimask])
        P.op(DVE, lambda e: e.tensor_copy(out=identb[:], in_=identf[:]), reads=[identf], writes=[identb])
        P.op(DVE, lambda e: e.tensor_copy(out=trimb[:], in_=trimask[:]), reads=[trimask], writes=[trimb])
        P.op(DVE, lambda e: e.tensor_tensor(out=tmpc2[:, 0:128], in0=tmpc[:, 0:128], in1=tmpc[:, 0:128], op=ALU.mult), reads=[tmpc], writes=[tmpc2])
        P.op(DVE, lambda e: e.tensor_single_scalar(out=identsw[:], in_=tmpc2[:, 0:128], scalar=4096.0, op=ALU.is_equal),
             reads=[tmpc2], writes=[identsw])
        P.op(DVE, lambda e: e.memset(onesb[:], 1.0), writes=[onesb])
        P.op(DVE, lambda e: e.memset(epsT[:], EPS), writes=[epsT])
        P.op(DVE, lambda e: e.memset(zbias[:], 0.0), writes=[zbias])
        P.dma(SP, lambda e: e.dma_start(out=gmix[:], in_=gam_mix.rearrange("l p k -> p l k")), writes=[gmix])
        P.dma(SP, lambda e: e.dma_start(out=gffn[:], in_=gam_ffn.rearrange("l p k -> p l k")), writes=[gffn])
        P.dma(SP, lambda e: e.dma_start(out=gfin[:], in_=gam_fin), writes=[gfin])
        P.dma(SP, lambda e: e.dma_start(out=pbias[:], in_=pbias_d), writes=[pbias])
        P.dma(SP, lambda e: e.dma_start(out=bglu[:], in_=b_glu), writes=[bglu])
        P.dma(SP, lambda e: e.dma_start(out=sublnb[:], in_=subln_d.partition_broadcast(128).rearrange("p a f -> p (a f)")),
              writes=[sublnb])
        P.dma(SP, lambda e: e.dma_start(out=sgunb[:], in_=sgu_norm_b.partition_broadcast(128).rearrange("p a f -> p (a f)")),
              writes=[sgunb])
        P.op(DVE, lambda e: e.tensor_scalar_mul(out=sublnb[:], in0=sublnb[:], scalar1=1.0 - LAM_INIT),
             reads=[sublnb], writes=[sublnb])
        P.dma(SP, lambda e: e.dma_start(out=tmpc[:, 0:256], in_=lam_d.rearrange("(o a) f -> o (a f)", o=1)
                                        .partition_broadcast(128).rearrange("p a f -> p (a f)")), writes=[tmpc])
        P.op(DVE, lambda e: e.tensor_tensor(out=tmpc2[:, 0:64], in0=tmpc[:, 0:64], in1=tmpc[:, 64:128], op=ALU.mult),
             reads=[tmpc], writes=[tmpc2])
        P.op(DVE, lambda e: e.tensor_tensor(out=tmpc2[:, 64:128], in0=tmpc[:, 128:192], in1=tmpc[:, 192:256], op=ALU.mult),
             reads=[tmpc], writes=[tmpc2])
        P.op(DVE, lambda e: e.tensor_reduce(out=tmpc2[:, 128:130], in_=tmpc2[:, 0:128].rearrange("p (a b) -> p a b", a=2),
                                            axis=mybir.AxisListType.X, op=ALU.add), reads=[tmpc2], writes=[tmpc2])
        P.op(ACT, lambda e: e.activation(out=tmpc2[:, 130:132], in_=tmpc2[:, 128:130], func=AF.Exp),
             reads=[tmpc2], writes=[tmpc2])
        P.op(DVE, lambda e: e.tensor_tensor(out=lamneg[:], in0=tmpc2[:, 131:132], in1=tmpc2[:, 130:131], op=ALU.subtract),
             reads=[tmpc2], writes=[lamneg])
        P.op(DVE, lambda e: e.tensor_scalar_add(out=lamneg[:], in0=lamneg[:], scalar1=-LAM_INIT),
             reads=[lamneg], writes=[lamneg])
        for g in range(4):
            P.dma(SP, lambda e, g=g: e.dma_start(out=tmpc[:, 0:128], in_=sgu_wT[g]), writes=[tmpc])
            P.op(DVE, lambda e, g=g: e.tensor_tensor(out=wmT[:, g, :], in0=tmpc[:, 0:128], in1=trimask[:], op=ALU.mult),
                 reads=[tmpc, trimask], writes=[wmT])
            P.dma(SP, lambda e, g=g: e.dma_start(out=tmpc2[0:32, 0:32], in_=sgu_wTs[g]), writes=[tmpc2])
            P.op(DVE, lambda e, g=g: e.tensor_tensor(out=wmTs[0:32, g, :], in0=tmpc2[0:32, 0:32], in1=trimask[0:32, 0:32],
                                                     op=ALU.mult), reads=[tmpc2, trimask], writes=[wmTs])
        P.dma(SP, lambda e: e.dma_start(out=tmpc[0:1, 0:512], in_=sgu_b.rearrange("(o g) t -> o (g t)", o=1)), writes=[tmpc])
        P.op(DVE, lambda e: e.tensor_copy(out=sgub[0:1], in_=tmpc[0:1, 0:512].rearrange("p (g t) -> p g t", g=4)),
             reads=[tmpc], writes=[sgub])
        P.dma(SP, lambda e: e.dma_start(out=tmpc2[0:1, 0:128], in_=sgu_bs.rearrange("(o g) t -> o (g t)", o=1)), writes=[tmpc2])
        P.op(DVE, lambda e: e.tensor_copy(out=sgubs[0:1], in_=tmpc2[0:1, 0:128].rearrange("p (g t) -> p g t", g=4)),
             reads=[tmpc2], writes=[sgubs])

        def rmsnorm_cols(xT, c0, n, gam_ap, hT, h0, sq, rstd):
            ps = pf()
            for k in range(8):
                P.op(ACT, lambda e, k=k: e.activation(out=sq[:, 0:n], in_=xT[:, k, c0:c0 + n], func=AF.Square),
                     reads=[xT], writes=[sq])
                P.op(PE, lambda e, k=k: e.matmul(ps[:, 0:n], lhsT=onesb[:], rhs=sq[:, 0:n], start=(k == 0), stop=(k == 7)),
                     reads=[sq, onesb], writes=[ps])
            P.op(ACT, lambda e: e.activation(out=rstd[:, 0:n], in_=ps[:, 0:n], func=AF.Sqrt, bias=epsT[:, 0:1], scale=1.0 / D),
                 reads=[ps, epsT], writes=[rstd])
            P.op(DVE, lambda e: e.reciprocal(out=rstd[:, 0:n], in_=rstd[:, 0:n]), reads=[rstd], writes=[rstd])
            for k in range(8):
                P.op(DVE, lambda e, k=k: e.scalar_tensor_tensor(out=hT[:, k, h0:h0 + n], in0=xT[:, k, c0:c0 + n],
                                                                scalar=gam_ap[:, k:k + 1], in1=rstd[:, 0:n],
                                                                op0=ALU.mult, op1=ALU.mult),
                     reads=[xT, rstd], writes=[hT])

        def load_w(dst, src_ap, q=POOL):
            P.dma(q, lambda e: e.dma_start(out=dst.ap if isinstance(dst, T) else dst, in_=src_ap), writes=[dst] if isinstance(dst, T) else [])

        KTp = R2.alloc("KTp", [4, 2048], BF16)
        V1p = R2.alloc("V1p", [16, 4, 130], BF16)
        KTo = R2.alloc("KTo", [4, 2048], BF16)
        V1o = R2.alloc("V1o", [16, 4, 130], BF16)
        KTt = [T((KTp if i < 16 else KTo).ap, f"KTt{i}") for i in range(32)]
        V1t = [T((V1p if i < 16 else V1o).ap, f"V1t{i}") for i in range(32)]

        def ktile(j, half, h):
            return KTt[j][half * 64:(half + 1) * 64, h, (j % 16) * 128:(j % 16 + 1) * 128]

        def vtile(j, h):
            return V1t[j][:, j % 16, h, 0:129]
        xT = R1.alloc("xT", [8, NTOT], F32)

        def layer0_mixer(pas):
            own = pas == "own"
            xsrc = x_own if own else x_pred
            kbase = 2048 if own else 0
            P.barrier_all()
            R3.reset()
            RW.reset()
            vt_ = V1o if own else V1p
            P.op(POOL, lambda e: e.memset(vt_[:, :, :, 128:130], 1.0), writes=[V1t[i + (16 if own else 0)] for i in range(16)])
            hT = R3.alloc("hT", [8, 512], BF16)
            uT = R3.alloc("uT", [4, 512], BF16)
            qT = R3.alloc("qT", [4, 512], BF16)
            aT = R3.alloc("aT", [4, 512], BF16)
            bT = R3.alloc("bT", [4, 512], BF16)
            vn = [R3.alloc(f"vn{i}", [512], BF16) for i in range(4)]
            sq = R3.alloc("sq", [512], BF16)
            rstd = R3.alloc("rstd", [512], F32)
            xs = [RW.alloc(f"xs{i}", [1024], F32) for i in range(2)]
            wtm = [RW.alloc(f"wtm{i}", [8, 512], BF16) for i in range(1)]
            rows = [RW.alloc(f"rows{i}", [512], F32) for i in range(2)]
            gl = RW.alloc("gl", [512], F32)
            wfm = [RW.alloc(f"wfm{i}", [8, 128], BF16) for i in range(2)]
            PT = [RW.alloc(f"PT{i}", [512], BF16) for i in range(2)]
            osb = RW.alloc("osb", [128], F32)
            onb = RW.alloc("onb", [128], BF16)
            sm = RW.alloc("sm", [8], F32)
            cnt = {"xs": 0, "wfm": 0, "rows": 0, "pt": 0}
            w_in_v = w_in.rearrange("(k p) c -> p k c", p=128)
            w_out_v = w_out.rearrange("(k p) c -> p k c", p=128)

            blocks = [(b * 512, 512) for b in range(4)] + ([(NT, NS)] if own else [])
            def do_block(c0, n):
                smp = c0 == NT
                ntt = (n + 127) // 128
                for tt in range(ntt):
                    r = min(128, n - tt * 128)
                    xs_t = xs[cnt["xs"] % 2]
                    cnt["xs"] += 1
                    src = x_smp if smp else xsrc[c0 + tt * 128:c0 + tt * 128 + r, :]
                    P.dma(SP, lambda e, xs_t=xs_t, src=src, r=r: e.dma_start(out=xs_t[0:r, :], in_=src), writes=[xs_t])
                    for kk in range(2):
                        ps = pf()
                        for k4 in range(4):
                            k = kk * 4 + k4
                            P.op(PE, lambda e, ps=ps, k=k, k4=k4, xs_t=xs_t, r=r: e.transpose(
                                out=ps[:, k4 * 128:k4 * 128 + r], in_=xs_t[0:r, k * 128:(k + 1) * 128], identity=identf[0:r, 0:r]),
                                reads=[xs_t, identf], writes=[ps])
                        P.op(ACT, lambda e, ps=ps, kk=kk, tt=tt, r=r: e.copy(
                            out=xT[:, kk * 4:kk * 4 + 4, c0 + tt * 128:c0 + tt * 128 + r],
                            in_=ps[:].rearrange("p (a b) -> p a b", a=4)[:, :, 0:r]), reads=[ps], writes=[xT])
                rmsnorm_cols(xT, c0, n, gmix[:, 0, :], hT, 0, sq, rstd)
                for cc in [0, 1, 2, 3, 8, 9, 10, 11, 12, 13, 14, 15]:
                    if cc in (8, 9, 10, 11) and not own:
                        pass
                    wt = wfm[cnt["wfm"] % 2]
                    cnt["wfm"] += 1
                    P.dma(POOL, lambda e, wt=wt, cc=cc: e.dma_start(out=wt[:], in_=w_in_v[:, :, cc * 128:(cc + 1) * 128]), writes=[wt])
                    ps = pf()
                    for k in range(8):
                        P.op(PE, lambda e, ps=ps, wt=wt, k=k: e.matmul(ps[:, 0:n], lhsT=wt[:, k, :], rhs=hT[:, k, 0:n],
                                                                      start=(k == 0), stop=(k == 7)),
                             reads=[wt, hT], writes=[ps])
                    if cc < 4:
                        P.op(ACT, lambda e, ps=ps, cc=cc: e.activation(out=uT[:, cc, 0:n], in_=ps[:, 0:n], func=AF.Gelu_apprx_tanh),
                             reads=[ps], writes=[uT])
                    elif cc < 12:
                        P.op(ACT, lambda e, ps=ps, cc=cc: e.copy(out=qT[:, cc - 8, 0:n], in_=ps[:, 0:n]), reads=[ps], writes=[qT])
                    else:
                        h = cc - 12
                        if smp:
                            P.op(DVE, lambda e, ps=ps, h=h: e.tensor_copy(out=KTs[:, h, 0:n], in_=ps[:, 0:n]), reads=[ps], writes=[KTs_t])
                        else:
                            for tt in range(4):
                                ti = (kbase + c0) // 128 + tt
                                P.op(DVE, lambda e, ps=ps, h=h, tt=tt, ti=ti: e.tensor_copy(
                                    out=KTt[ti][:, h, (ti % 16) * 128:(ti % 16 + 1) * 128], in_=ps[:, tt * 128:(tt + 1) * 128]),
                                    reads=[ps], writes=[KTt[ti]])
                for gi, col0 in enumerate((512, 1536, 2048)):
                    if gi == 1 and not own:
                        continue
                    wt = wtm[0]
                    for hh in range(2):
                        P.dma(POOL, lambda e, wt=wt, col0=col0, hh=hh: e.dma_start(
                            out=wt[:, hh * 4:hh * 4 + 4, :], in_=w_in_v[:, hh * 4:hh * 4 + 4, col0:col0 + 512]), writes=[wt])
                    for tt in range(ntt):
                        r = min(128, n - tt * 128)
                        ps = pf()
                        for k in range(8):
                            P.op(PE, lambda e, ps=ps, wt=wt, k=k, tt=tt, r=r: e.matmul(
                                ps[0:r, :], lhsT=hT[:, k, tt * 128:tt * 128 + r], rhs=wt[:, k, :], start=(k == 0), stop=(k == 7)),
                                reads=[wt, hT], writes=[ps])
                        if gi == 0:
                            P.op(ACT, lambda e, ps=ps, r=r: e.activation(out=gl[0:r, :], in_=ps[0:r, :], func=AF.Gelu_apprx_tanh),
                                 reads=[ps], writes=[gl])
                            rw = rows[cnt["rows"] % 2]
                            cnt["rows"] += 1
                            P.op(ACT, lambda e, r=r, rw=rw: e.activation(out=rw[0:r, :], in_=gl[0:r, :], func=AF.Square,
                                                                         accum_out=sm[0:r, 0:1]), reads=[gl], writes=[rw, sm])
                            P.op(ACT, lambda e, r=r: e.activation(out=sm[0:r, 1:2], in_=sm[0:r, 0:1], func=AF.Sqrt,
                                                                  bias=epsT[0:r, 0:1], scale=1.0 / 512), reads=[sm, epsT], writes=[sm])
                            P.op(DVE, lambda e, r=r: e.reciprocal(out=sm[0:r, 2:3], in_=sm[0:r, 1:2]), reads=[sm], writes=[sm])
                            if smp:
                                P.op(DVE, lambda e, r=r, rw=rw: e.scalar_tensor_tensor(
                                    out=rw[0:r, :], in0=gl[0:r, :], scalar=sm[0:r, 2:3], in1=sgunb[0:r, :], op0=ALU.mult, op1=ALU.mult),
                                    reads=[gl, sm, sgunb], writes=[rw])
                                P.dma(SP, lambda e, rw=rw, r=r: e.dma_start(out=cv_s, in_=rw[0:r, :]), reads=[rw])
                                P.op(DVE, lambda e, r=r, rw=rw, tt=tt: e.tensor_copy(out=vn[tt][0:r, :], in_=rw[0:r, :]),
                                     reads=[rw], writes=[vn[tt]])
                            else:
                                P.op(DVE, lambda e, r=r, tt=tt: e.scalar_tensor_tensor(
                                    out=vn[tt][0:r, :], in0=gl[0:r, :], scalar=sm[0:r, 2:3], in1=sgunb[0:r, :], op0=ALU.mult, op1=ALU.mult),
                                    reads=[gl, sm, sgunb], writes=[vn[tt]])
                        else:
                            rw = rows[cnt["rows"] % 2]
                            cnt["rows"] += 1
                            P.op(ACT, lambda e, ps=ps, r=r, rw=rw: e.copy(out=rw[0:r, :], in_=ps[0:r, :]), reads=[ps], writes=[rw])
                            if own:
                                if smp:
                                    dst = kr_s if gi == 1 else vr_s
                                else:
                                    dst = (kr_p if gi == 1 else vr_p)[c0 + tt * 128:c0 + tt * 128 + r, :]
                                P.dma(SP, lambda e, rw=rw, r=r, dst=dst: e.dma_start(out=dst, in_=rw[0:r, :]), reads=[rw])
                            if gi == 2:
                                if smp:
                                    for h_ in range(4):
                                        P.op(POOL, lambda e, rw=rw, r=r, h_=h_: e.tensor_copy(
                                            out=V1s[0:r, h_, 0:128], in_=rw[0:r, h_ * 128:(h_ + 1) * 128]),
                                            reads=[rw], writes=[V1s_t])
                                else:
                                    ti = (kbase + c0) // 128 + tt
                                    for h_ in range(4):
                                        P.op(POOL, lambda e, rw=rw, ti=ti, h_=h_: e.tensor_copy(
                                            out=V1t[ti][:, ti % 16, h_, 0:128], in_=rw[:, h_ * 128:(h_ + 1) * 128]),
                                            reads=[rw], writes=[V1t[ti]])
                if not FLAGS['sgu']:
                    pass
                elif not smp:
                    for tt in range(4):
                        ps = pf()
                        for g in range(4):
                            P.op(PE, lambda e, ps=ps, g=g, tt=tt: e.matmul(ps[:, g * 128:(g + 1) * 128], lhsT=vn[tt][:, g * 128:(g + 1) * 128],
                                                                           rhs=wmT[:, g, :], start=True, stop=False),
                                 reads=[vn[tt], wmT], writes=[ps])
                            P.op(PE, lambda e, ps=ps, g=g: e.matmul(ps[:, g * 128:(g + 1) * 128], lhsT=onesb[0:1, :],
                                                                    rhs=sgub[0:1, g, :], start=False, stop=True),
                                 reads=[sgub, onesb], writes=[ps])
                        P.op(DVE, lambda e, ps=ps, tt=tt: e.tensor_tensor(
                            out=aT[:, :, tt * 128:(tt + 1) * 128], in0=ps[:].rearrange("p (g t) -> p g t", g=4),
                            in1=uT[:, :, tt * 128:(tt + 1) * 128], op=ALU.mult), reads=[ps, uT], writes=[aT])
                else:
                    ps = pf()
                    for g in range(4):
                        P.op(PE, lambda e, ps=ps, g=g: e.matmul(ps[:, g * 32:(g + 1) * 32], lhsT=vn[0][0:32, g * 128:(g + 1) * 128],
                                                                rhs=wmTs[0:32, g, :], start=True, stop=False),
                             reads=[vn[0], wmTs], writes=[ps])
                        P.op(PE, lambda e, ps=ps, g=g: e.matmul(ps[:, g * 32:(g + 1) * 32], lhsT=onesb[0:1, :],
                                                                rhs=sgubs[0:1, g, :], start=False, stop=True),
                             reads=[sgubs, onesb], writes=[ps])
                    P.op(DVE, lambda e, ps=ps: e.tensor_tensor(
                        out=aT[:, :, 0:32], in0=ps[:, 0:128].rearrange("p (g t) -> p g t", g=4),
                        in1=uT[:, :, 0:32], op=ALU.mult), reads=[ps, uT], writes=[aT])
                if not FLAGS['attn']:
                    pass
                elif not smp:
                    attention_prompt(own, c0, kbase, qT, bT, PT, osb, onb, sm, cnt)
                elif FLAGS['smp_attn']:
                    attention_sample(qT, bT, osb, onb, sm)
                if FLAGS.get('stop') == 'ab' and smp:
                    P.op(DVE, lambda e: e.tensor_copy(out=xT[:, 0:4, 0:32], in_=aT[:, :, 0:32]), reads=[aT], writes=[xT])
                    P.op(DVE, lambda e: e.tensor_copy(out=xT[:, 4:8, 0:32], in_=bT[:, :, 0:32]), reads=[bT], writes=[xT])
                    return
                if FLAGS.get('dbg') == 'bT' and own and c0 == 1536:
                    P.op(DVE, lambda e: e.tensor_copy(out=xT[:, 0:4, 0:512], in_=bT[:]), reads=[bT], writes=[xT])
                    return
                for m in (range(8) if FLAGS['outp'] else []):
                    wt = wfm[cnt["wfm"] % 2]
                    cnt["wfm"] += 1
                    P.dma(POOL, lambda e, wt=wt, m=m: e.dma_start(out=wt[:], in_=w_out_v[:, :, m * 128:(m + 1) * 128]), writes=[wt])
                    ps = pf()
                    for kc in range(8):
                        srcT = aT if kc < 4 else bT
                        P.op(PE, lambda e, ps=ps, wt=wt, kc=kc, srcT=srcT: e.matmul(
                            ps[:, 0:n], lhsT=wt[:, kc, :], rhs=srcT[:, kc % 4, 0:n], start=(kc == 0), stop=(kc == 7)),
                            reads=[wt, srcT], writes=[ps])
                    P.op(DVE, lambda e, ps=ps, m=m: e.tensor_tensor(out=xT[:, m, c0:c0 + n], in0=xT[:, m, c0:c0 + n], in1=ps[:, 0:n],
                                                                   op=ALU.add), reads=[ps, xT], writes=[xT])

            for (c0_, n_) in blocks:
                do_block(c0_, n_)

        acc = [psf[4], psf[5]]

        def attention_prompt(own, c0, kbase, qT, bT, PT, osb, onb, sm, cnt):
            for qb in range(2):
                q0 = qb * 256
                tA = (kbase + c0 + q0) // 128
                jlist = list(range(0 if own else 0, tA + 2))
                if not own:
                    jlist = list(range(0, tA + 2))
                for h in range(4):
                    first = {0: True, 1: True}
                    for j in jlist:
                        ps = pf()
                        for half in range(2):
                            P.op(PE, lambda e, ps=ps, half=half, h=h, j=j, q0=q0: e.matmul(
                                ps[:, half * 256:(half + 1) * 256], lhsT=ktile(j, half, h),
                                rhs=qT[half * 64:(half + 1) * 64, h, q0:q0 + 256], start=True, stop=True),
                                reads=[KTt[j], qT], writes=[ps])
                        pt = PT[cnt["pt"] % 2]
                        cnt["pt"] += 1
                        bias = pbias if (own and j < 16) else zbias
                        P.op(ACT, lambda e, ps=ps, pt=pt, bias=bias: e.activation(out=pt[:], in_=ps[:], func=AF.Exp,
                                                                                  bias=bias[:, 0:1], scale=SCALE),
                             reads=[ps, bias], writes=[pt])
                        for qt in range(2):
                            tq = tA + qt
                            if j > tq:
                                continue
                            if j == tq:
                                for half in range(2):
                                    o_ = half * 256 + qt * 128
                                    P.op(POOL, lambda e, pt=pt, o_=o_: e.tensor_tensor(
                                        out=pt[:, o_:o_ + 128], in0=pt[:, o_:o_ + 128], in1=trimb[:], op=ALU.mult),
                                        reads=[pt, trimb], writes=[pt])
                        for qt in range(2):
                            tq = tA + qt
                            if j > tq:
                                continue
                            for half in range(2):
                                a = acc[qt]
                                st_flag = first[qt]
                                first[qt] = False
                                P.op(PE, lambda e, a=a, pt=pt, half=half, qt=qt, h=h, j=j, st_flag=st_flag: e.matmul(
                                    a[:, half * 129:(half + 1) * 129], lhsT=pt[:, half * 256 + qt * 128:half * 256 + (qt + 1) * 128],
                                    rhs=vtile(j, h), start=st_flag, stop=False, skip_group_check=True),
                                    reads=[pt, V1t[j]], writes=[a])
                    for qt in range(2):
                        attn_epilogue(acc[qt], 128, bT, h, q0 + qt * 128, osb, onb, sm)

        def attn_epilogue(a, r, bT, h, col0, osb, onb, sm):
            P.op(DVE, lambda e: e.reciprocal(out=sm[0:r, 0:1], in_=a[0:r, 128:129]), reads=[a], writes=[sm])
            P.op(DVE, lambda e: e.reciprocal(out=sm[0:r, 1:2], in_=a[0:r, 257:258]), reads=[a], writes=[sm])
            P.op(DVE, lambda e: e.tensor_tensor(out=sm[0:r, 1:2], in0=sm[0:r, 1:2], in1=lamneg[0:r, 0:1], op=ALU.mult),
                 reads=[sm, lamneg], writes=[sm])
            P.op(DVE, lambda e: e.tensor_scalar_mul(out=osb[0:r, :], in0=a[0:r, 0:128], scalar1=sm[0:r, 0:1]),
                 reads=[a, sm], writes=[osb])
            P.op(DVE, lambda e: e.scalar_tensor_tensor(out=osb[0:r, :], in0=a[0:r, 129:257], scalar=sm[0:r, 1:2], in1=osb[0:r, :],
                                                       op0=ALU.mult, op1=ALU.add), reads=[a, sm, osb], writes=[osb])
            P.op(ACT, lambda e: e.activation(out=onb[0:r, :], in_=osb[0:r, :], func=AF.Square, accum_out=sm[0:r, 2:3]),
                 reads=[osb], writes=[onb, sm])
            P.op(ACT, lambda e: e.activation(out=sm[0:r, 3:4], in_=sm[0:r, 2:3], func=AF.Sqrt, bias=epsT[0:r, 0:1], scale=1.0 / 128),
                 reads=[sm, epsT], writes=[sm])
            P.op(DVE, lambda e: e.reciprocal(out=sm[0:r, 4:5], in_=sm[0:r, 3:4]), reads=[sm], writes=[sm])
            P.op(DVE, lambda e: e.scalar_tensor_tensor(out=onb[0:r, :], in0=osb[0:r, :], scalar=sm[0:r, 4:5], in1=sublnb[0:r, :],
                                                       op0=ALU.mult, op1=ALU.mult), reads=[osb, sm, sublnb], writes=[onb])
            pb_ = pbk()
            P.op(PE, lambda e: e.transpose(out=pb_[:, 0:r], in_=onb[0:r, :], identity=identb[0:r, 0:r]),
                 reads=[onb, identb], writes=[pb_])
            P.op(ACT, lambda e: e.copy(out=bT[:, h, col0:col0 + r], in_=pb_[:, 0:r]), reads=[pb_], writes=[bT])

        KTs = RC.alloc("KTs", [4, 32], BF16)
        KTs_t = KTs
        V1s = RC.alloc("V1s", [4, 130], BF16)
        V1s_t = V1s
        P.op(POOL, lambda e: e.memset(V1s[:, :, 128:129], 1.0), writes=[V1s])

        def attention_sample(qT, bT, osb, onb, sm):
            P.barrier_all()
            R2.reset()
            pidx_i = R2.alloc("pidx_i", [4, NPAGE], I32)
            pidx_f = R2.alloc("pidx_f", [4, NPAGE], F32)
            iot = R2.alloc("iot", [NPAGE], F32)
            pidx = R2.alloc("pidx", [4, NPAGE], I32)
            NKB, NVB = 8, 16
            kpg = [R2.alloc(f"kpg{i}", [512], F32) for i in range(NKB)]
            vpg = [R2.alloc(f"vpg{i}", [512], F32) for i in range(NVB)]
            kpb = R2.alloc("kpb", [512], BF16)
            kTp = R2.alloc("kTp", [4, 128], BF16)
            vpb = [R2.alloc(f"vpb{i}", [4, 130], BF16) for i in range(2)]
            PTs = [R2.alloc(f"PTs{i}", [512], BF16) for i in range(2)]
            qbd = R2.alloc("qbd", [4, 4, 16], BF16)
            smask = R2.alloc("smask", [4, 8], BF16)
            est = R2.alloc("est", [258], F32)
            mk = R2.alloc("mk", [96], F32)
            mk2 = R2.alloc("mk2", [96], F32)
            for vb_ in vpb:
                P.op(POOL, lambda e, vb_=vb_: e.memset(vb_[:, :, 128:129], 1.0), writes=[vb_])
            P.dma(SP, lambda e: e.dma_start(out=pidx_i[:], in_=ptab.rearrange("(o s) n -> o s n", o=1).partition_broadcast(128)
                                            .rearrange("p o s n -> p (o s) n")), writes=[pidx_i])
            P.op(POOL, lambda e: e.iota(iot[:], pattern=[[0, NPAGE]], base=0, channel_multiplier=1, allow_small_or_imprecise_dtypes=True),
                 writes=[iot])
            P.op(DVE, lambda e: e.tensor_copy(out=pidx_f[:], in_=pidx_i[:]), reads=[pidx_i], writes=[pidx_f])
            for s_ in range(4):
                P.op(DVE, lambda e, s_=s_: e.scalar_tensor_tensor(out=pidx_f[:, s_, :], in0=pidx_f[:, s_, :], scalar=128.0, in1=iot[:],
                                                                  op0=ALU.mult, op1=ALU.add), reads=[pidx_f, iot], writes=[pidx_f])
            P.op(DVE, lambda e: e.tensor_copy(out=pidx[:], in_=pidx_f[:]), reads=[pidx_f], writes=[pidx])
            P.op(POOL, lambda e: e.iota(mk[0:32, 0:32], pattern=[[8, 4], [1, 8]], base=0, channel_multiplier=-1,
                                        allow_small_or_imprecise_dtypes=True), writes=[mk])
            P.op(POOL, lambda e: e.iota(mk[0:32, 32:64], pattern=[[8, 4], [0, 8]], base=7, channel_multiplier=-1,
                                        allow_small_or_imprecise_dtypes=True), writes=[mk])
            P.op(POOL, lambda e: e.iota(mk[0:32, 64:96], pattern=[[-8, 4], [0, 8]], base=0, channel_multiplier=1,
                                        allow_small_or_imprecise_dtypes=True), writes=[mk])
            P.op(DVE, lambda e: e.tensor_single_scalar(out=mk2[0:32, :], in_=mk[0:32, :], scalar=0.0, op=ALU.is_ge),
                 reads=[mk], writes=[mk2])
            P.op(DVE, lambda e: e.tensor_tensor(out=mk2[0:32, 0:32], in0=mk2[0:32, 0:32], in1=mk2[0:32, 32:64], op=ALU.mult),
                 reads=[mk2], writes=[mk2])
            P.op(DVE, lambda e: e.tensor_tensor(out=smask[0:32].rearrange("p s q -> p (s q)"), in0=mk2[0:32, 0:32], in1=mk2[0:32, 64:96],
                                                op=ALU.mult), reads=[mk2], writes=[smask])
            P.op(POOL, lambda e: e.memset(qbd[:], 0.0), writes=[qbd])
            for h in range(4):
                P.op(DVE, lambda e, h=h: e.tensor_copy(out=qbd[0:64, h, :, 0:8], in_=qT[0:64, h, 0:32].rearrange("p (s q) -> p s q", s=4)),
                     reads=[qT], writes=[qbd])
                P.op(DVE, lambda e, h=h: e.tensor_copy(out=qbd[64:128, h, :, 8:16], in_=qT[64:128, h, 0:32].rearrange("p (s q) -> p s q", s=4)),
                     reads=[qT], writes=[qbd])
            a, a2, a3 = acc[0], acc[1], psf[3]
            rot["n"] = 3

            def region(h, half):
                if h < 3:
                    return (a if half == 0 else a2), (a if half == 0 else a2)[0:8, h * 129:(h + 1) * 129]
                return a3, a3[0:8, half * 129:(half + 1) * 129]

            steps = [(s_, grp) for s_ in range(4) for grp in range(NPAGE // 8)]

            def gathers(si, which):
                s_, grp = steps[si]
                for jj in range(8):
                    j = grp * 8 + jj
                    pi = si * 8 + jj
                    kp, vp = kpg[pi % NKB], vpg[pi % NVB]
                    if which == "k":
                        P.dma(POOL, lambda e, kp=kp, s_=s_, j=j: e.indirect_dma_start(
                            out=kp[:], out_offset=None, in_=cache_k,
                            in_offset=bass.IndirectOffsetOnAxis(ap=pidx[:, s_, j:j + 1], axis=0)), reads=[pidx], writes=[kp])
                    else:
                        P.dma(POOL, lambda e, vp=vp, s_=s_, j=j: e.indirect_dma_start(
                            out=vp[:], out_offset=None, in_=cache_v,
                            in_offset=bass.IndirectOffsetOnAxis(ap=pidx[:, s_, j:j + 1], axis=0)), reads=[pidx], writes=[vp])

            gathers(0, "k")
            gathers(0, "v")
            for si, (s_, grp) in enumerate(steps):
                if si + 1 < len(steps):
                    gathers(si + 1, "v")
                sc = pf()
                for jj in range(8):
                    pi = si * 8 + jj
                    kp = kpg[pi % NKB]
                    P.op(ACT, lambda e, kp=kp: e.copy(out=kpb[:], in_=kp[:]), reads=[kp], writes=[kpb])
                    pb_ = pbk()
                    for h in range(4):
                        P.op(PE, lambda e, pb_=pb_, h=h: e.transpose(out=pb_[:, h * 128:(h + 1) * 128], in_=kpb[:, h * 128:(h + 1) * 128],
                                                                     identity=identb[:]), reads=[kpb, identb], writes=[pb_])
                    P.op(DVE, lambda e, pb_=pb_: e.tensor_copy(out=kTp[:], in_=pb_[:, 0:512].rearrange("p (h t) -> p h t", h=4)),
                         reads=[pb_], writes=[kTp])
                    for h in range(4):
                        P.op(PE, lambda e, sc=sc, h=h, jj=jj, s_=s_: e.matmul(
                            sc[:, jj * 64 + h * 16:jj * 64 + (h + 1) * 16], lhsT=kTp[:, h, :], rhs=qbd[:, h, s_, :], start=True, stop=True),
                            reads=[kTp, qbd], writes=[sc])
                if si + 1 < len(steps):
                    gathers(si + 1, "k")
                pt = PTs[si % 2]
                P.op(ACT, lambda e, sc=sc, pt=pt: e.activation(out=pt[:], in_=sc[:], func=AF.Exp, scale=SCALE, bias=zbias[:, 0:1]),
                     reads=[sc, zbias], writes=[pt])
                if FLAGS.get('stop') == 'ab' and si == 0:
                    P.op(DVE, lambda e: e.tensor_copy(out=xT[:, 0, 64:576], in_=kpg[0][:]), reads=[kpg[0]], writes=[xT])
                    P.op(DVE, lambda e: e.tensor_copy(out=xT[:, 1, 64:576], in_=kpg[7][:]), reads=[kpg[7]], writes=[xT])
                    P.op(DVE, lambda e, sc=sc: e.tensor_copy(out=xT[:, 2, 64:576], in_=sc[:]), reads=[sc], writes=[xT])
                    P.op(DVE, lambda e: e.tensor_copy(out=xT[:, 3, 64:576], in_=vpg[0][:]), reads=[vpg[0]], writes=[xT])
                    P.op(DVE, lambda e: e.tensor_copy(out=xT[:, 4, 64:320], in_=qbd[:].rearrange("p a b c -> p (a b c)")), reads=[qbd], writes=[xT])
                    P.op(DVE, lambda e: e.tensor_copy(out=xT[:, 5, 64:576], in_=kTp[:].rearrange("p a b -> p (a b)")), reads=[kTp], writes=[xT])
                    P.op(DVE, lambda e, pt=pt: e.tensor_copy(out=xT[:, 6, 64:576], in_=pt[:]), reads=[pt], writes=[xT])
                    P.op(DVE, lambda e: e.tensor_copy(out=xT[:, 7, 64:320], in_=pidx[:].rearrange("p a b -> p (a b)")), reads=[pidx], writes=[xT])
                for jj in range(8):
                    pi = si * 8 + jj
                    vp = vpg[pi % NVB]
                    vb_ = vpb[pi % 2]
                    P.op(POOL, lambda e, vp=vp, vb_=vb_: e.tensor_copy(out=vb_[:, :, 0:128], in_=vp[:].rearrange("p (h d) -> p h d", h=4)),
                         reads=[vp], writes=[vb_])
                    for h in range(4):
                        for half in range(2):
                            tT, tg = region(h, half)
                            st_flag = (grp == 0 and jj == 0) and ((h == 0) or (h == 3 and half == 0))
                            P.op(PE, lambda e, tg=tg, pt=pt, jj=jj, h=h, half=half, st_flag=st_flag, vb_=vb_: e.matmul(
                                tg, lhsT=pt[:, jj * 64 + h * 16 + half * 8:jj * 64 + h * 16 + half * 8 + 8],
                                rhs=vb_[:, h, 0:129], start=st_flag, stop=False, skip_group_check=True),
                                reads=[pt, vb_], writes=[tT])
                if grp < NPAGE // 8 - 1:
                    continue
                sc = pf()
                for h in range(4):
                    P.op(PE, lambda e, sc=sc, h=h, s_=s_: e.matmul(sc[0:32, h * 16:(h + 1) * 16], lhsT=KTs[:, h, :], rhs=qbd[:, h, s_, :],
                                                                   start=True, stop=True), reads=[KTs_t, qbd], writes=[sc])
                pt = PTs[(si + 1) % 2]
                P.op(ACT, lambda e, sc=sc, pt=pt: e.activation(out=pt[0:32, 0:64], in_=sc[0:32, 0:64], func=AF.Exp, scale=SCALE,
                                                               bias=zbias[0:32, 0:1]), reads=[sc, zbias], writes=[pt])
                P.op(DVE, lambda e, pt=pt, s_=s_: e.tensor_tensor(
                    out=pt[0:32, 0:64].rearrange("p (a q) -> p a q", q=8), in0=pt[0:32, 0:64].rearrange("p (a q) -> p a q", q=8),
                    in1=smask[0:32, s_, :].unsqueeze(1).to_broadcast([32, 8, 8]), op=ALU.mult), reads=[pt, smask], writes=[pt])
                for h in range(4):
                    for half in range(2):
                        tT, tg = region(h, half)
                        P.op(PE, lambda e, tg=tg, pt=pt, h=h, half=half: e.matmul(
                            tg, lhsT=pt[0:32, h * 16 + half * 8:h * 16 + half * 8 + 8], rhs=V1s[0:32, h, 0:129], start=False, stop=True,
                            skip_group_check=True), reads=[pt, V1s_t], writes=[tT])
                for h in range(3):
                    cs = h * 129
                    P.op(DVE, lambda e, cs=cs: e.tensor_copy(out=est[0:8, 0:129], in_=a[0:8, cs:cs + 129]), reads=[a], writes=[est])
                    P.op(DVE, lambda e, cs=cs: e.tensor_copy(out=est[0:8, 129:258], in_=a2[0:8, cs:cs + 129]), reads=[a2], writes=[est])
                    attn_epilogue(est, 8, bT, h, s_ * 8, osb, onb, sm)
                attn_epilogue(a3, 8, bT, 3, s_ * 8, osb, onb, sm)
            rot["n"] = 4

        def ffn(layer, ncols, actT, G):
            P.barrier_all()
            R3.reset()
            RW.reset()
            hT = R3.alloc("hTall", [8, NTOT], BF16)
            sq = RW.alloc("sq", [512], BF16)
            rstd = RW.alloc("rstd", [512], F32)
            wg = [RW.alloc(f"wg{i}", [8, 128], BF16) for i in range(2)]
            wu = [RW.alloc(f"wu{i}", [8, 128], BF16) for i in range(2)]
            w2 = [RW.alloc(f"w2{i}", [G, 128], BF16) for i in range(2)]
            sg = [RW.alloc(f"sg{i}", [512], BF16) for i in range(2)]
            blocks = [(c, min(512, ncols - c)) for c in range(0, ncols, 512)]
            for (c0, n) in blocks:
                rmsnorm_cols(xT, c0, n, gffn[:, layer, :], hT, c0, sq, rstd)
            w1v = w_f1[layer].rearrange("(k p) c -> p k c", p=128)
            w2v = w_f2[layer].rearrange("(j p) c -> p j c", p=128)
            ci = 0
            groups = [(j0, min(G, 22 - j0)) for j0 in range(0, 22, G)]
            for (j0, gsz) in groups:
                for jj in range(gsz):
                    j = j0 + jj
                    wgt = wg[ci % 2]
                    wut = wu[ci % 2]
                    ci += 1
                    P.dma(POOL, lambda e, wgt=wgt, j=j: e.dma_start(out=wgt[:], in_=w1v[:, :, j * 128:(j + 1) * 128]), writes=[wgt])
                    P.dma(POOL, lambda e, wut=wut, j=j: e.dma_start(out=wut[:], in_=w1v[:, :, DFF + j * 128:DFF + (j + 1) * 128]), writes=[wut])
                    for bi, (c0, n) in enumerate(blocks):
                        pg_ = pf()
                        pu_ = pf()
                        for k in range(8):
                            P.op(PE, lambda e, pg_=pg_, wgt=wgt, k=k, c0=c0, n=n: e.matmul(pg_[:, 0:n], lhsT=wgt[:, k, :], rhs=hT[:, k, c0:c0 + n],
                                                                                        start=(k == 0), stop=(k == 7)), reads=[wgt, hT], writes=[pg_])
                        for k in range(8):
                            P.op(PE, lambda e, pu_=pu_, wut=wut, k=k, c0=c0, n=n: e.matmul(pu_[:, 0:n], lhsT=wut[:, k, :], rhs=hT[:, k, c0:c0 + n],
                                                                                        start=(k == 0), stop=(k == 7)), reads=[wut, hT], writes=[pu_])
                        sgt = sg[bi % 2]
                        P.op(ACT, lambda e, pg_=pg_, sgt=sgt, n=n: e.activation(out=sgt[:, 0:n], in_=pg_[:, 0:n], func=AF.Silu),
                             reads=[pg_], writes=[sgt])
                        P.op(DVE, lambda e, pu_=pu_, sgt=sgt, jj=jj, c0=c0, n=n: e.tensor_tensor(
                            out=actT[:, jj, c0:c0 + n], in0=sgt[:, 0:n], in1=pu_[:, 0:n], op=ALU.mult), reads=[pu_, sgt], writes=[actT])
                for m in range(8):
                    w2t = w2[m % 2]
                    P.dma(POOL, lambda e, w2t=w2t, m=m, j0=j0, gsz=gsz: e.dma_start(
                        out=w2t[:, 0:gsz, :], in_=w2v[:, j0:j0 + gsz, m * 128:(m + 1) * 128]), writes=[w2t])
                    for (c0, n) in blocks:
                        ps = pf()
                        for jj in range(gsz):
                            P.op(PE, lambda e, ps=ps, w2t=w2t, jj=jj, c0=c0, n=n, gsz=gsz: e.matmul(ps[:, 0:n], lhsT=w2t[:, jj, :], rhs=actT[:, jj, c0:c0 + n],
                                                                                        start=(jj == 0), stop=(jj == gsz - 1)), reads=[w2t, actT], writes=[ps])
                        P.op(DVE, lambda e, ps=ps, m=m, c0=c0, n=n: e.tensor_tensor(out=xT[:, m, c0:c0 + n], in0=xT[:, m, c0:c0 + n], in1=ps[:, 0:n],
                                                                                 op=ALU.add), reads=[ps, xT], writes=[xT])


        Sst = RC.alloc("Sst", [64], F32)
        Ssw = RC.alloc("Ssw", [64], F32)
        A1 = RC.alloc("A1", [64], F32)
        A2 = RC.alloc("A2", [64], F32)
        sgn = RC.alloc("sgn", [1], F32)
        mtop = RC.alloc("mtop", [1], F32)
        nmbot = RC.alloc("nmbot", [1], F32)
        negpi = RC.alloc("negpi", [1], F32)
        P.op(POOL, lambda e: e.iota(tmpc[:, 0:1], pattern=[[0, 1]], base=-64, channel_multiplier=1,
                                    allow_small_or_imprecise_dtypes=True), writes=[tmpc])
        P.op(DVE, lambda e: e.tensor_single_scalar(out=nmbot[:], in_=tmpc[:, 0:1], scalar=0.0, op=ALU.is_ge), reads=[tmpc], writes=[nmbot])
        P.op(DVE, lambda e: e.tensor_scalar(out=sgn[:], in0=nmbot[:], scalar1=2.0, scalar2=-1.0, op0=ALU.mult, op1=ALU.add),
             reads=[nmbot], writes=[sgn])
        P.op(DVE, lambda e: e.tensor_scalar(out=mtop[:], in0=nmbot[:], scalar1=-1.0, scalar2=1.0, op0=ALU.mult, op1=ALU.add),
             reads=[nmbot], writes=[mtop])
        P.op(DVE, lambda e: e.tensor_scalar_mul(out=nmbot[:], in0=nmbot[:], scalar1=-1.0), reads=[nmbot], writes=[nmbot])
        P.op(DVE, lambda e: e.memset(negpi[:], -math.pi), writes=[negpi])

        TWO_PI = 2.0 * math.pi

        def ssm_tables(full):
            R2.reset()
            RW.reset()
            if not full:
                R2.off = 33024
            XB = R2.alloc("XB", [64, 128], BF16)
            Wz = R2.alloc("Wz", [64, 128], BF16)
            if full:
                Tt = R2.alloc("Tt", [64, 128], BF16)
                YC = R2.alloc("YC", [64, 128], BF16)
                tflat = R2.base[:, 32768 // 4:(32768 + 16384) // 4]
                inA = T(tflat[:, 0:1024].rearrange("p (g c) -> p g c", g=64), "inA")
                inB = T(tflat[:, 1024:2048].rearrange("p (g c) -> p g c", g=64), "inB")
                parts = [(0, 64)]
            else:
                Tt = YC = None
                inA = RW.alloc("inA", [32, 16], F32)
                inB = RW.alloc("inB", [32, 16], F32)
                parts = [(0, 32), (32, 64)]
            tb = [RW.alloc(f"tb{i}", [64, 8], F32) for i in range(10)]
            sm_ = [RW.alloc(f"ts{i}", [64], F32) for i in range(8)]
            kidx = RW.alloc("kidx", [8], F32)
            if full:
                tmask4 = RW.alloc("tmask4", [4, 128], F32)
                dsk = RW.alloc("dsk", [64], F32)
                tmpf = RW.alloc("tmpf", [512], F32)
            aR, aI = sm_[6], sm_[7]
            P.dma(SP, lambda e: e.dma_start(out=aR[:], in_=a_re), writes=[aR])
            P.dma(SP, lambda e: e.dma_start(out=aI[:], in_=a_im), writes=[aI])
            P.dma(SP, lambda e: e.dma_start(out=sm_[0][:], in_=logdt), writes=[sm_[0]])
            if full:
                P.dma(SP, lambda e: e.dma_start(out=dsk[:], in_=d_skip), writes=[dsk])
            P.op(ACT, lambda e: e.activation(out=sm_[0][:], in_=sm_[0][:], func=AF.Exp), reads=[sm_[0]], writes=[sm_[0]])
            P.op(DVE, lambda e: e.tensor_tensor(out=sm_[1][:], in0=aR[:], in1=sm_[0][:], op=ALU.mult), reads=[aR, sm_[0]], writes=[sm_[1]])
            P.op(DVE, lambda e: e.tensor_tensor(out=sm_[2][:], in0=aI[:], in1=sm_[0][:], op=ALU.mult), reads=[aI, sm_[0]], writes=[sm_[2]])
            P.op(POOL, lambda e: e.iota(kidx[:], pattern=[[1, 8]], base=1, channel_multiplier=0, allow_small_or_imprecise_dtypes=True), writes=[kidx])
            kb = lambda: kidx[:].unsqueeze(1).broadcast_to([128, 64, 8])
            gb = lambda t: t[:].unsqueeze(2).broadcast_to([128, 64, 8])
            P.op(DVE, lambda e: e.tensor_tensor(out=tb[0][:], in0=gb(sm_[1]), in1=kb(), op=ALU.mult), reads=[sm_[1], kidx], writes=[tb[0]])
            P.op(ACT, lambda e: e.activation(out=tb[1][:], in_=tb[0][:], func=AF.Exp), reads=[tb[0]], writes=[tb[1]])
            P.op(ACT, lambda e: e.activation(out=tb[2][:], in_=tb[0][:], func=AF.Exp, scale=-1.0), reads=[tb[0]], writes=[tb[2]])
            P.op(DVE, lambda e: e.tensor_tensor(out=tb[3][:], in0=gb(sm_[2]), in1=kb(), op=ALU.mult), reads=[sm_[2], kidx], writes=[tb[3]])
            def sincos(dst, off):
                yi = T(tb[9].ap.bitcast(I32), "yi")
                P.op(DVE, lambda e: e.tensor_scalar(out=tb[0][:], in0=tb[3][:], scalar1=1.0 / TWO_PI, scalar2=off, op0=ALU.mult, op1=ALU.add),
                     reads=[tb[3]], writes=[tb[0]])
                P.op(DVE, lambda e: e.tensor_copy(out=yi[:], in_=tb[0][:]), reads=[tb[0]], writes=[yi, tb[9]])
                P.op(DVE, lambda e: e.tensor_copy(out=tb[8][:], in_=yi[:]), reads=[yi, tb[9]], writes=[tb[8]])
                P.op(DVE, lambda e: e.tensor_tensor(out=tb[0][:], in0=tb[0][:], in1=tb[8][:], op=ALU.subtract), reads=[tb[0], tb[8]], writes=[tb[0]])
                P.op(DVE, lambda e: e.tensor_single_scalar(out=tb[8][:], in_=tb[0][:], scalar=0.5, op=ALU.is_gt), reads=[tb[0]], writes=[tb[8]])
                P.op(DVE, lambda e: e.tensor_tensor(out=tb[0][:], in0=tb[0][:], in1=tb[8][:], op=ALU.subtract), reads=[tb[0], tb[8]], writes=[tb[0]])
                P.op(DVE, lambda e: e.tensor_single_scalar(out=tb[8][:], in_=tb[0][:], scalar=-0.5, op=ALU.is_lt), reads=[tb[0]], writes=[tb[8]])
                P.op(DVE, lambda e: e.tensor_tensor(out=tb[0][:], in0=tb[0][:], in1=tb[8][:], op=ALU.add), reads=[tb[0], tb[8]], writes=[tb[0]])
                P.op(ACT, lambda e: e.activation(out=dst[:], in_=tb[0][:], func=AF.Sin, scale=TWO_PI), reads=[tb[0]], writes=[dst])

            sincos(tb[4], 0.0)
            sincos(tb[5], 0.25)
            PWre, PWim, MWre, MWim = tb[6], tb[7], tb[8], tb[9]
            P.op(DVE, lambda e: e.tensor_tensor(out=PWre[:], in0=tb[1][:], in1=tb[5][:], op=ALU.mult), reads=[tb[1], tb[5]], writes=[PWre])
            P.op(DVE, lambda e: e.tensor_tensor(out=PWim[:], in0=tb[1][:], in1=tb[4][:], op=ALU.mult), reads=[tb[1], tb[4]], writes=[PWim])
            P.op(DVE, lambda e: e.tensor_tensor(out=MWre[:], in0=tb[2][:], in1=tb[5][:], op=ALU.mult), reads=[tb[2], tb[5]], writes=[MWre])
            P.op(DVE, lambda e: e.scalar_tensor_tensor(out=MWim[:], in0=tb[2][:], scalar=-1.0, in1=tb[4][:], op0=ALU.mult, op1=ALU.mult), reads=[tb[2], tb[4]], writes=[MWim])
            P.op(DVE, lambda e: e.tensor_copy(out=A1[:], in_=PWre[:, :, 7]), reads=[PWre], writes=[A1])
            P.op(DVE, lambda e: e.tensor_scalar_mul(out=A2[:], in0=PWim[:, :, 7], scalar1=sgn[:, 0:1]), reads=[PWim, sgn], writes=[A2])
            nr, den, qre, qim, t_ = sm_[3], sm_[4], sm_[5], sm_[0], sm_[1]
            P.op(DVE, lambda e: e.tensor_scalar_add(out=nr[:], in0=PWre[:, :, 0], scalar1=-1.0), reads=[PWre], writes=[nr])
            P.op(DVE, lambda e: e.tensor_tensor(out=den[:], in0=aR[:], in1=aR[:], op=ALU.mult), reads=[aR], writes=[den])
            P.op(DVE, lambda e: e.tensor_tensor(out=t_[:], in0=aI[:], in1=aI[:], op=ALU.mult), reads=[aI], writes=[t_])
            P.op(DVE, lambda e: e.tensor_tensor(out=den[:], in0=den[:], in1=t_[:], op=ALU.add), reads=[den, t_], writes=[den])
            P.op(DVE, lambda e: e.reciprocal(out=den[:], in_=den[:]), reads=[den], writes=[den])
            P.op(DVE, lambda e: e.tensor_tensor(out=qre[:], in0=nr[:], in1=aR[:], op=ALU.mult), reads=[nr, aR], writes=[qre])
            P.op(DVE, lambda e: e.tensor_tensor(out=t_[:], in0=PWim[:, :, 0], in1=aI[:], op=ALU.mult), reads=[PWim, aI], writes=[t_])
            P.op(DVE, lambda e: e.tensor_tensor(out=qre[:], in0=qre[:], in1=t_[:], op=ALU.add), reads=[qre, t_], writes=[qre])
            P.op(DVE, lambda e: e.tensor_tensor(out=qre[:], in0=qre[:], in1=den[:], op=ALU.mult), reads=[qre, den], writes=[qre])
            P.op(DVE, lambda e: e.tensor_tensor(out=qim[:], in0=PWim[:, :, 0], in1=aR[:], op=ALU.mult), reads=[PWim, aR], writes=[qim])
            P.op(DVE, lambda e: e.tensor_tensor(out=t_[:], in0=nr[:], in1=aI[:], op=ALU.mult), reads=[nr, aI], writes=[t_])
            P.op(DVE, lambda e: e.tensor_tensor(out=qim[:], in0=qim[:], in1=t_[:], op=ALU.subtract), reads=[qim, t_], writes=[qim])
            P.op(DVE, lambda e: e.tensor_tensor(out=qim[:], in0=qim[:], in1=den[:], op=ALU.mult), reads=[qim, den], writes=[qim])
            W1re, W1im = tb[0], tb[1]
            P.op(DVE, lambda e: e.tensor_tensor(out=W1re[:], in0=MWre[:], in1=gb(qre), op=ALU.mult), reads=[MWre, qre], writes=[W1re])
            P.op(DVE, lambda e: e.tensor_tensor(out=tb[2][:], in0=MWim[:], in1=gb(qim), op=ALU.mult), reads=[MWim, qim], writes=[tb[2]])
            P.op(DVE, lambda e: e.tensor_tensor(out=W1re[:], in0=W1re[:], in1=tb[2][:], op=ALU.subtract), reads=[W1re, tb[2]], writes=[W1re])
            P.op(DVE, lambda e: e.tensor_tensor(out=W1im[:], in0=MWim[:], in1=gb(qre), op=ALU.mult), reads=[MWim, qre], writes=[W1im])
            P.op(DVE, lambda e: e.tensor_tensor(out=tb[2][:], in0=MWre[:], in1=gb(qim), op=ALU.mult), reads=[MWre, qim], writes=[tb[2]])
            P.op(DVE, lambda e: e.tensor_tensor(out=W1im[:], in0=W1im[:], in1=tb[2][:], op=ALU.add), reads=[W1im, tb[2]], writes=[W1im])
            P.op(DVE, lambda e: e.tensor_scalar_mul(out=W1im[:], in0=W1im[:], scalar1=sgn[:, 0:1]), reads=[W1im, sgn], writes=[W1im])
            for (g0, g1) in parts:
                ng = g1 - g0
                P.dma(SP, lambda e, g0=g0, g1=g1: e.dma_start(out=inA[:], in_=b_A[:, g0:g1, :]), writes=[inA])
                P.dma(SP, lambda e, g0=g0, g1=g1: e.dma_start(out=inB[:], in_=b_B[:, g0:g1, :]), writes=[inB])
                xb4 = lambda t, g0=g0, g1=g1: t[:, g0:g1, :].rearrange("p g (k c) -> p g k c", k=8)
                wb4 = lambda t, g0=g0, g1=g1, ng=ng: t[:, g0:g1, :].unsqueeze(3).broadcast_to([128, ng, 8, 16])
                ib4 = lambda t, ng=ng: t[:].unsqueeze(2).broadcast_to([128, ng, 8, 16])
                P.op(DVE, lambda e, xb4=xb4, wb4=wb4, ib4=ib4: e.tensor_tensor(out=xb4(XB), in0=wb4(W1re), in1=ib4(inA), op=ALU.mult), reads=[W1re, inA], writes=[XB])
                P.op(POOL, lambda e, xb4=xb4, wb4=wb4, ib4=ib4: e.tensor_tensor(out=xb4(Wz), in0=wb4(W1im), in1=ib4(inB), op=ALU.mult), reads=[W1im, inB], writes=[Wz])
            P.op(DVE, lambda e: e.tensor_tensor(out=XB[:], in0=XB[:], in1=Wz[:], op=ALU.add), reads=[XB, Wz], writes=[XB])
            xb4 = lambda t: t[:].rearrange("p g (k c) -> p g k c", k=8)
            wb4 = lambda t: t[:].unsqueeze(3).broadcast_to([128, 64, 8, 16])
            ib4 = lambda t: t[:].unsqueeze(2).broadcast_to([128, 64, 8, 16])
            if full:
                PA, PB = tb[2], tb[3]
                P.op(DVE, lambda e: e.tensor_scalar_mul(out=PA[:], in0=PWre[:], scalar1=mtop[:, 0:1]), reads=[PWre, mtop], writes=[PA])
                P.op(DVE, lambda e: e.scalar_tensor_tensor(out=PA[:], in0=PWim[:], scalar=nmbot[:, 0:1], in1=PA[:], op0=ALU.mult, op1=ALU.add),
                     reads=[PWim, nmbot, PA], writes=[PA])
                P.op(DVE, lambda e: e.tensor_scalar_mul(out=PB[:], in0=PWre[:], scalar1=nmbot[:, 0:1]), reads=[PWre, nmbot], writes=[PB])
                P.op(DVE, lambda e: e.tensor_scalar(out=tb[4][:], in0=PWim[:], scalar1=mtop[:, 0:1], scalar2=-1.0, op0=ALU.mult, op1=ALU.mult),
                     reads=[PWim, mtop], writes=[tb[4]])
                P.op(DVE, lambda e: e.tensor_tensor(out=PB[:], in0=PB[:], in1=tb[4][:], op=ALU.add), reads=[PB, tb[4]], writes=[PB])
                P.dma(SP, lambda e: e.dma_start(out=inA[:], in_=c_A), writes=[inA])
                P.dma(SP, lambda e: e.dma_start(out=inB[:], in_=c_B), writes=[inB])
                P.op(DVE, lambda e: e.tensor_tensor(out=xb4(YC), in0=wb4(PA), in1=ib4(inA), op=ALU.mult), reads=[PA, inA], writes=[YC])
                P.op(POOL, lambda e: e.tensor_tensor(out=xb4(Wz), in0=wb4(PB), in1=ib4(inB), op=ALU.mult), reads=[PB, inB], writes=[Wz])
                P.op(DVE, lambda e: e.tensor_tensor(out=YC[:], in0=YC[:], in1=Wz[:], op=ALU.add), reads=[YC, Wz], writes=[YC])
                for q4 in range(4):
                    P.op(POOL, lambda e, q4=q4: e.iota(tmask4[:, q4, :], pattern=[[16, 8], [0, 16]], base=15, channel_multiplier=-1,
                                                      allow_small_or_imprecise_dtypes=True), writes=[tmask4])
                P.op(DVE, lambda e: e.tensor_single_scalar(out=tmask4[:], in_=tmask4[:], scalar=0.0, op=ALU.is_ge), reads=[tmask4], writes=[tmask4])
                for g4 in range(16):
                    ps = pf()
                    for gi in range(4):
                        g = g4 * 4 + gi
                        P.op(PE, lambda e, ps=ps, g=g, gi=gi: e.matmul(ps[:, gi * 128:(gi + 1) * 128], lhsT=XB[:, g, :], rhs=YC[:, g, :], start=True, stop=True),
                             reads=[XB, YC], writes=[ps])
                    P.op(DVE, lambda e, ps=ps: e.tensor_tensor(out=tmpf[:], in0=ps[:], in1=tmask4[:].rearrange("p a b -> p (a b)"), op=ALU.mult),
                         reads=[ps, tmask4], writes=[tmpf])
                    for gi in range(4):
                        g = g4 * 4 + gi
                        P.op(DVE, lambda e, g=g, gi=gi: e.scalar_tensor_tensor(out=Tt[:, g, :], in0=identf[:], scalar=dsk[:, g:g + 1],
                                                                              in1=tmpf[:, gi * 128:(gi + 1) * 128], op0=ALU.mult, op1=ALU.add),
                             reads=[identf, dsk, tmpf], writes=[Tt])
            for g8 in range(8):
                pb_ = pbk()
                for gi in range(8):
                    g = g8 * 8 + gi
                    P.op(PE, lambda e, pb_=pb_, g=g, gi=gi: e.transpose(out=pb_[:, gi * 128:(gi + 1) * 128], in_=XB[:, g, :], identity=identb[:]),
                         reads=[XB, identb], writes=[pb_])
                pb2 = pbk()
                for gi in range(8):
                    g = g8 * 8 + gi
                    P.op(PE, lambda e, pb2=pb2, g=g, gi=gi: e.transpose(out=pb2[:, gi * 128:(gi + 1) * 128], in_=XB[:, g, :], identity=identsw[:]),
                         reads=[XB, identsw], writes=[pb2])
                P.op(ACT, lambda e, pb_=pb_, g8=g8: e.copy(out=Wz[:, g8 * 8:(g8 + 1) * 8, :], in_=pb_[:].rearrange("p (g s) -> p g s", g=8)),
                     reads=[pb_], writes=[Wz])
                P.op(ACT, lambda e, pb2=pb2, g8=g8: e.copy(out=XB[:, g8 * 8:(g8 + 1) * 8, :], in_=pb2[:].rearrange("p (g s) -> p g s", g=8)),
                     reads=[pb2], writes=[XB])
            Wz.sw = XB
            return Wz, Tt, YC

        def ssm_blocks(h1T, ncols_prompt, Wz, Tt, YC, full, sample):
            RW.reset()
            NJ = 32
            Zt = RW.alloc("Zt", [64, NJ], BF16)
            Zs = RW.alloc("Zs", [64, NJ], BF16)
            Sall = RW.alloc("Sall", [64, NJ], BF16)
            UJ = RW.alloc("UJ", [8, 128], BF16)
            U8 = RW.alloc("U8", [8, NJ], BF16)
            GJ = RW.alloc("GJ", [8, 128], BF16)
            t1 = RW.alloc("t1", [64], F32)
            t2 = RW.alloc("t2", [64], F32)
            m1 = RW.alloc("m1", [64], F32)
            m2 = RW.alloc("m2", [64], F32)
            stS = RW.alloc("stS", [4, 64], F32)
            stW = RW.alloc("stW", [4, 64], F32)
            so = RW.alloc("so", [64, 4], F32)
            WzS = Wz.sw

            def make_u8(c0, nj, m):
                pb_ = pbk()
                for tp in range(8):
                    P.op(PE, lambda e, pb_=pb_, tp=tp: e.transpose(
                        out=pb_[0:nj, tp * 128:(tp + 1) * 128],
                        in_=h1T[:, m, c0:c0 + nj * 8].rearrange("p (j t) -> p t j", t=8)[:, tp, :], identity=identb[:]),
                        reads=[h1T, identb], writes=[pb_])
                P.op(ACT, lambda e, pb_=pb_: e.copy(out=UJ[0:nj].rearrange("p g (t c) -> p t g c", t=8), in_=pb_[0:nj, :].rearrange("p (t g c) -> p t g c", t=8, g=8)), reads=[pb_], writes=[UJ])
                pb2 = pbk()
                for gl in range(8):
                    P.op(PE, lambda e, pb2=pb2, gl=gl: e.transpose(out=pb2[:, gl * NJ:gl * NJ + nj], in_=UJ[0:nj, gl, :],
                                                                   identity=identb[0:nj, 0:nj]), reads=[UJ, identb], writes=[pb2])
                P.op(DVE, lambda e, pb2=pb2: e.tensor_copy(out=U8[:, :, 0:nj], in_=pb2[:, 0:8 * NJ].rearrange("p (g j) -> p g j", g=8)[:, :, 0:nj]),
                     reads=[pb2], writes=[U8])

            def to_state(c0, nj):
                for m in range(8):
                    make_u8(c0, nj, m)
                    ps = pf()
                    for gl in range(8):
                        g = m * 8 + gl
                        P.op(PE, lambda e, ps=ps, g=g, gl=gl: e.matmul(ps[:, gl * NJ:gl * NJ + nj], lhsT=Wz[:, g, :], rhs=U8[:, gl, 0:nj], start=True, stop=True),
                             reads=[Wz, U8], writes=[ps])
                        P.op(PE, lambda e, ps=ps, g=g, gl=gl: e.matmul(ps[:, 256 + gl * NJ:256 + gl * NJ + nj], lhsT=WzS[:, g, :], rhs=U8[:, gl, 0:nj],
                                                                       start=True, stop=True), reads=[WzS, U8], writes=[ps])
                    P.op(ACT, lambda e, ps=ps, m=m: e.copy(out=Zt[:, m * 8:(m + 1) * 8, 0:nj],
                                                           in_=ps[:, 0:256].rearrange("p (g j) -> p g j", g=8)[:, :, 0:nj]), reads=[ps], writes=[Zt])
                    P.op(ACT, lambda e, ps=ps, m=m: e.copy(out=Zs[:, m * 8:(m + 1) * 8, 0:nj],
                                                           in_=ps[:, 256:512].rearrange("p (g j) -> p g j", g=8)[:, :, 0:nj]), reads=[ps], writes=[Zs])

            def out_stage(c0, nj):
                for m in range(8):
                    make_u8(c0, nj, m)
                    for hb in range(2):
                        ps = pf()
                        for gi in range(4):
                            gl = hb * 4 + gi
                            g = m * 8 + gl
                            P.op(PE, lambda e, ps=ps, g=g, gl=gl, gi=gi: e.matmul(ps[0:nj, gi * 128:(gi + 1) * 128], lhsT=U8[:, gl, 0:nj], rhs=Tt[:, g, :],
                                                                                 start=True, stop=False), reads=[U8, Tt], writes=[ps])
                            P.op(PE, lambda e, ps=ps, g=g, gi=gi: e.matmul(ps[0:nj, gi * 128:(gi + 1) * 128], lhsT=Sall[:, g, 0:nj], rhs=YC[:, g, :],
                                                                          start=False, stop=True), reads=[Sall, YC], writes=[ps])
                        P.op(ACT, lambda e, ps=ps, hb=hb: e.activation(out=GJ[0:nj].rearrange("p t (g c) -> p g t c", g=8)[:, hb * 4:hb * 4 + 4], in_=ps[0:nj, :].rearrange("p (g t c) -> p g t c", g=4, t=8),
                                                                       func=AF.Gelu_apprx_tanh), reads=[ps], writes=[GJ])
                    pb_ = pbk()
                    for tp in range(8):
                        P.op(PE, lambda e, pb_=pb_, tp=tp: e.transpose(out=pb_[:, tp * NJ:tp * NJ + nj], in_=GJ[0:nj, tp, :],
                                                                       identity=identb[0:nj, 0:nj]), reads=[GJ, identb], writes=[pb_])
                    P.op(DVE, lambda e, pb_=pb_, m=m: e.tensor_copy(out=h1T[:, m, c0:c0 + nj * 8].rearrange("p (j t) -> p t j", t=8),
                                                                    in_=pb_[:, 0:8 * NJ].rearrange("p (t j) -> p t j", t=8)[:, :, 0:nj]),
                         reads=[pb_], writes=[h1T])

            for c0 in range(0, ncols_prompt, NJ * 8):
                to_state(c0, NJ)
                for j in range(NJ):
                    if full:
                        P.op(POOL, lambda e, j=j: e.tensor_copy(out=Sall[:, :, j], in_=Sst[:]), reads=[Sst], writes=[Sall])
                    P.op(DVE, lambda e, j=j: e.tensor_tensor(out=t1[:], in0=Sst[:], in1=Zt[:, :, j], op=ALU.add), reads=[Sst, Zt], writes=[t1])
                    P.op(DVE, lambda e, j=j: e.tensor_tensor(out=t2[:], in0=Ssw[:], in1=Zs[:, :, j], op=ALU.add), reads=[Ssw, Zs], writes=[t2])
                    P.op(DVE, lambda e: e.tensor_tensor(out=m1[:], in0=A1[:], in1=t1[:], op=ALU.mult), reads=[A1, t1], writes=[m1])
                    P.op(DVE, lambda e: e.tensor_tensor(out=m2[:], in0=A2[:], in1=t2[:], op=ALU.mult), reads=[A2, t2], writes=[m2])
                    P.op(DVE, lambda e: e.tensor_tensor(out=Sst[:], in0=m1[:], in1=m2[:], op=ALU.add), reads=[m1, m2], writes=[Sst])
                    P.op(DVE, lambda e: e.tensor_tensor(out=m1[:], in0=A1[:], in1=t2[:], op=ALU.mult), reads=[A1, t2], writes=[m1])
                    P.op(DVE, lambda e: e.tensor_tensor(out=m2[:], in0=A2[:], in1=t1[:], op=ALU.mult), reads=[A2, t1], writes=[m2])
                    P.op(DVE, lambda e: e.tensor_tensor(out=Ssw[:], in0=m1[:], in1=m2[:], op=ALU.subtract), reads=[m1, m2], writes=[Ssw])
                if full:
                    out_stage(c0, NJ)
            if full:
                P.dma(SP, lambda e: e.dma_start(out=so_p, in_=Sst[:]), reads=[Sst])
            if sample:
                c0 = NT
                P.dma(SP, lambda e: e.dma_start(out=stS[:], in_=st_s), writes=[stS])
                P.dma(SP, lambda e: e.dma_start(out=stW[:], in_=st_sw), writes=[stW])
                to_state(c0, 4)
                P.op(POOL, lambda e: e.tensor_copy(out=Sall[:, :, 0:4], in_=stS[:].rearrange("p s g -> p g s")), reads=[stS], writes=[Sall])
                out_stage(c0, 4)
                t1v = so[:]
                P.op(DVE, lambda e: e.tensor_tensor(out=so[:], in0=stS[:].rearrange("p s g -> p g s"), in1=Zt[:, :, 0:4], op=ALU.add), reads=[stS, Zt], writes=[so])
                sw = RW.alloc("sw", [64, 4], F32)
                P.op(DVE, lambda e: e.tensor_tensor(out=sw[:], in0=stW[:].rearrange("p s g -> p g s"), in1=Zs[:, :, 0:4], op=ALU.add), reads=[stW, Zs], writes=[sw])
                P.op(DVE, lambda e: e.tensor_tensor(out=so[:], in0=so[:], in1=A1[:].unsqueeze(2).broadcast_to([128, 64, 4]), op=ALU.mult), reads=[so, A1], writes=[so])
                P.op(DVE, lambda e: e.tensor_tensor(out=sw[:], in0=sw[:], in1=A2[:].unsqueeze(2).broadcast_to([128, 64, 4]), op=ALU.mult), reads=[sw, A2], writes=[sw])
                P.op(DVE, lambda e: e.tensor_tensor(out=stS[:].rearrange("p s g -> p g s"), in0=so[:], in1=sw[:], op=ALU.add), reads=[so, sw], writes=[stS])
                P.dma(SP, lambda e: e.dma_start(out=so_s, in_=stS[:]), reads=[stS])

        def layer1_norm(ncols):
            P.barrier_all()
            R3.reset()
            RW.reset()
            h1T = R3.alloc("h1T", [8, NTOT], BF16)
            sq = RW.alloc("sq", [512], BF16)
            rstd = RW.alloc("rstd", [512], F32)
            for c0 in range(0, ncols, 512):
                n = min(512, ncols - c0)
                rmsnorm_cols(xT, c0, n, gmix[:, 1, :], h1T, c0, sq, rstd)
            return h1T

        def ssm_pred():
            P.mark('ssm_pred start')
            h1T = layer1_norm(NT)
            P.mark('pred norm done')
            P.barrier_all()
            P.op(DVE, lambda e: e.memset(Sst[:], 0.0), writes=[Sst])
            P.op(DVE, lambda e: e.memset(Ssw[:], 0.0), writes=[Ssw])
            Wz, Tt, YC = ssm_tables(False)
            P.mark('pred tables done')
            ssm_blocks(h1T, NT, Wz, Tt, YC, False, False)
            P.mark('pred blocks done')

        def ssm_own():
            h1T = layer1_norm(NTOT)
            P.barrier_all()
            if not FLAGS["pred"]:
                P.op(DVE, lambda e: e.memset(Sst[:], 0.0), writes=[Sst])
                P.op(DVE, lambda e: e.memset(Ssw[:], 0.0), writes=[Ssw])
            P.mark('own ssm tables start')
            Wz, Tt, YC = ssm_tables(True)
            P.mark('own tables done')
            ssm_blocks(h1T, NT, Wz, Tt, YC, True, True)
            P.mark('own blocks done')
            return h1T

        def glu_mix(GT):
            RW.reset()
            wa = [RW.alloc(f"wa{i}", [8, 128], BF16) for i in range(2)]
            wb = [RW.alloc(f"wb{i}", [8, 128], BF16) for i in range(2)]
            sig = [RW.alloc(f"sig{i}", [512], F32) for i in range(2)]
            tmp = [RW.alloc(f"gtmp{i}", [512], F32) for i in range(2)]
            wv = w_glu.rearrange("(k p) c -> p k c", p=128)
            blocks = [(c, min(512, NTOT - c)) for c in range(0, NTOT, 512)]
            for cc in range(8):
                wat, wbt = wa[cc % 2], wb[cc % 2]
                P.dma(POOL, lambda e, wat=wat, cc=cc: e.dma_start(out=wat[:], in_=wv[:, :, cc * 128:(cc + 1) * 128]), writes=[wat])
                P.dma(POOL, lambda e, wbt=wbt, cc=cc: e.dma_start(out=wbt[:], in_=wv[:, :, D + cc * 128:D + (cc + 1) * 128]), writes=[wbt])
                for bi, (c0, n) in enumerate(blocks):
                    pa, pb2 = pf(), pf()
                    for k in range(8):
                        P.op(PE, lambda e, pa=pa, wat=wat, k=k, c0=c0, n=n: e.matmul(pa[:, 0:n], lhsT=wat[:, k, :], rhs=GT[:, k, c0:c0 + n],
                                                                                  start=(k == 0), stop=(k == 7)), reads=[wat, GT], writes=[pa])
                    for k in range(8):
                        P.op(PE, lambda e, pb2=pb2, wbt=wbt, k=k, c0=c0, n=n: e.matmul(pb2[:, 0:n], lhsT=wbt[:, k, :], rhs=GT[:, k, c0:c0 + n],
                                                                                    start=(k == 0), stop=(k == 7)), reads=[wbt, GT], writes=[pb2])
                    sg_, tp_ = sig[bi % 2], tmp[bi % 2]
                    P.op(ACT, lambda e, pb2=pb2, sg_=sg_, cc=cc, n=n: e.activation(out=sg_[:, 0:n], in_=pb2[:, 0:n], func=AF.Sigmoid,
                                                                                   bias=bglu[:, 8 + cc:9 + cc]), reads=[pb2, bglu], writes=[sg_])
                    P.op(DVE, lambda e, pa=pa, sg_=sg_, tp_=tp_, cc=cc, n=n: e.scalar_tensor_tensor(
                        out=tp_[:, 0:n], in0=pa[:, 0:n], scalar=bglu[:, cc:cc + 1], in1=sg_[:, 0:n], op0=ALU.add, op1=ALU.mult),
                        reads=[pa, sg_, bglu], writes=[tp_])
                    P.op(POOL, lambda e, tp_=tp_, cc=cc, c0=c0, n=n: e.tensor_tensor(out=xT[:, cc, c0:c0 + n], in0=xT[:, cc, c0:c0 + n], in1=tp_[:, 0:n],
                                                                                     op=ALU.add), reads=[tp_, xT], writes=[xT])

        def final_out():
            P.barrier_all()
            R3.reset()
            RW.reset()
            sq = RW.alloc("sq", [512], BF16)
            rstd = RW.alloc("rstd", [512], F32)
            yT = RW.alloc("yT", [8, 512], F32)
            ys = [RW.alloc(f"ys{i}", [1024], F32) for i in range(2)]
            ps_ss = psf[4]
            cnt_ = 0
            for c0 in range(0, NTOT, 512):
                n = min(512, NTOT - c0)
                for k in range(8):
                    P.op(ACT, lambda e, k=k, c0=c0, n=n: e.activation(out=sq[:, 0:n], in_=xT[:, k, c0:c0 + n], func=AF.Square), reads=[xT], writes=[sq])
                    P.op(PE, lambda e, k=k, n=n: e.matmul(ps_ss[:, 0:n], lhsT=onesb[:], rhs=sq[:, 0:n], start=(k == 0), stop=(k == 7)),
                         reads=[sq, onesb], writes=[ps_ss])
                P.op(ACT, lambda e, n=n: e.activation(out=rstd[:, 0:n], in_=ps_ss[:, 0:n], func=AF.Sqrt, bias=epsT[:, 0:1], scale=1.0 / D),
                     reads=[ps_ss, epsT], writes=[rstd])
                P.op(DVE, lambda e, n=n: e.reciprocal(out=rstd[:, 0:n], in_=rstd[:, 0:n]), reads=[rstd], writes=[rstd])
                for k in range(8):
                    P.op(DVE, lambda e, k=k, c0=c0, n=n: e.scalar_tensor_tensor(out=yT[:, k, 0:n], in0=xT[:, k, c0:c0 + n], scalar=gfin[:, k:k + 1],
                                                                                in1=rstd[:, 0:n], op0=ALU.mult, op1=ALU.mult), reads=[xT, rstd, gfin], writes=[yT])
                for tt in range((n + 127) // 128):
                    r = min(128, n - tt * 128)
                    yst = ys[cnt_ % 2]
                    cnt_ += 1
                    for kk in range(2):
                        ps = pf()
                        for k4 in range(4):
                            k = kk * 4 + k4
                            P.op(PE, lambda e, ps=ps, k=k, k4=k4, tt=tt, r=r: e.transpose(out=ps[0:r, k4 * 128:(k4 + 1) * 128],
                                                                                         in_=yT[:, k, tt * 128:tt * 128 + r], identity=identf[:]),
                                 reads=[yT, identf], writes=[ps])
                        P.op(ACT, lambda e, ps=ps, kk=kk, r=r, yst=yst: e.copy(out=yst[0:r, kk * 512:(kk + 1) * 512], in_=ps[0:r, :]), reads=[ps], writes=[yst])
                    dst = y_s if c0 == NT else y_p[c0 + tt * 128:c0 + tt * 128 + r, :]
                    P.dma(SP, lambda e, yst=yst, r=r, dst=dst: e.dma_start(out=dst, in_=yst[0:r, :]), reads=[yst])

        def act_view(off_bytes, nblk, ncols):
            v = R2.base[:, off_bytes // 4:(off_bytes + nblk * ncols * 2) // 4].bitcast(BF16)
            return T(v.rearrange("p (a b) -> p a b", a=nblk), "actT")

        if FLAGS["pred"]:
            layer0_mixer("pred")
            if FLAGS.get("ffn"):
                ffn(0, NT, act_view(33024, 8, NT), 8)
            if FLAGS.get("ssm"):
                ssm_pred()
        P.mark('own layer0 start')
        layer0_mixer("own")
        P.mark('own layer0 done')
        if FLAGS.get("ffn"):
            ffn(0, NTOT, act_view(0, 11, NTOT), 11)
        if FLAGS.get("ssm"):
            GT = ssm_own()
            if FLAGS.get('dbg') == 'GT':
                P.barrier_all()
                for k_ in range(8):
                    P.op(DVE, lambda e, k_=k_: e.tensor_copy(out=xT[:, k_, :], in_=GT[:, k_, :]), reads=[GT], writes=[xT])
            if FLAGS.get("glu"):
                P.barrier_all()
                glu_mix(GT)
                P.mark('glu done')
                if FLAGS.get('stop') != 'x3':
                    ffn(1, NTOT, act_view(0, 11, NTOT), 11)
                    P.mark('ffn1 done')
                    final_out()
        if FLAGS.get('dbg'):
            P.dma(SP, lambda e: e.dma_start(out=dbg_x, in_=xT[:]), reads=[xT])
        P.barrier_all()
        print('recorded ops', P.n)
        P.emit()
    return nc


_NC_CACHE = {}


def _prep(x_prompt, x_sample, cache_k, cache_v, page_table, state_ssm_re, state_ssm_im,
          norm_mix, norm_ffn, norm_final, w_in_even, w_out_even, sgu_norm, sgu_w, sgu_b,
          lambda_q1, lambda_k1, lambda_q2, lambda_k2, attn_subln,
          ssm_a_re, ssm_a_im, ssm_log_dt, ssm_b_re, ssm_b_im, ssm_c_re, ssm_c_im, ssm_d,
          w_glu, b_glu, w_ffn_in, w_ffn_out):
    f = lambda a: np.ascontiguousarray(np.asarray(a))
    x_prompt, x_sample = f(x_prompt), f(x_sample)
    ck = f(cache_k).reshape(-1, 512)
    cv = f(cache_v).reshape(-1, 512)

    def gam(a):
        a = f(a)
        return np.ascontiguousarray(a.reshape(a.shape[0], 8, 128).transpose(0, 2, 1))

    sgu_w = f(sgu_w)[0]
    sgu_wT = np.ascontiguousarray(sgu_w.transpose(0, 2, 1))
    sgu_wTs = np.zeros((4, 32, 32), np.float32)
    for s_ in range(4):
        sgu_wTs[:, s_ * 8:(s_ + 1) * 8, s_ * 8:(s_ + 1) * 8] = sgu_wT[:, :8, :8]
    sgu_b0 = f(sgu_b)[0]
    lam = np.stack([f(lambda_q1)[0], f(lambda_k1)[0], f(lambda_q2)[0], f(lambda_k2)[0]])
    dsk = f(ssm_d)[0].reshape(64, 16)
    d_skip = np.ascontiguousarray(np.tile(dsk.T[None, :, :], (8, 1, 1)).reshape(128, 64))
    bre_ = np.ascontiguousarray(f(ssm_b_re)[0].transpose(1, 0, 2))
    bim_ = np.ascontiguousarray(f(ssm_b_im)[0].transpose(1, 0, 2))
    cre_ = np.ascontiguousarray(f(ssm_c_re)[0].transpose(2, 0, 1))
    cim_ = np.ascontiguousarray(f(ssm_c_im)[0].transpose(2, 0, 1))
    common = {
        "cache_k": ck, "cache_v": cv,
        "gam_mix": gam(norm_mix), "gam_ffn": gam(norm_ffn), "gam_fin": gam(f(norm_final)[None])[0],
        "w_in": f(w_in_even)[0], "w_out": f(w_out_even)[0],
        "sgu_norm": f(sgu_norm)[0][None, :], "sgu_wT": sgu_wT, "sgu_wTs": sgu_wTs,
        "sgu_b": sgu_b0, "sgu_bs": np.ascontiguousarray(np.tile(sgu_b0[:, :8], (1, 4))),
        "lam": lam, "subln": f(attn_subln)[0][None, :],
        "a_re": np.ascontiguousarray(np.tile(f(ssm_a_re)[0].T, (2, 1))), "a_im": np.ascontiguousarray(np.tile(f(ssm_a_im)[0].T, (2, 1))),
        "logdt": np.ascontiguousarray(np.tile(f(ssm_log_dt)[0][None, :], (128, 1))),
        "b_A": np.concatenate([bre_, bim_], 0), "b_B": np.concatenate([bim_, bre_], 0),
        "c_A": np.concatenate([cre_, cre_], 0), "c_B": np.concatenate([cim_, cim_], 0),
        "d_skip": d_skip,
        "w_glu": f(w_glu)[0], "b_glu": np.ascontiguousarray(f(b_glu)[0].reshape(16, 128).T),
        "w_f1": f(w_ffn_in), "w_f2": f(w_ffn_out),
    }
    in_maps = []
    for c in range(8):
        b, half = c // 2, c % 2
        m = dict(common)
        m["x_own"] = x_prompt[b, half * NT:(half + 1) * NT]
        m["x_pred"] = x_prompt[b, 0:NT] if half == 1 else np.zeros((NT, D), np.float32)
        m["x_smp"] = x_sample[4 * c:4 * c + 4].reshape(NS, D)
        m["pbias"] = np.full((128, 1), 0.0 if half == 1 else -30000.0, np.float32)
        m["ptab"] = f(page_table)[4 * c:4 * c + 4].astype(np.int32)
        sre_ = np.ascontiguousarray(f(state_ssm_re)[0, 4 * c:4 * c + 4].transpose(2, 0, 1))
        sim_ = np.ascontiguousarray(f(state_ssm_im)[0, 4 * c:4 * c + 4].transpose(2, 0, 1))
        m["st_s"] = np.concatenate([sre_, sim_], 0)
        m["st_sw"] = np.concatenate([sim_, sre_], 0)
        in_maps.append(m)
    return in_maps


def kernel(**inputs):
    in_maps = _prep(**inputs)
    n_rows = in_maps[0]["cache_k"].shape[0]
    if n_rows not in _NC_CACHE:
        _NC_CACHE[n_rows] = build_program(n_rows)
    nc = _NC_CACHE[n_rows]
    res = run_bass_kernel_spmd(nc, in_maps, core_ids=list(range(8))).results

    B, S = 4, 4096
    y_prompt = np.zeros((B, S, D), np.float32)
    y_sample = np.zeros((32, 8, D), np.float32)
    k_p = np.zeros((1, B, S, 4, 128), np.float32)
    v_p = np.zeros((1, B, S, 4, 128), np.float32)
    k_s = np.zeros((1, 32, 8, 4, 128), np.float32)
    v_s = np.zeros((1, 32, 8, 4, 128), np.float32)
    cvs = np.zeros((1, 32, 8, 512), np.float32)
    sp_re = np.zeros((1, B, 64, 64), np.float32)
    sp_im = np.zeros((1, B, 64, 64), np.float32)
    ss_re = np.zeros((1, 32, 64, 64), np.float32)
    ss_im = np.zeros((1, 32, 64, 64), np.float32)
    for c in range(8):
        b, half = c // 2, c % 2
        r = res[c]
        sl = slice(half * NT, (half + 1) * NT)
        y_prompt[b, sl] = r["y_p"]
        y_sample[4 * c:4 * c + 4] = r["y_s"].reshape(4, 8, D)
        k_p[0, b, sl] = r["kr_p"].reshape(NT, 4, 128)
        v_p[0, b, sl] = r["vr_p"].reshape(NT, 4, 128)
        k_s[0, 4 * c:4 * c + 4] = r["kr_s"].reshape(4, 8, 4, 128)
        v_s[0, 4 * c:4 * c + 4] = r["vr_s"].reshape(4, 8, 4, 128)
        cvs[0, 4 * c:4 * c + 4] = r["cv_s"].reshape(4, 8, 512)
        if half == 1:
            sp_re[0, b] = r["so_p"][0:64].T
            sp_im[0, b] = r["so_p"][64:128].T
        ss_re[0, 4 * c:4 * c + 4] = r["so_s"][0:64].transpose(1, 2, 0)
        ss_im[0, 4 * c:4 * c + 4] = r["so_s"][64:128].transpose(1, 2, 0)
    return (y_prompt, y_sample, k_p, v_p, k_s, v_s, cvs, sp_re, sp_im, ss_re, ss_im)
```

```python
import contextlib
import math
import numpy as np
import concourse.bass as bass
import concourse.mybir as mybir
from concourse.bass_utils import run_bass_kernel_spmd

F32 = mybir.dt.float32
BF16 = mybir.dt.bfloat16
I32 = mybir.dt.int32
AF = mybir.ActivationFunctionType
ALU = mybir.AluOpType

PE, ACT, DVE, POOL, SP = "pe", "act", "dve", "pool", "sp"
COMPUTE = (PE, ACT, DVE, POOL)
NRING = {SP: 12, POOL: 8}

D = 1024
NT = 2048
NS = 32
NTOT = NT + NS
NPH = 2560
DFF = 2816
EPS = 1e-6
LAM_INIT = 0.8 - 0.6 * math.exp(-0.3 * 0)
SCALE = 64 ** -0.5
NPAGE = 64
FLAGS = dict(sgu=True, attn=True, outp=True, smp_attn=True, pred=True, ffn=True, ssm=True, glu=True)


class T:
    __slots__ = ("ap", "lw", "rd", "name", "sw")

    def __init__(self, ap, name=""):
        self.ap = ap
        self.lw = None
        self.rd = []
        self.name = name

    def __getitem__(self, idx):
        return self.ap[idx]


class Prog:
    def __init__(self, nc):
        self.nc = nc
        self.streams = {e: [] for e in (PE, ACT, DVE, POOL, SP)}
        self.cnt = {e: 0 for e in COMPUTE}
        self.seen = {e: {} for e in (PE, ACT, DVE, POOL, SP)}
        self.ring_pos = {q: 0 for q in NRING}
        self.ring_cnt = {q: [0] * NRING[q] for q in NRING}
        self.n = 0
        self.limit = FLAGS.get('limit', 10 ** 9)
        self.scope = 'setup'
        self.pe_free = False

    def _need(self, eng, events):
        seen = self.seen[eng]
        best = {}
        for ev in events:
            if ev is None:
                continue
            k, v = ev
            if best.get(k, 0) < v:
                best[k] = v
        out = []
        for k, v in best.items():
            if seen.get(k, 0) >= v:
                continue
            seen[k] = v
            out.append((k, v))
        return out

    @staticmethod
    def _deps(reads, writes):
        evs = []
        for t in reads:
            evs.append(t.lw)
        for t in writes:
            evs.append(t.lw)
            evs.extend(t.rd)
        return evs

    @staticmethod
    def _commit(ev, reads, writes):
        for t in reads:
            t.rd.append(ev)
            if len(t.rd) > 16:
                best = {}
                for k, v in t.rd:
                    if best.get(k, 0) < v:
                        best[k] = v
                t.rd = list(best.items())
        for t in writes:
            t.lw = ev
            t.rd = []

    def op(self, eng, fn, reads=(), writes=()):
        self.n += 1
        if self.n > self.limit:
            return None
        waits = self._need(eng, [ev for ev in self._deps(reads, writes) if ev is not None and not (self.pe_free and eng == PE and ev[0] == PE)])
        self.cnt[eng] += 1
        ev = (eng, self.cnt[eng])
        self.streams[eng].append((waits, fn, ev, 1, self.scope))
        self._commit(ev, reads, writes)
        return ev

    def dma(self, q, fn, reads=(), writes=()):
        self.n += 1
        if self.n > self.limit:
            return None
        n = NRING[q]
        slot = self.ring_pos[q] % n
        self.ring_pos[q] += 1
        key = ("ring", q, slot)
        prev = self.ring_cnt[q][slot]
        evs = self._deps(reads, writes)
        if prev > 0:
            evs.append((key, prev))
        waits = self._need(q, evs)
        self.ring_cnt[q][slot] = prev + 16
        ev = (key, prev + 16)
        self.streams[q].append((waits, fn, ev, 16, self.scope))
        self._commit(ev, reads, writes)
        return ev

    def mark(self, name):
        self.scope = name.replace(' ', '_')
        if FLAGS.get('marks'):
            print('MARK', name, self.n)

    def barrier_all(self):
        evs = [(e, self.cnt[e]) for e in COMPUTE if self.cnt[e] > 0]
        for q in NRING:
            for s in range(NRING[q]):
                if self.ring_cnt[q][s] > 0:
                    evs.append((("ring", q, s), self.ring_cnt[q][s]))
        for e in (PE, ACT, DVE, POOL, SP):
            waits = self._need(e, evs)
            if waits:
                self.streams[e].append((waits, None, None, 0, self.scope))

    def emit(self):
        nc = self.nc
        keys = set()
        for e in self.streams:
            for waits, fn, ev, inc, _sc in self.streams[e]:
                for k, v in waits:
                    keys.add(k)
                if ev is not None:
                    keys.add(ev[0])
        keys = sorted(keys, key=str)
        needed = {e: set() for e in COMPUTE}
        for e in self.streams:
            for waits, fn, ev, inc, _sc in self.streams[e]:
                for k, v in waits:
                    if k in needed:
                        needed[k].add(v)
        rank = {e: {v: i + 1 for i, v in enumerate(sorted(needed[e]))} for e in COMPUTE}
        if FLAGS.get('allsem'):
            rank = {}
        with contextlib.ExitStack() as st:
            sems = {}
            for k in keys:
                nm = "s_" + "_".join(str(x) for x in (k if isinstance(k, tuple) else (k,)))
                sems[k] = st.enter_context(nc.semaphore(nm))
            block = st.enter_context(nc.Block())

            def run(stream, eh):
                use_scopes = FLAGS.get('scopes')
                for waits, fn, ev, inc, sc in stream:
                    for k, v in waits:
                        eh.wait_ge(sems[k], rank[k][v] if k in rank else v)
                    if fn is None:
                        continue
                    if use_scopes:
                        with nc.named_scope(sc):
                            ins = fn(eh)
                    else:
                        ins = fn(eh)
                    if ev[0] in rank:
                        if ev[1] in rank[ev[0]]:
                            ins.then_inc(sems[ev[0]], 1)
                    else:
                        ins.then_inc(sems[ev[0]], inc)

            @block.tensor
            def _(e):
                run(self.streams[PE], e)

            @block.scalar
            def _(e):
                run(self.streams[ACT], e)

            @block.vector
            def _(e):
                run(self.streams[DVE], e)

            @block.gpsimd
            def _(e):
                run(self.streams[POOL], e)

            @block.sync
            def _(e):
                run(self.streams[SP], e)


class Arena:
    def __init__(self, nc, st, name, nbytes):
        self.base = st.enter_context(nc.sbuf_tensor(name, [128, nbytes // 4], F32))
        self.cap = nbytes
        self.off = 0
        self.name = name

    def reset(self):
        self.off = 0

    def alloc(self, name, free_shape, dt):
        esz = 2 if dt == BF16 else 4
        n = 1
        for s in free_shape:
            n *= s
        nb = (n * esz + 31) // 32 * 32
        assert self.off + nb <= self.cap, f"arena {self.name} overflow at {name}: {self.off}+{nb}>{self.cap}"
        ap = self.base[:, self.off // 4:(self.off + nb) // 4]
        if dt == BF16:
            ap = ap.bitcast(BF16)
        elif dt == I32:
            ap = ap.bitcast(I32)
        ap = ap[:, 0:n]
        if len(free_shape) == 2:
            ap = ap.rearrange("p (a b) -> p a b", a=free_shape[0])
        elif len(free_shape) == 3:
            ap = ap.rearrange("p (a b c) -> p a b c", a=free_shape[0], b=free_shape[1])
        self.off += nb
        return T(ap, name)


def build_program(n_rows_cache):
    nc = bass.Bass("TRN2", target_bir_lowering=False)

    def din(name, shape, dt=F32):
        return nc.dram_tensor(name, list(shape), dt, kind="ExternalInput").ap()

    def dout(name, shape, dt=F32):
        return nc.dram_tensor(name, list(shape), dt, kind="ExternalOutput").ap()

    x_own = din("x_own", [NT, D])
    x_pred = din("x_pred", [NT, D])
    x_smp = din("x_smp", [NS, D])
    pbias_d = din("pbias", [128, 1])
    cache_k = din("cache_k", [n_rows_cache, 512])
    cache_v = din("cache_v", [n_rows_cache, 512])
    ptab = din("ptab", [4, NPAGE], I32)
    st_s = din("st_s", [128, 4, 64])
    st_sw = din("st_sw", [128, 4, 64])
    gam_mix = din("gam_mix", [2, 128, 8])
    gam_ffn = din("gam_ffn", [2, 128, 8])
    gam_fin = din("gam_fin", [128, 8])
    w_in = din("w_in", [D, NPH])
    w_out = din("w_out", [D, D])
    sgu_norm_b = din("sgu_norm", [1, 512])
    sgu_wT = din("sgu_wT", [4, 128, 128])
    sgu_wTs = din("sgu_wTs", [4, 32, 32])
    sgu_b = din("sgu_b", [4, 128])
    sgu_bs = din("sgu_bs", [4, 32])
    lam_d = din("lam", [4, 64])
    subln_d = din("subln", [1, 128])
    a_re = din("a_re", [128, 64])
    a_im = din("a_im", [128, 64])
    logdt = din("logdt", [128, 64])
    b_A = din("b_A", [128, 64, 16])
    b_B = din("b_B", [128, 64, 16])
    c_A = din("c_A", [128, 64, 16])
    c_B = din("c_B", [128, 64, 16])
    d_skip = din("d_skip", [128, 64])
    w_glu = din("w_glu", [D, 2 * D])
    b_glu = din("b_glu", [128, 16])
    w_f1 = din("w_f1", [2, D, 2 * DFF])
    w_f2 = din("w_f2", [2, DFF, D])

    y_p = dout("y_p", [NT, D])
    y_s = dout("y_s", [NS, D])
    kr_p = dout("kr_p", [NT, 512])
    vr_p = dout("vr_p", [NT, 512])
    kr_s = dout("kr_s", [NS, 512])
    vr_s = dout("vr_s", [NS, 512])
    cv_s = dout("cv_s", [NS, 512])
    so_p = dout("so_p", [128, 64])
    so_s = dout("so_s", [128, 4, 64])
    dbg_x = dout("dbg_x", [128, 8, NTOT]) if FLAGS.get("dbg") else None

    P = Prog(nc)
    with contextlib.ExitStack() as st:
        R1 = Arena(nc, st, "R1", 8 * NTOT * 4)
        R2 = Arena(nc, st, "R2", 66560)
        R3 = Arena(nc, st, "R3", 8 * NTOT * 2)
        RW = Arena(nc, st, "RW", 30720)
        RC = Arena(nc, st, "RC", 14336)
        psf = [T(st.enter_context(nc.psum_tensor(f"psf{i}", [128, 512], F32)), f"psf{i}") for i in range(6)]
        psb = [T(st.enter_context(nc.psum_tensor(f"psb{i}", [128, 1024], BF16)), f"psb{i}") for i in range(2)]
        rot = {"f": 0, "b": 0, "n": 4}

        def pf():
            rot["f"] = (rot["f"] + 1) % rot["n"]
            return psf[rot["f"]]

        def pbk():
            rot["b"] = (rot["b"] + 1) % 2
            return psb[rot["b"]]

        identf = RC.alloc("identf", [128], F32)
        identb = RC.alloc("identb", [128], BF16)
        trimask = RC.alloc("trimask", [128], F32)
        trimb = RC.alloc("trimb", [128], BF16)
        identsw = RC.alloc("identsw", [128], BF16)
        onesb = RC.alloc("onesb", [128], BF16)
        epsT = RC.alloc("epsT", [1], F32)
        gmix = RC.alloc("gmix", [2, 8], F32)
        gffn = RC.alloc("gffn", [2, 8], F32)
        gfin = RC.alloc("gfin", [8], F32)
        pbias = RC.alloc("pbias", [1], F32)
        zbias = RC.alloc("zbias", [1], F32)
        lamneg = RC.alloc("lamneg", [1], F32)
        sublnb = RC.alloc("sublnb", [128], F32)
        sgunb = RC.alloc("sgunb", [512], F32)
        bglu = RC.alloc("bglu", [16], F32)
        sgub = RC.alloc("sgub", [4, 128], BF16)
        sgubs = RC.alloc("sgubs", [4, 32], BF16)
        wmT = RC.alloc("wmT", [4, 128], BF16)
        wmTs = RC.alloc("wmTs", [4, 32], BF16)
        tmpc = RC.alloc("tmpc", [512], F32)
        tmpc2 = RC.alloc("tmpc2", [512], F32)

        P.op(POOL, lambda e: e.iota(tmpc[:, 0:128], pattern=[[1, 128]], base=0, channel_multiplier=-1,
                                    allow_small_or_imprecise_dtypes=True), writes=[tmpc])
        P.op(DVE, lambda e: e.tensor_single_scalar(out=identf[:], in_=tmpc[:, 0:128], scalar=0.0, op=ALU.is_equal),
             reads=[tmpc], writes=[identf])
        P.op(DVE, lambda e: e.tensor_single_scalar(out=trimask[:], in_=tmpc[:, 0:128], scalar=0.0, op=ALU.is_ge),
             reads=[tmpc], writes=[trimask])
        P.op(DVE, lambda e: e.tensor_copy(out=identb[:], in_=identf[:]), reads=[identf], writes=[identb])
        P.op(DVE, lambda e: e.tensor_copy(out=trimb[:], in_=trimask[:]), reads=[trimask], writes=[trimb])
        P.op(DVE, lambda e: e.tensor_tensor(out=tmpc2[:, 0:128], in0=tmpc[:, 0:128], in1=tmpc[:, 0:128], op=ALU.mult), reads=[tmpc], writes=[tmpc2])
        P.op(DVE, lambda e: e.tensor_single_scalar(out=identsw[:], in_=tmpc2[:, 0:128], scalar=4096.0, op=ALU.is_equal),
             reads=[tmpc2], writes=[identsw])
        P.op(DVE, lambda e: e.memset(onesb[:], 1.0), writes=[onesb])
        P.op(DVE, lambda e: e.memset(epsT[:], EPS), writes=[epsT])
        P.op(DVE, lambda e: e.memset(zbias[:], 0.0), writes=[zbias])
        P.dma(SP, lambda e: e.dma_start(out=gmix[:], in_=gam_mix.rearrange("l p k -> p l k")), writes=[gmix])
        P.dma(SP, lambda e: e.dma_start(out=gffn[:], in_=gam_ffn.rearrange("l p k -> p l k")), writes=[gffn])
        P.dma(SP, lambda e: e.dma_start(out=gfin[:], in_=gam_fin), writes=[gfin])
        P.dma(SP, lambda e: e.dma_start(out=pbias[:], in_=pbias_d), writes=[pbias])
        P.dma(SP, lambda e: e.dma_start(out=bglu[:], in_=b_glu), writes=[bglu])
        P.dma(SP, lambda e: e.dma_start(out=sublnb[:], in_=subln_d.partition_broadcast(128).rearrange("p a f -> p (a f)")),
              writes=[sublnb])
        P.dma(SP, lambda e: e.dma_start(out=sgunb[:], in_=sgu_norm_b.partition_broadcast(128).rearrange("p a f -> p (a f)")),
              writes=[sgunb])
        P.op(DVE, lambda e: e.tensor_scalar_mul(out=sublnb[:], in0=sublnb[:], scalar1=1.0 - LAM_INIT),
             reads=[sublnb], writes=[sublnb])
        P.dma(SP, lambda e: e.dma_start(out=tmpc[:, 0:256], in_=lam_d.rearrange("(o a) f -> o (a f)", o=1)
                                        .partition_broadcast(128).rearrange("p a f -> p (a f)")), writes=[tmpc])
        P.op(DVE, lambda e: e.tensor_tensor(out=tmpc2[:, 0:64], in0=tmpc[:, 0:64], in1=tmpc[:, 64:128], op=ALU.mult),
             reads=[tmpc], writes=[tmpc2])
        P.op(DVE, lambda e: e.tensor_tensor(out=tmpc2[:, 64:128], in0=tmpc[:, 128:192], in1=tmpc[:, 192:256], op=ALU.mult),
             reads=[tmpc], writes=[tmpc2])
        P.op(DVE, lambda e: e.tensor_reduce(out=tmpc2[:, 128:130], in_=tmpc2[:, 0:128].rearrange("p (a b) -> p a b", a=2),
                                            axis=mybir.AxisListType.X, op=ALU.add), reads=[tmpc2], writes=[tmpc2])
        P.op(ACT, lambda e: e.activation(out=tmpc2[:, 130:132], in_=tmpc2[:, 128:130], func=AF.Exp),
             reads=[tmpc2], writes=[tmpc2])
        P.op(DVE, lambda e: e.tensor_tensor(out=lamneg[:], in0=tmpc2[:, 131:132], in1=tmpc2[:, 130:131], op=ALU.subtract),
             reads=[tmpc2], writes=[lamneg])
        P.op(DVE, lambda e: e.tensor_scalar_add(out=lamneg[:], in0=lamneg[:], scalar1=-LAM_INIT),
             reads=[lamneg], writes=[lamneg])
        for g in range(4):
            P.dma(SP, lambda e, g=g: e.dma_start(out=tmpc[:, 0:128], in_=sgu_wT[g]), writes=[tmpc])
            P.op(DVE, lambda e, g=g: e.tensor_tensor(out=wmT[:, g, :], in0=tmpc[:, 0:128], in1=trimask[:], op=ALU.mult),
                 reads=[tmpc, trimask], writes=[wmT])
            P.dma(SP, lambda e, g=g: e.dma_start(out=tmpc2[0:32, 0:32], in_=sgu_wTs[g]), writes=[tmpc2])
            P.op(DVE, lambda e, g=g: e.tensor_tensor(out=wmTs[0:32, g, :], in0=tmpc2[0:32, 0:32], in1=trimask[0:32, 0:32],
                                                     op=ALU.mult), reads=[tmpc2, trimask], writes=[wmTs])
        P.dma(SP, lambda e: e.dma_start(out=tmpc[0:1, 0:512], in_=sgu_b.rearrange("(o g) t -> o (g t)", o=1)), writes=[tmpc])
        P.op(DVE, lambda e: e.tensor_copy(out=sgub[0:1], in_=tmpc[0:1, 0:512].rearrange("p (g t) -> p g t", g=4)),
             reads=[tmpc], writes=[sgub])
        P.dma(SP, lambda e: e.dma_start(out=tmpc2[0:1, 0:128], in_=sgu_bs.rearrange("(o g) t -> o (g t)", o=1)), writes=[tmpc2])
        P.op(DVE, lambda e: e.tensor_copy(out=sgubs[0:1], in_=tmpc2[0:1, 0:128].rearrange("p (g t) -> p g t", g=4)),
             reads=[tmpc2], writes=[sgubs])

        def rmsnorm_cols(xT, c0, n, gam_ap, hT, h0, sq, rstd):
            ps = pf()
            for k in range(8):
                P.op(ACT, lambda e, k=k: e.activation(out=sq[:, 0:n], in_=xT[:, k, c0:c0 + n], func=AF.Square),
                     reads=[xT], writes=[sq])
                P.op(PE, lambda e, k=k: e.matmul(ps[:, 0:n], lhsT=onesb[:], rhs=sq[:, 0:n], start=(k == 0), stop=(k == 7)),
                     reads=[sq, onesb], writes=[ps])
            P.op(ACT, lambda e: e.activation(out=rstd[:, 0:n], in_=ps[:, 0:n], func=AF.Sqrt, bias=epsT[:, 0:1], scale=1.0 / D),
                 reads=[ps, epsT], writes=[rstd])
            P.op(DVE, lambda e: e.reciprocal(out=rstd[:, 0:n], in_=rstd[:, 0:n]), reads=[rstd], writes=[rstd])
            for k in range(8):
                P.op(DVE, lambda e, k=k: e.scalar_tensor_tensor(out=hT[:, k, h0:h0 + n], in0=xT[:, k, c0:c0 + n],
                                                                scalar=gam_ap[:, k:k + 1], in1=rstd[:, 0:n],
                                                                op0=ALU.mult, op1=ALU.mult),
                     reads=[xT, rstd], writes=[hT])

        def load_w(dst, src_ap, q=POOL):
            P.dma(q, lambda e: e.dma_start(out=dst.ap if isinstance(dst, T) else dst, in_=src_ap), writes=[dst] if isinstance(dst, T) else [])

        KTp = R2.alloc("KTp", [4, 2048], BF16)
        V1p = R2.alloc("V1p", [16, 4, 130], BF16)
        KTo = R2.alloc("KTo", [4, 2048], BF16)
        V1o = R2.alloc("V1o", [16, 4, 130], BF16)
        KTt = [T((KTp if i < 16 else KTo).ap, f"KTt{i}") for i in range(32)]
        V1t = [T((V1p if i < 16 else V1o).ap, f"V1t{i}") for i in range(32)]

        def ktile(j, half, h):
            return KTt[j][half * 64:(half + 1) * 64, h, (j % 16) * 128:(j % 16 + 1) * 128]

        def vtile(j, h):
            return V1t[j][:, j % 16, h, 0:129]
        xT = R1.alloc("xT", [8, NTOT], F32)

        def layer0_mixer(pas):
            own = pas == "own"
            xsrc = x_own if own else x_pred
            kbase = 2048 if own else 0
            P.barrier_all()
            R3.reset()
            RW.reset()
            vt_ = V1o if own else V1p
            P.op(POOL, lambda e: e.memset(vt_[:, :, :, 128:130], 1.0), writes=[V1t[i + (16 if own else 0)] for i in range(16)])
            hT = R3.alloc("hT", [8, 512], BF16)
            uT = R3.alloc("uT", [4, 512], BF16)
            qT = R3.alloc("qT", [4, 512], BF16)
            aT = R3.alloc("aT", [4, 512], BF16)
            bT = R3.alloc("bT", [4, 512], BF16)
            vn = [R3.alloc(f"vn{i}", [512], BF16) for i in range(4)]
            sq = R3.alloc("sq", [512], BF16)
            rstd = R3.alloc("rstd", [512], F32)
            xs = [RW.alloc(f"xs{i}", [1024], F32) for i in range(2)]
            wtm = [RW.alloc(f"wtm{i}", [8, 512], BF16) for i in range(1)]
            rows = [RW.alloc(f"rows{i}", [512], F32) for i in range(2)]
            gl = RW.alloc("gl", [512], F32)
            wfm = [RW.alloc(f"wfm{i}", [8, 128], BF16) for i in range(2)]
            PT = [RW.alloc(f"PT{i}", [512], BF16) for i in range(2)]
            osb = RW.alloc("osb", [128], F32)
            onb = RW.alloc("onb", [128], BF16)
            sm = RW.alloc("sm", [8], F32)
            cnt = {"xs": 0, "wfm": 0, "rows": 0, "pt": 0}
            w_in_v = w_in.rearrange("(k p) c -> p k c", p=128)
            w_out_v = w_out.rearrange("(k p) c -> p k c", p=128)

            blocks = [(b * 512, 512) for b in range(4)] + ([(NT, NS)] if own else [])
            def do_block(c0, n):
                smp = c0 == NT
                ntt = (n + 127) // 128
                for tt in range(ntt):
                    r = min(128, n - tt * 128)
                    xs_t = xs[cnt["xs"] % 2]
                    cnt["xs"] += 1
                    src = x_smp if smp else xsrc[c0 + tt * 128:c0 + tt * 128 + r, :]
                    P.dma(SP, lambda e, xs_t=xs_t, src=src, r=r: e.dma_start(out=xs_t[0:r, :], in_=src), writes=[xs_t])
                    for kk in range(2):
                        ps = pf()
                        for k4 in range(4):
                            k = kk * 4 + k4
                            P.op(PE, lambda e, ps=ps, k=k, k4=k4, xs_t=xs_t, r=r: e.transpose(
                                out=ps[:, k4 * 128:k4 * 128 + r], in_=xs_t[0:r, k * 128:(k + 1) * 128], identity=identf[0:r, 0:r]),
                                reads=[xs_t, identf], writes=[ps])
                        P.op(ACT, lambda e, ps=ps, kk=kk, tt=tt, r=r: e.copy(
                            out=xT[:, kk * 4:kk * 4 + 4, c0 + tt * 128:c0 + tt * 128 + r],
                            in_=ps[:].rearrange("p (a b) -> p a b", a=4)[:, :, 0:r]), reads=[ps], writes=[xT])
                rmsnorm_cols(xT, c0, n, gmix[:, 0, :], hT, 0, sq, rstd)
                for cc in [0, 1, 2, 3, 8, 9, 10, 11, 12, 13, 14, 15]:
                    if cc in (8, 9, 10, 11) and not own:
                        pass
                    wt = wfm[cnt["wfm"] % 2]
                    cnt["wfm"] += 1
                    P.dma(POOL, lambda e, wt=wt, cc=cc: e.dma_start(out=wt[:], in_=w_in_v[:, :, cc * 128:(cc + 1) * 128]), writes=[wt])
                    ps = pf()
                    for k in range(8):
                        P.op(PE, lambda e, ps=ps, wt=wt, k=k: e.matmul(ps[:, 0:n], lhsT=wt[:, k, :], rhs=hT[:, k, 0:n],
                                                                      start=(k == 0), stop=(k == 7)),
                             reads=[wt, hT], writes=[ps])
                    if cc < 4:
                        P.op(ACT, lambda e, ps=ps, cc=cc: e.activation(out=uT[:, cc, 0:n], in_=ps[:, 0:n], func=AF.Gelu_apprx_tanh),
                             reads=[ps], writes=[uT])
                    elif cc < 12:
                        P.op(ACT, lambda e, ps=ps, cc=cc: e.copy(out=qT[:, cc - 8, 0:n], in_=ps[:, 0:n]), reads=[ps], writes=[qT])
                    else:
                        h = cc - 12
                        if smp:
                            P.op(DVE, lambda e, ps=ps, h=h: e.tensor_copy(out=KTs[:, h, 0:n], in_=ps[:, 0:n]), reads=[ps], writes=[KTs_t])
                        else:
                            for tt in range(4):
                                ti = (kbase + c0) // 128 + tt
                                P.op(DVE, lambda e, ps=ps, h=h, tt=tt, ti=ti: e.tensor_copy(
                                    out=KTt[ti][:, h, (ti % 16) * 128:(ti % 16 + 1) * 128], in_=ps[:, tt * 128:(tt + 1) * 128]),
                                    reads=[ps], writes=[KTt[ti]])
                for gi, col0 in enumerate((512, 1536, 2048)):
                    if gi == 1 and not own:
                        continue
                    wt = wtm[0]
                    for hh in range(2):
                        P.dma(POOL, lambda e, wt=wt, col0=col0, hh=hh: e.dma_start(
                            out=wt[:, hh * 4:hh * 4 + 4, :], in_=w_in_v[:, hh * 4:hh * 4 + 4, col0:col0 + 512]), writes=[wt])
                    for tt in range(ntt):
                        r = min(128, n - tt * 128)
                        ps = pf()
                        for k in range(8):
                            P.op(PE, lambda e, ps=ps, wt=wt, k=k, tt=tt, r=r: e.matmul(
                                ps[0:r, :], lhsT=hT[:, k, tt * 128:tt * 128 + r], rhs=wt[:, k, :], start=(k == 0), stop=(k == 7)),
                                reads=[wt, hT], writes=[ps])
                        if gi == 0:
                            P.op(ACT, lambda e, ps=ps, r=r: e.activation(out=gl[0:r, :], in_=ps[0:r, :], func=AF.Gelu_apprx_tanh),
                                 reads=[ps], writes=[gl])
                            rw = rows[cnt["rows"] % 2]
                            cnt["rows"] += 1
                            P.op(ACT, lambda e, r=r, rw=rw: e.activation(out=rw[0:r, :], in_=gl[0:r, :], func=AF.Square,
                                                                         accum_out=sm[0:r, 0:1]), reads=[gl], writes=[rw, sm])
                            P.op(ACT, lambda e, r=r: e.activation(out=sm[0:r, 1:2], in_=sm[0:r, 0:1], func=AF.Sqrt,
                                                                  bias=epsT[0:r, 0:1], scale=1.0 / 512), reads=[sm, epsT], writes=[sm])
                            P.op(DVE, lambda e, r=r: e.reciprocal(out=sm[0:r, 2:3], in_=sm[0:r, 1:2]), reads=[sm], writes=[sm])
                            if smp:
                                P.op(DVE, lambda e, r=r, rw=rw: e.scalar_tensor_tensor(
                                    out=rw[0:r, :], in0=gl[0:r, :], scalar=sm[0:r, 2:3], in1=sgunb[0:r, :], op0=ALU.mult, op1=ALU.mult),
                                    reads=[gl, sm, sgunb], writes=[rw])
                                P.dma(SP, lambda e, rw=rw, r=r: e.dma_start(out=cv_s, in_=rw[0:r, :]), reads=[rw])
                                P.op(DVE, lambda e, r=r, rw=rw, tt=tt: e.tensor_copy(out=vn[tt][0:r, :], in_=rw[0:r, :]),
                                     reads=[rw], writes=[vn[tt]])
                            else:
                                P.op(DVE, lambda e, r=r, tt=tt: e.scalar_tensor_tensor(
                                    out=vn[tt][0:r, :], in0=gl[0:r, :], scalar=sm[0:r, 2:3], in1=sgunb[0:r, :], op0=ALU.mult, op1=ALU.mult),
                                    reads=[gl, sm, sgunb], writes=[vn[tt]])
                        else:
                            rw = rows[cnt["rows"] % 2]
                            cnt["rows"] += 1
                            P.op(ACT, lambda e, ps=ps, r=r, rw=rw: e.copy(out=rw[0:r, :], in_=ps[0:r, :]), reads=[ps], writes=[rw])
                            if own:
                                if smp:
                                    dst = kr_s if gi == 1 else vr_s
                                else:
                                    dst = (kr_p if gi == 1 else vr_p)[c0 + tt * 128:c0 + tt * 128 + r, :]
                                P.dma(SP, lambda e, rw=rw, r=r, dst=dst: e.dma_start(out=dst, in_=rw[0:r, :]), reads=[rw])
                            if gi == 2:
                                if smp:
                                    for h_ in range(4):
                                        P.op(POOL, lambda e, rw=rw, r=r, h_=h_: e.tensor_copy(
                                            out=V1s[0:r, h_, 0:128], in_=rw[0:r, h_ * 128:(h_ + 1) * 128]),
                                            reads=[rw], writes=[V1s_t])
                                else:
                                    ti = (kbase + c0) // 128 + tt
                                    for h_ in range(4):
                                        P.op(POOL, lambda e, rw=rw, ti=ti, h_=h_: e.tensor_copy(
                                            out=V1t[ti][:, ti % 16, h_, 0:128], in_=rw[:, h_ * 128:(h_ + 1) * 128]),
                                            reads=[rw], writes=[V1t[ti]])
                if not FLAGS['sgu']:
                    pass
                elif not smp:
                    for tt in range(4):
                        ps = pf()
                        for g in range(4):
                            P.op(PE, lambda e, ps=ps, g=g, tt=tt: e.matmul(ps[:, g * 128:(g + 1) * 128], lhsT=vn[tt][:, g * 128:(g + 1) * 128],
                                                                           rhs=wmT[:, g, :], start=True, stop=False),
                                 reads=[vn[tt], wmT], writes=[ps])
                            P.op(PE, lambda e, ps=ps, g=g: e.matmul(ps[:, g * 128:(g + 1) * 128], lhsT=onesb[0:1, :],
                                                                    rhs=sgub[0:1, g, :], start=False, stop=True),
                                 reads=[sgub, onesb], writes=[ps])
                        P.op(DVE, lambda e, ps=ps, tt=tt: e.tensor_tensor(
                            out=aT[:, :, tt * 128:(tt + 1) * 128], in0=ps[:].rearrange("p (g t) -> p g t", g=4),
                            in1=uT[:, :, tt * 128:(tt + 1) * 128], op=ALU.mult), reads=[ps, uT], writes=[aT])
                else:
                    ps = pf()
                    for g in range(4):
                        P.op(PE, lambda e, ps=ps, g=g: e.matmul(ps[:, g * 32:(g + 1) * 32], lhsT=vn[0][0:32, g * 128:(g + 1) * 128],
                                                                rhs=wmTs[0:32, g, :], start=True, stop=False),
                             reads=[vn[0], wmTs], writes=[ps])
                        P.op(PE, lambda e, ps=ps, g=g: e.matmul(ps[:, g * 32:(g + 1) * 32], lhsT=onesb[0:1, :],
                                                                rhs=sgubs[0:1, g, :], start=False, stop=True),
                             reads=[sgubs, onesb], writes=[ps])
                    P.op(DVE, lambda e, ps=ps: e.tensor_tensor(
                        out=aT[:, :, 0:32], in0=ps[:, 0:128].rearrange("p (g t) -> p g t", g=4),
                        in1=uT[:, :, 0:32], op=ALU.mult), reads=[ps, uT], writes=[aT])
                if not FLAGS['attn']:
                    pass
                elif not smp:
                    attention_prompt(own, c0, kbase, qT, bT, PT, osb, onb, sm, cnt)
                elif FLAGS['smp_attn']:
                    attention_sample(qT, bT, osb, onb, sm)
                if FLAGS.get('stop') == 'ab' and smp:
                    P.op(DVE, lambda e: e.tensor_copy(out=xT[:, 0:4, 0:32], in_=aT[:, :, 0:32]), reads=[aT], writes=[xT])
                    P.op(DVE, lambda e: e.tensor_copy(out=xT[:, 4:8, 0:32], in_=bT[:, :, 0:32]), reads=[bT], writes=[xT])
                    return
                if FLAGS.get('dbg') == 'bT' and own and c0 == 1536:
                    P.op(DVE, lambda e: e.tensor_copy(out=xT[:, 0:4, 0:512], in_=bT[:]), reads=[bT], writes=[xT])
                    return
                for m in (range(8) if FLAGS['outp'] else []):
                    wt = wfm[cnt["wfm"] % 2]
                    cnt["wfm"] += 1
                    P.dma(POOL, lambda e, wt=wt, m=m: e.dma_start(out=wt[:], in_=w_out_v[:, :, m * 128:(m + 1) * 128]), writes=[wt])
                    ps = pf()
                    for kc in range(8):
                        srcT = aT if kc < 4 else bT
                        P.op(PE, lambda e, ps=ps, wt=wt, kc=kc, srcT=srcT: e.matmul(
                            ps[:, 0:n], lhsT=wt[:, kc, :], rhs=srcT[:, kc % 4, 0:n], start=(kc == 0), stop=(kc == 7)),
                            reads=[wt, srcT], writes=[ps])
                    P.op(DVE, lambda e, ps=ps, m=m: e.tensor_tensor(out=xT[:, m, c0:c0 + n], in0=xT[:, m, c0:c0 + n], in1=ps[:, 0:n],
                                                                   op=ALU.add), reads=[ps, xT], writes=[xT])

            for (c0_, n_) in blocks:
                do_block(c0_, n_)

        acc = [psf[4], psf[5]]

        def attention_prompt(own, c0, kbase, qT, bT, PT, osb, onb, sm, cnt):
            units = []
            for qb in range(2):
                q0 = qb * 256
                tA = (kbase + c0 + q0) // 128
                jlist = list(range(0, tA + 2))
                for h in range(4):
                    for idx, j in enumerate(jlist):
                        units.append((q0, tA, h, j, idx == 0, idx == len(jlist) - 1))

            def front(u):
                q0, tA, h, j, is_first, is_last = u
                ps = pf()
                for half in range(2):
                    P.op(PE, lambda e, ps=ps, half=half, h=h, j=j, q0=q0: e.matmul(
                        ps[:, half * 256:(half + 1) * 256], lhsT=ktile(j, half, h),
                        rhs=qT[half * 64:(half + 1) * 64, h, q0:q0 + 256], start=True, stop=True),
                        reads=[KTt[j], qT], writes=[ps])
                pt = PT[cnt["pt"] % 2]
                cnt["pt"] += 1
                bias = pbias if (own and j < 16) else zbias
                P.op(ACT, lambda e, ps=ps, pt=pt, bias=bias: e.activation(out=pt[:], in_=ps[:], func=AF.Exp,
                                                                          bias=bias[:, 0:1], scale=SCALE),
                     reads=[ps, bias], writes=[pt])
                for qt in range(2):
                    if j == tA + qt:
                        for half in range(2):
                            o_ = half * 256 + qt * 128
                            P.op(POOL, lambda e, pt=pt, o_=o_: e.tensor_tensor(
                                out=pt[:, o_:o_ + 128], in0=pt[:, o_:o_ + 128], in1=trimb[:], op=ALU.mult),
                                reads=[pt, trimb], writes=[pt])
                return pt

            first = {0: True, 1: True}

            def back(u, pt):
                q0, tA, h, j, is_first, is_last = u
                if is_first:
                    first[0] = first[1] = True
                for qt in range(2):
                    tq = tA + qt
                    if j > tq:
                        continue
                    for half in range(2):
                        a = acc[qt]
                        st_flag = first[qt]
                        first[qt] = False
                        P.op(PE, lambda e, a=a, pt=pt, half=half, qt=qt, h=h, j=j, st_flag=st_flag: e.matmul(
                            a[:, half * 129:(half + 1) * 129], lhsT=pt[:, half * 256 + qt * 128:half * 256 + (qt + 1) * 128],
                            rhs=vtile(j, h), start=st_flag, stop=False, skip_group_check=True),
                            reads=[pt, V1t[j]], writes=[a])
                if is_last:
                    for qt in range(2):
                        attn_epilogue(acc[qt], 128, bT, h, q0 + qt * 128, osb, onb, sm)

            nxt = front(units[0])
            for ui, u in enumerate(units):
                cur = nxt
                if ui + 1 < len(units):
                    nxt = front(units[ui + 1])
                back(u, cur)

        def attn_epilogue(a, r, bT, h, col0, osb, onb, sm):
            P.op(DVE, lambda e: e.reciprocal(out=sm[0:r, 0:1], in_=a[0:r, 128:129]), reads=[a], writes=[sm])
            P.op(DVE, lambda e: e.reciprocal(out=sm[0:r, 1:2], in_=a[0:r, 257:258]), reads=[a], writes=[sm])
            P.op(DVE, lambda e: e.tensor_tensor(out=sm[0:r, 1:2], in0=sm[0:r, 1:2], in1=lamneg[0:r, 0:1], op=ALU.mult),
                 reads=[sm, lamneg], writes=[sm])
            P.op(DVE, lambda e: e.tensor_scalar_mul(out=osb[0:r, :], in0=a[0:r, 0:128], scalar1=sm[0:r, 0:1]),
                 reads=[a, sm], writes=[osb])
            P.op(DVE, lambda e: e.scalar_tensor_tensor(out=osb[0:r, :], in0=a[0:r, 129:257], scalar=sm[0:r, 1:2], in1=osb[0:r, :],
                                                       op0=ALU.mult, op1=ALU.add), reads=[a, sm, osb], writes=[osb])
            P.op(ACT, lambda e: e.activation(out=onb[0:r, :], in_=osb[0:r, :], func=AF.Square, accum_out=sm[0:r, 2:3]),
                 reads=[osb], writes=[onb, sm])
            P.op(ACT, lambda e: e.activation(out=sm[0:r, 3:4], in_=sm[0:r, 2:3], func=AF.Sqrt, bias=epsT[0:r, 0:1], scale=1.0 / 128),
                 reads=[sm, epsT], writes=[sm])
            P.op(DVE, lambda e: e.reciprocal(out=sm[0:r, 4:5], in_=sm[0:r, 3:4]), reads=[sm], writes=[sm])
            P.op(DVE, lambda e: e.scalar_tensor_tensor(out=onb[0:r, :], in0=osb[0:r, :], scalar=sm[0:r, 4:5], in1=sublnb[0:r, :],
                                                       op0=ALU.mult, op1=ALU.mult), reads=[osb, sm, sublnb], writes=[onb])
            pb_ = pbk()
            P.op(PE, lambda e: e.transpose(out=pb_[:, 0:r], in_=onb[0:r, :], identity=identb[0:r, 0:r]),
                 reads=[onb, identb], writes=[pb_])
            P.op(ACT, lambda e: e.copy(out=bT[:, h, col0:col0 + r], in_=pb_[:, 0:r]), reads=[pb_], writes=[bT])

        KTs = RC.alloc("KTs", [4, 32], BF16)
        KTs_t = KTs
        V1s = RC.alloc("V1s", [4, 130], BF16)
        V1s_t = V1s
        P.op(POOL, lambda e: e.memset(V1s[:, :, 128:129], 1.0), writes=[V1s])

        def attention_sample(qT, bT, osb, onb, sm):
            P.mark('sample attn')
            P.barrier_all()
            R2.reset()
            pidx_i = R2.alloc("pidx_i", [4, NPAGE], I32)
            pidx_f = R2.alloc("pidx_f", [4, NPAGE], F32)
            iot = R2.alloc("iot", [NPAGE], F32)
            pidx = R2.alloc("pidx", [4, NPAGE], I32)
            NKB, NVB = 8, 16
            kpg = [R2.alloc(f"kpg{i}", [512], F32) for i in range(NKB)]
            vpg = [R2.alloc(f"vpg{i}", [512], F32) for i in range(NVB)]
            kpb2 = [R2.alloc(f"kpb{i}", [512], BF16) for i in range(2)]
            kTp2 = [R2.alloc(f"kTp{i}", [4, 128], BF16) for i in range(2)]
            PTn = R2.alloc("PTn", [64], BF16)
            vpb = [R2.alloc(f"vpb{i}", [4, 130], BF16) for i in range(2)]
            PTs = [R2.alloc(f"PTs{i}", [512], BF16) for i in range(2)]
            qbd = R2.alloc("qbd", [4, 4, 16], BF16)
            smask = R2.alloc("smask", [4, 8], BF16)
            est = R2.alloc("est", [258], F32)
            mk = R2.alloc("mk", [96], F32)
            mk2 = R2.alloc("mk2", [96], F32)
            for vb_ in vpb:
                P.op(POOL, lambda e, vb_=vb_: e.memset(vb_[:, :, 128:129], 1.0), writes=[vb_])
            P.dma(SP, lambda e: e.dma_start(out=pidx_i[:], in_=ptab.rearrange("(o s) n -> o s n", o=1).partition_broadcast(128)
                                            .rearrange("p o s n -> p (o s) n")), writes=[pidx_i])
            P.op(POOL, lambda e: e.iota(iot[:], pattern=[[0, NPAGE]], base=0, channel_multiplier=1, allow_small_or_imprecise_dtypes=True),
                 writes=[iot])
            P.op(DVE, lambda e: e.tensor_copy(out=pidx_f[:], in_=pidx_i[:]), reads=[pidx_i], writes=[pidx_f])
            for s_ in range(4):
                P.op(DVE, lambda e, s_=s_: e.scalar_tensor_tensor(out=pidx_f[:, s_, :], in0=pidx_f[:, s_, :], scalar=128.0, in1=iot[:],
                                                                  op0=ALU.mult, op1=ALU.add), reads=[pidx_f, iot], writes=[pidx_f])
            P.op(DVE, lambda e: e.tensor_copy(out=pidx[:], in_=pidx_f[:]), reads=[pidx_f], writes=[pidx])
            P.op(POOL, lambda e: e.iota(mk[0:32, 0:32], pattern=[[8, 4], [1, 8]], base=0, channel_multiplier=-1,
                                        allow_small_or_imprecise_dtypes=True), writes=[mk])
            P.op(POOL, lambda e: e.iota(mk[0:32, 32:64], pattern=[[8, 4], [0, 8]], base=7, channel_multiplier=-1,
                                        allow_small_or_imprecise_dtypes=True), writes=[mk])
            P.op(POOL, lambda e: e.iota(mk[0:32, 64:96], pattern=[[-8, 4], [0, 8]], base=0, channel_multiplier=1,
                                        allow_small_or_imprecise_dtypes=True), writes=[mk])
            P.op(DVE, lambda e: e.tensor_single_scalar(out=mk2[0:32, :], in_=mk[0:32, :], scalar=0.0, op=ALU.is_ge),
                 reads=[mk], writes=[mk2])
            P.op(DVE, lambda e: e.tensor_tensor(out=mk2[0:32, 0:32], in0=mk2[0:32, 0:32], in1=mk2[0:32, 32:64], op=ALU.mult),
                 reads=[mk2], writes=[mk2])
            P.op(DVE, lambda e: e.tensor_tensor(out=smask[0:32].rearrange("p s q -> p (s q)"), in0=mk2[0:32, 0:32], in1=mk2[0:32, 64:96],
                                                op=ALU.mult), reads=[mk2], writes=[smask])
            P.op(POOL, lambda e: e.memset(qbd[:], 0.0), writes=[qbd])
            for h in range(4):
                P.op(DVE, lambda e, h=h: e.tensor_copy(out=qbd[0:64, h, :, 0:8], in_=qT[0:64, h, 0:32].rearrange("p (s q) -> p s q", s=4)),
                     reads=[qT], writes=[qbd])
                P.op(DVE, lambda e, h=h: e.tensor_copy(out=qbd[64:128, h, :, 8:16], in_=qT[64:128, h, 0:32].rearrange("p (s q) -> p s q", s=4)),
                     reads=[qT], writes=[qbd])
            a, a2, a3 = acc[0], acc[1], psf[3]
            rot["n"] = 3

            def region(h, half):
                if h < 3:
                    return (a if half == 0 else a2), (a if half == 0 else a2)[0:8, h * 129:(h + 1) * 129]
                return a3, a3[0:8, half * 129:(half + 1) * 129]

            steps = [(s_, grp) for s_ in range(4) for grp in range(NPAGE // 8)]

            def gathers(si, which):
                s_, grp = steps[si]
                for jj in range(8):
                    j = grp * 8 + jj
                    pi = si * 8 + jj
                    kp, vp = kpg[pi % NKB], vpg[pi % NVB]
                    if which == "k":
                        P.dma(POOL, lambda e, kp=kp, s_=s_, j=j: e.indirect_dma_start(
                            out=kp[:], out_offset=None, in_=cache_k,
                            in_offset=bass.IndirectOffsetOnAxis(ap=pidx[:, s_, j:j + 1], axis=0)), reads=[pidx], writes=[kp])
                    else:
                        P.dma(POOL, lambda e, vp=vp, s_=s_, j=j: e.indirect_dma_start(
                            out=vp[:], out_offset=None, in_=cache_v,
                            in_offset=bass.IndirectOffsetOnAxis(ap=pidx[:, s_, j:j + 1], axis=0)), reads=[pidx], writes=[vp])

            def front(si):
                s_, grp = steps[si]
                sc = pf()
                for jj in range(8):
                    pi = si * 8 + jj
                    kp = kpg[pi % NKB]
                    kpb, kTp = kpb2[pi % 2], kTp2[pi % 2]
                    P.op(ACT, lambda e, kp=kp, kpb=kpb: e.copy(out=kpb[:], in_=kp[:]), reads=[kp], writes=[kpb])
                    pb_ = pbk()
                    for h in range(4):
                        P.op(PE, lambda e, pb_=pb_, h=h, kpb=kpb: e.transpose(out=pb_[:, h * 128:(h + 1) * 128], in_=kpb[:, h * 128:(h + 1) * 128],
                                                                              identity=identb[:]), reads=[kpb, identb], writes=[pb_])
                    P.op(DVE, lambda e, pb_=pb_, kTp=kTp: e.tensor_copy(out=kTp[:], in_=pb_[:, 0:512].rearrange("p (h t) -> p h t", h=4)),
                         reads=[pb_], writes=[kTp])
                    for h in range(4):
                        P.op(PE, lambda e, sc=sc, h=h, jj=jj, s_=s_, kTp=kTp: e.matmul(
                            sc[:, jj * 64 + h * 16:jj * 64 + (h + 1) * 16], lhsT=kTp[:, h, :], rhs=qbd[:, h, s_, :], start=True, stop=True),
                            reads=[kTp, qbd], writes=[sc])
                if si + 1 < len(steps):
                    gathers(si + 1, "k")
                pt = PTs[si % 2]
                P.op(ACT, lambda e, sc=sc, pt=pt: e.activation(out=pt[:], in_=sc[:], func=AF.Exp, scale=SCALE, bias=zbias[:, 0:1]),
                     reads=[sc, zbias], writes=[pt])
                return pt

            def back(si, pt):
                s_, grp = steps[si]
                for jj in range(8):
                    pi = si * 8 + jj
                    vp = vpg[pi % NVB]
                    vb_ = vpb[pi % 2]
                    P.op(POOL, lambda e, vp=vp, vb_=vb_: e.tensor_copy(out=vb_[:, :, 0:128], in_=vp[:].rearrange("p (h d) -> p h d", h=4)),
                         reads=[vp], writes=[vb_])
                    for h in range(4):
                        for half in range(2):
                            tT, tg = region(h, half)
                            st_flag = (grp == 0 and jj == 0) and ((h == 0) or (h == 3 and half == 0))
                            P.op(PE, lambda e, tg=tg, pt=pt, jj=jj, h=h, half=half, st_flag=st_flag, vb_=vb_: e.matmul(
                                tg, lhsT=pt[:, jj * 64 + h * 16 + half * 8:jj * 64 + h * 16 + half * 8 + 8],
                                rhs=vb_[:, h, 0:129], start=st_flag, stop=False, skip_group_check=True),
                                reads=[pt, vb_], writes=[tT])
                if grp < NPAGE // 8 - 1:
                    return
                sc = pf()
                for h in range(4):
                    P.op(PE, lambda e, sc=sc, h=h, s_=s_: e.matmul(sc[0:32, h * 16:(h + 1) * 16], lhsT=KTs[:, h, :], rhs=qbd[:, h, s_, :],
                                                                   start=True, stop=True), reads=[KTs_t, qbd], writes=[sc])
                P.op(ACT, lambda e, sc=sc: e.activation(out=PTn[0:32, 0:64], in_=sc[0:32, 0:64], func=AF.Exp, scale=SCALE,
                                                        bias=zbias[0:32, 0:1]), reads=[sc, zbias], writes=[PTn])
                P.op(DVE, lambda e, s_=s_: e.tensor_tensor(
                    out=PTn[0:32, 0:64].rearrange("p (a q) -> p a q", q=8), in0=PTn[0:32, 0:64].rearrange("p (a q) -> p a q", q=8),
                    in1=smask[0:32, s_, :].unsqueeze(1).to_broadcast([32, 8, 8]), op=ALU.mult), reads=[PTn, smask], writes=[PTn])
                for h in range(4):
                    for half in range(2):
                        tT, tg = region(h, half)
                        P.op(PE, lambda e, tg=tg, h=h, half=half: e.matmul(
                            tg, lhsT=PTn[0:32, h * 16 + half * 8:h * 16 + half * 8 + 8], rhs=V1s[0:32, h, 0:129], start=False, stop=True,
                            skip_group_check=True), reads=[PTn, V1s_t], writes=[tT])
                for h in range(3):
                    cs = h * 129
                    P.op(DVE, lambda e, cs=cs: e.tensor_copy(out=est[0:8, 0:129], in_=a[0:8, cs:cs + 129]), reads=[a], writes=[est])
                    P.op(DVE, lambda e, cs=cs: e.tensor_copy(out=est[0:8, 129:258], in_=a2[0:8, cs:cs + 129]), reads=[a2], writes=[est])
                    attn_epilogue(est, 8, bT, h, s_ * 8, osb, onb, sm)
                attn_epilogue(a3, 8, bT, 3, s_ * 8, osb, onb, sm)

            gathers(0, "k")
            gathers(0, "v")
            nxt = front(0)
            for si in range(len(steps)):
                cur = nxt
                if si + 1 < len(steps):
                    gathers(si + 1, "v")
                    nxt = front(si + 1)
                back(si, cur)
            rot["n"] = 4
            P.mark('sample outproj')

        def ffn(layer, ncols, actT, G, stage_base):
            P.barrier_all()
            R3.reset()
            RW.reset()
            hT = R3.alloc("hTall", [8, NTOT], BF16)
            sq = RW.alloc("sq", [512], BF16)
            rstd = RW.alloc("rstd", [512], F32)
            sg = [RW.alloc(f"sg{i}", [512], BF16) for i in range(2)]
            nr2 = min(4, (R2.cap - stage_base) // 8192)
            w1t = [T(R2.base[:, (stage_base + i * 8192) // 4:(stage_base + (i + 1) * 8192) // 4].bitcast(BF16)
                     .rearrange("p (k c) -> p k c", k=8), f"w1t{i}") for i in range(nr2)]
            w1t += [RW.alloc(f"w1t{i}", [8, 512], BF16) for i in range(nr2, 4)]
            w2 = [RW.alloc(f"w2{i}", [G, 512], BF16) for i in range(2)]
            blocks = [(c, min(512, ncols - c)) for c in range(0, ncols, 512)]
            for (c0, n) in blocks:
                rmsnorm_cols(xT, c0, n, gffn[:, layer, :], hT, c0, sq, rstd)
            w1v = w_f1[layer].rearrange("(k p) c -> p k c", p=128)
            w2v = w_f2[layer].rearrange("(j p) c -> p j c", p=128)
            groups = [(j0, min(G, 22 - j0)) for j0 in range(0, 22, G)]
            chunks = []
            for gi, (j0, gsz) in enumerate(groups):
                qs = list(range(0, gsz, 4))
                for q0 in qs:
                    chunks.append((gi, j0, gsz, q0, min(4, gsz - q0), q0 == qs[0], q0 == qs[-1]))

            def load_chunk(idx):
                if FLAGS.get('noffndma'):
                    return
                gi, j0, gsz, q0, nq, _, _ = chunks[idx]
                wgt, wut = w1t[2 * (idx % 2)], w1t[2 * (idx % 2) + 1]
                col, w = (j0 + q0) * 128, nq * 128
                for hh in range(2):
                    P.dma(POOL, lambda e, wgt=wgt, hh=hh, col=col, w=w: e.dma_start(
                        out=wgt[:, hh * 4:hh * 4 + 4, 0:w], in_=w1v[:, hh * 4:hh * 4 + 4, col:col + w]), writes=[wgt])
                    P.dma(POOL, lambda e, wut=wut, hh=hh, col=col, w=w: e.dma_start(
                        out=wut[:, hh * 4:hh * 4 + 4, 0:w], in_=w1v[:, hh * 4:hh * 4 + 4, DFF + col:DFF + col + w]), writes=[wut])

            def load_w2(j0, gsz):
                if FLAGS.get('noffndma'):
                    return
                for m4 in range(2):
                    w2t = w2[m4]
                    for a0 in range(0, gsz, 4):
                        a1 = min(gsz, a0 + 4)
                        P.dma(POOL, lambda e, w2t=w2t, m4=m4, a0=a0, a1=a1, j0=j0: e.dma_start(
                            out=w2t[:, a0:a1, :], in_=w2v[:, j0 + a0:j0 + a1, m4 * 512:(m4 + 1) * 512]), writes=[w2t])

            P.mark(f'ffn{layer}_{ncols}_w1')
            P.pe_free = FLAGS.get('pe_free', True)
            load_chunk(0)
            for idx, (gi, j0, gsz, q0, nq, g_first, g_last) in enumerate(chunks):
                if g_first:
                    load_w2(j0, gsz)
                if idx + 1 < len(chunks):
                    load_chunk(idx + 1)
                wgt, wut = w1t[2 * (idx % 2)], w1t[2 * (idx % 2) + 1]
                for jq in range(nq):
                    jj = q0 + jq
                    for bi, (c0, n) in enumerate(blocks):
                        pg_ = pf()
                        pu_ = pf()
                        for k in range(8):
                            P.op(PE, lambda e, pg_=pg_, wgt=wgt, k=k, c0=c0, n=n, jq=jq: e.matmul(
                                pg_[:, 0:n], lhsT=wgt[:, k, jq * 128:(jq + 1) * 128], rhs=hT[:, k, c0:c0 + n],
                                start=(k == 0), stop=(k == 7)), reads=[wgt, hT], writes=[pg_])
                        for k in range(8):
                            P.op(PE, lambda e, pu_=pu_, wut=wut, k=k, c0=c0, n=n, jq=jq: e.matmul(
                                pu_[:, 0:n], lhsT=wut[:, k, jq * 128:(jq + 1) * 128], rhs=hT[:, k, c0:c0 + n],
                                start=(k == 0), stop=(k == 7)), reads=[wut, hT], writes=[pu_])
                        sgt = sg[bi % 2]
                        P.op(ACT, lambda e, pg_=pg_, sgt=sgt, n=n: e.activation(out=sgt[:, 0:n], in_=pg_[:, 0:n], func=AF.Silu),
                             reads=[pg_], writes=[sgt])
                        P.op(DVE, lambda e, pu_=pu_, sgt=sgt, jj=jj, c0=c0, n=n: e.tensor_tensor(
                            out=actT[:, jj, c0:c0 + n], in0=sgt[:, 0:n], in1=pu_[:, 0:n], op=ALU.mult), reads=[pu_, sgt], writes=[actT])
                if not g_last:
                    continue
                P.mark(f'ffn{layer}_{ncols}_w2_{gi}')
                for m in range(8):
                    w2t = w2[m // 4]
                    mm = m % 4
                    for (c0, n) in blocks:
                        ps = pf()
                        for jj in range(gsz):
                            P.op(PE, lambda e, ps=ps, w2t=w2t, jj=jj, c0=c0, n=n, gsz=gsz, mm=mm: e.matmul(
                                ps[:, 0:n], lhsT=w2t[:, jj, mm * 128:(mm + 1) * 128], rhs=actT[:, jj, c0:c0 + n],
                                start=(jj == 0), stop=(jj == gsz - 1)), reads=[w2t, actT], writes=[ps])
                        P.op(DVE, lambda e, ps=ps, m=m, c0=c0, n=n: e.tensor_tensor(out=xT[:, m, c0:c0 + n], in0=xT[:, m, c0:c0 + n], in1=ps[:, 0:n],
                                                                                 op=ALU.add), reads=[ps, xT], writes=[xT])
                P.mark(f'ffn{layer}_{ncols}_w1b_{gi}')
            P.pe_free = False


        Sst = RC.alloc("Sst", [64], F32)
        Ssw = RC.alloc("Ssw", [64], F32)
        A1 = RC.alloc("A1", [64], F32)
        A2 = RC.alloc("A2", [64], F32)
        sgn = RC.alloc("sgn", [1], F32)
        mtop = RC.alloc("mtop", [1], F32)
        nmbot = RC.alloc("nmbot", [1], F32)
        negpi = RC.alloc("negpi", [1], F32)
        P.op(POOL, lambda e: e.iota(tmpc[:, 0:1], pattern=[[0, 1]], base=-64, channel_multiplier=1,
                                    allow_small_or_imprecise_dtypes=True), writes=[tmpc])
        P.op(DVE, lambda e: e.tensor_single_scalar(out=nmbot[:], in_=tmpc[:, 0:1], scalar=0.0, op=ALU.is_ge), reads=[tmpc], writes=[nmbot])
        P.op(DVE, lambda e: e.tensor_scalar(out=sgn[:], in0=nmbot[:], scalar1=2.0, scalar2=-1.0, op0=ALU.mult, op1=ALU.add),
             reads=[nmbot], writes=[sgn])
        P.op(DVE, lambda e: e.tensor_scalar(out=mtop[:], in0=nmbot[:], scalar1=-1.0, scalar2=1.0, op0=ALU.mult, op1=ALU.add),
             reads=[nmbot], writes=[mtop])
        P.op(DVE, lambda e: e.tensor_scalar_mul(out=nmbot[:], in0=nmbot[:], scalar1=-1.0), reads=[nmbot], writes=[nmbot])
        P.op(DVE, lambda e: e.memset(negpi[:], -math.pi), writes=[negpi])

        TWO_PI = 2.0 * math.pi

        def ssm_tables(full):
            R2.reset()
            RW.reset()
            if not full:
                R2.off = 33024
            XB = R2.alloc("XB", [64, 128], BF16)
            Wz = R2.alloc("Wz", [64, 128], BF16)
            if full:
                Tt = R2.alloc("Tt", [64, 128], BF16)
                YC = R2.alloc("YC", [64, 128], BF16)
                tflat = R2.base[:, 32768 // 4:(32768 + 16384) // 4]
                inA = T(tflat[:, 0:1024].rearrange("p (g c) -> p g c", g=64), "inA")
                inB = T(tflat[:, 1024:2048].rearrange("p (g c) -> p g c", g=64), "inB")
                parts = [(0, 64)]
            else:
                Tt = YC = None
                inA = RW.alloc("inA", [32, 16], F32)
                inB = RW.alloc("inB", [32, 16], F32)
                parts = [(0, 32), (32, 64)]
            tb = [RW.alloc(f"tb{i}", [64, 8], F32) for i in range(10)]
            sm_ = [RW.alloc(f"ts{i}", [64], F32) for i in range(8)]
            kidx = RW.alloc("kidx", [8], F32)
            if full:
                tmask4 = RW.alloc("tmask4", [4, 128], F32)
                dsk = RW.alloc("dsk", [64], F32)
                tmpf = RW.alloc("tmpf", [512], F32)
            aR, aI = sm_[6], sm_[7]
            P.dma(SP, lambda e: e.dma_start(out=aR[:], in_=a_re), writes=[aR])
            P.dma(SP, lambda e: e.dma_start(out=aI[:], in_=a_im), writes=[aI])
            P.dma(SP, lambda e: e.dma_start(out=sm_[0][:], in_=logdt), writes=[sm_[0]])
            if full:
                P.dma(SP, lambda e: e.dma_start(out=dsk[:], in_=d_skip), writes=[dsk])
            P.op(ACT, lambda e: e.activation(out=sm_[0][:], in_=sm_[0][:], func=AF.Exp), reads=[sm_[0]], writes=[sm_[0]])
            P.op(DVE, lambda e: e.tensor_tensor(out=sm_[1][:], in0=aR[:], in1=sm_[0][:], op=ALU.mult), reads=[aR, sm_[0]], writes=[sm_[1]])
            P.op(DVE, lambda e: e.tensor_tensor(out=sm_[2][:], in0=aI[:], in1=sm_[0][:], op=ALU.mult), reads=[aI, sm_[0]], writes=[sm_[2]])
            P.op(POOL, lambda e: e.iota(kidx[:], pattern=[[1, 8]], base=1, channel_multiplier=0, allow_small_or_imprecise_dtypes=True), writes=[kidx])
            kb = lambda: kidx[:].unsqueeze(1).broadcast_to([128, 64, 8])
            gb = lambda t: t[:].unsqueeze(2).broadcast_to([128, 64, 8])
            P.op(DVE, lambda e: e.tensor_tensor(out=tb[0][:], in0=gb(sm_[1]), in1=kb(), op=ALU.mult), reads=[sm_[1], kidx], writes=[tb[0]])
            P.op(ACT, lambda e: e.activation(out=tb[1][:], in_=tb[0][:], func=AF.Exp), reads=[tb[0]], writes=[tb[1]])
            P.op(ACT, lambda e: e.activation(out=tb[2][:], in_=tb[0][:], func=AF.Exp, scale=-1.0), reads=[tb[0]], writes=[tb[2]])
            P.op(DVE, lambda e: e.tensor_tensor(out=tb[3][:], in0=gb(sm_[2]), in1=kb(), op=ALU.mult), reads=[sm_[2], kidx], writes=[tb[3]])
            def sincos(dst, off):
                yi = T(tb[9].ap.bitcast(I32), "yi")
                P.op(DVE, lambda e: e.tensor_scalar(out=tb[0][:], in0=tb[3][:], scalar1=1.0 / TWO_PI, scalar2=off, op0=ALU.mult, op1=ALU.add),
                     reads=[tb[3]], writes=[tb[0]])
                P.op(DVE, lambda e: e.tensor_copy(out=yi[:], in_=tb[0][:]), reads=[tb[0]], writes=[yi, tb[9]])
                P.op(DVE, lambda e: e.tensor_copy(out=tb[8][:], in_=yi[:]), reads=[yi, tb[9]], writes=[tb[8]])
                P.op(DVE, lambda e: e.tensor_tensor(out=tb[0][:], in0=tb[0][:], in1=tb[8][:], op=ALU.subtract), reads=[tb[0], tb[8]], writes=[tb[0]])
                P.op(DVE, lambda e: e.tensor_single_scalar(out=tb[8][:], in_=tb[0][:], scalar=0.5, op=ALU.is_gt), reads=[tb[0]], writes=[tb[8]])
                P.op(DVE, lambda e: e.tensor_tensor(out=tb[0][:], in0=tb[0][:], in1=tb[8][:], op=ALU.subtract), reads=[tb[0], tb[8]], writes=[tb[0]])
                P.op(DVE, lambda e: e.tensor_single_scalar(out=tb[8][:], in_=tb[0][:], scalar=-0.5, op=ALU.is_lt), reads=[tb[0]], writes=[tb[8]])
                P.op(DVE, lambda e: e.tensor_tensor(out=tb[0][:], in0=tb[0][:], in1=tb[8][:], op=ALU.add), reads=[tb[0], tb[8]], writes=[tb[0]])
                P.op(ACT, lambda e: e.activation(out=dst[:], in_=tb[0][:], func=AF.Sin, scale=TWO_PI), reads=[tb[0]], writes=[dst])

            sincos(tb[4], 0.0)
            sincos(tb[5], 0.25)
            PWre, PWim, MWre, MWim = tb[6], tb[7], tb[8], tb[9]
            P.op(DVE, lambda e: e.tensor_tensor(out=PWre[:], in0=tb[1][:], in1=tb[5][:], op=ALU.mult), reads=[tb[1], tb[5]], writes=[PWre])
            P.op(DVE, lambda e: e.tensor_tensor(out=PWim[:], in0=tb[1][:], in1=tb[4][:], op=ALU.mult), reads=[tb[1], tb[4]], writes=[PWim])
            P.op(DVE, lambda e: e.tensor_tensor(out=MWre[:], in0=tb[2][:], in1=tb[5][:], op=ALU.mult), reads=[tb[2], tb[5]], writes=[MWre])
            P.op(DVE, lambda e: e.scalar_tensor_tensor(out=MWim[:], in0=tb[2][:], scalar=-1.0, in1=tb[4][:], op0=ALU.mult, op1=ALU.mult), reads=[tb[2], tb[4]], writes=[MWim])
            P.op(DVE, lambda e: e.tensor_copy(out=A1[:], in_=PWre[:, :, 7]), reads=[PWre], writes=[A1])
            P.op(DVE, lambda e: e.tensor_scalar_mul(out=A2[:], in0=PWim[:, :, 7], scalar1=sgn[:, 0:1]), reads=[PWim, sgn], writes=[A2])
            nr, den, qre, qim, t_ = sm_[3], sm_[4], sm_[5], sm_[0], sm_[1]
            P.op(DVE, lambda e: e.tensor_scalar_add(out=nr[:], in0=PWre[:, :, 0], scalar1=-1.0), reads=[PWre], writes=[nr])
            P.op(DVE, lambda e: e.tensor_tensor(out=den[:], in0=aR[:], in1=aR[:], op=ALU.mult), reads=[aR], writes=[den])
            P.op(DVE, lambda e: e.tensor_tensor(out=t_[:], in0=aI[:], in1=aI[:], op=ALU.mult), reads=[aI], writes=[t_])
            P.op(DVE, lambda e: e.tensor_tensor(out=den[:], in0=den[:], in1=t_[:], op=ALU.add), reads=[den, t_], writes=[den])
            P.op(DVE, lambda e: e.reciprocal(out=den[:], in_=den[:]), reads=[den], writes=[den])
            P.op(DVE, lambda e: e.tensor_tensor(out=qre[:], in0=nr[:], in1=aR[:], op=ALU.mult), reads=[nr, aR], writes=[qre])
            P.op(DVE, lambda e: e.tensor_tensor(out=t_[:], in0=PWim[:, :, 0], in1=aI[:], op=ALU.mult), reads=[PWim, aI], writes=[t_])
            P.op(DVE, lambda e: e.tensor_tensor(out=qre[:], in0=qre[:], in1=t_[:], op=ALU.add), reads=[qre, t_], writes=[qre])
            P.op(DVE, lambda e: e.tensor_tensor(out=qre[:], in0=qre[:], in1=den[:], op=ALU.mult), reads=[qre, den], writes=[qre])
            P.op(DVE, lambda e: e.tensor_tensor(out=qim[:], in0=PWim[:, :, 0], in1=aR[:], op=ALU.mult), reads=[PWim, aR], writes=[qim])
            P.op(DVE, lambda e: e.tensor_tensor(out=t_[:], in0=nr[:], in1=aI[:], op=ALU.mult), reads=[nr, aI], writes=[t_])
            P.op(DVE, lambda e: e.tensor_tensor(out=qim[:], in0=qim[:], in1=t_[:], op=ALU.subtract), reads=[qim, t_], writes=[qim])
            P.op(DVE, lambda e: e.tensor_tensor(out=qim[:], in0=qim[:], in1=den[:], op=ALU.mult), reads=[qim, den], writes=[qim])
            W1re, W1im = tb[0], tb[1]
            P.op(DVE, lambda e: e.tensor_tensor(out=W1re[:], in0=MWre[:], in1=gb(qre), op=ALU.mult), reads=[MWre, qre], writes=[W1re])
            P.op(DVE, lambda e: e.tensor_tensor(out=tb[2][:], in0=MWim[:], in1=gb(qim), op=ALU.mult), reads=[MWim, qim], writes=[tb[2]])
            P.op(DVE, lambda e: e.tensor_tensor(out=W1re[:], in0=W1re[:], in1=tb[2][:], op=ALU.subtract), reads=[W1re, tb[2]], writes=[W1re])
            P.op(DVE, lambda e: e.tensor_tensor(out=W1im[:], in0=MWim[:], in1=gb(qre), op=ALU.mult), reads=[MWim, qre], writes=[W1im])
            P.op(DVE, lambda e: e.tensor_tensor(out=tb[2][:], in0=MWre[:], in1=gb(qim), op=ALU.mult), reads=[MWre, qim], writes=[tb[2]])
            P.op(DVE, lambda e: e.tensor_tensor(out=W1im[:], in0=W1im[:], in1=tb[2][:], op=ALU.add), reads=[W1im, tb[2]], writes=[W1im])
            P.op(DVE, lambda e: e.tensor_scalar_mul(out=W1im[:], in0=W1im[:], scalar1=sgn[:, 0:1]), reads=[W1im, sgn], writes=[W1im])
            for (g0, g1) in parts:
                ng = g1 - g0
                P.dma(SP, lambda e, g0=g0, g1=g1: e.dma_start(out=inA[:], in_=b_A[:, g0:g1, :]), writes=[inA])
                P.dma(SP, lambda e, g0=g0, g1=g1: e.dma_start(out=inB[:], in_=b_B[:, g0:g1, :]), writes=[inB])
                xb4 = lambda t, g0=g0, g1=g1: t[:, g0:g1, :].rearrange("p g (k c) -> p g k c", k=8)
                wb4 = lambda t, g0=g0, g1=g1, ng=ng: t[:, g0:g1, :].unsqueeze(3).broadcast_to([128, ng, 8, 16])
                ib4 = lambda t, ng=ng: t[:].unsqueeze(2).broadcast_to([128, ng, 8, 16])
                P.op(DVE, lambda e, xb4=xb4, wb4=wb4, ib4=ib4: e.tensor_tensor(out=xb4(XB), in0=wb4(W1re), in1=ib4(inA), op=ALU.mult), reads=[W1re, inA], writes=[XB])
                P.op(POOL, lambda e, xb4=xb4, wb4=wb4, ib4=ib4: e.tensor_tensor(out=xb4(Wz), in0=wb4(W1im), in1=ib4(inB), op=ALU.mult), reads=[W1im, inB], writes=[Wz])
            P.op(DVE, lambda e: e.tensor_tensor(out=XB[:], in0=XB[:], in1=Wz[:], op=ALU.add), reads=[XB, Wz], writes=[XB])
            xb4 = lambda t: t[:].rearrange("p g (k c) -> p g k c", k=8)
            wb4 = lambda t: t[:].unsqueeze(3).broadcast_to([128, 64, 8, 16])
            ib4 = lambda t: t[:].unsqueeze(2).broadcast_to([128, 64, 8, 16])
            if full:
                PA, PB = tb[2], tb[3]
                P.op(DVE, lambda e: e.tensor_scalar_mul(out=PA[:], in0=PWre[:], scalar1=mtop[:, 0:1]), reads=[PWre, mtop], writes=[PA])
                P.op(DVE, lambda e: e.scalar_tensor_tensor(out=PA[:], in0=PWim[:], scalar=nmbot[:, 0:1], in1=PA[:], op0=ALU.mult, op1=ALU.add),
                     reads=[PWim, nmbot, PA], writes=[PA])
                P.op(DVE, lambda e: e.tensor_scalar_mul(out=PB[:], in0=PWre[:], scalar1=nmbot[:, 0:1]), reads=[PWre, nmbot], writes=[PB])
                P.op(DVE, lambda e: e.tensor_scalar(out=tb[4][:], in0=PWim[:], scalar1=mtop[:, 0:1], scalar2=-1.0, op0=ALU.mult, op1=ALU.mult),
                     reads=[PWim, mtop], writes=[tb[4]])
                P.op(DVE, lambda e: e.tensor_tensor(out=PB[:], in0=PB[:], in1=tb[4][:], op=ALU.add), reads=[PB, tb[4]], writes=[PB])
                P.dma(SP, lambda e: e.dma_start(out=inA[:], in_=c_A), writes=[inA])
                P.dma(SP, lambda e: e.dma_start(out=inB[:], in_=c_B), writes=[inB])
                P.op(DVE, lambda e: e.tensor_tensor(out=xb4(YC), in0=wb4(PA), in1=ib4(inA), op=ALU.mult), reads=[PA, inA], writes=[YC])
                P.op(POOL, lambda e: e.tensor_tensor(out=xb4(Wz), in0=wb4(PB), in1=ib4(inB), op=ALU.mult), reads=[PB, inB], writes=[Wz])
                P.op(DVE, lambda e: e.tensor_tensor(out=YC[:], in0=YC[:], in1=Wz[:], op=ALU.add), reads=[YC, Wz], writes=[YC])
                for q4 in range(4):
                    P.op(POOL, lambda e, q4=q4: e.iota(tmask4[:, q4, :], pattern=[[16, 8], [0, 16]], base=15, channel_multiplier=-1,
                                                      allow_small_or_imprecise_dtypes=True), writes=[tmask4])
                P.op(DVE, lambda e: e.tensor_single_scalar(out=tmask4[:], in_=tmask4[:], scalar=0.0, op=ALU.is_ge), reads=[tmask4], writes=[tmask4])
                for g4 in range(16):
                    ps = pf()
                    for gi in range(4):
                        g = g4 * 4 + gi
                        P.op(PE, lambda e, ps=ps, g=g, gi=gi: e.matmul(ps[:, gi * 128:(gi + 1) * 128], lhsT=XB[:, g, :], rhs=YC[:, g, :], start=True, stop=True),
                             reads=[XB, YC], writes=[ps])
                    P.op(DVE, lambda e, ps=ps: e.tensor_tensor(out=tmpf[:], in0=ps[:], in1=tmask4[:].rearrange("p a b -> p (a b)"), op=ALU.mult),
                         reads=[ps, tmask4], writes=[tmpf])
                    for gi in range(4):
                        g = g4 * 4 + gi
                        P.op(DVE, lambda e, g=g, gi=gi: e.scalar_tensor_tensor(out=Tt[:, g, :], in0=identf[:], scalar=dsk[:, g:g + 1],
                                                                              in1=tmpf[:, gi * 128:(gi + 1) * 128], op0=ALU.mult, op1=ALU.add),
                             reads=[identf, dsk, tmpf], writes=[Tt])
            for g8 in range(8):
                pb_ = pbk()
                for gi in range(8):
                    g = g8 * 8 + gi
                    P.op(PE, lambda e, pb_=pb_, g=g, gi=gi: e.transpose(out=pb_[:, gi * 128:(gi + 1) * 128], in_=XB[:, g, :], identity=identb[:]),
                         reads=[XB, identb], writes=[pb_])
                pb2 = pbk()
                for gi in range(8):
                    g = g8 * 8 + gi
                    P.op(PE, lambda e, pb2=pb2, g=g, gi=gi: e.transpose(out=pb2[:, gi * 128:(gi + 1) * 128], in_=XB[:, g, :], identity=identsw[:]),
                         reads=[XB, identsw], writes=[pb2])
                P.op(ACT, lambda e, pb_=pb_, g8=g8: e.copy(out=Wz[:, g8 * 8:(g8 + 1) * 8, :], in_=pb_[:].rearrange("p (g s) -> p g s", g=8)),
                     reads=[pb_], writes=[Wz])
                P.op(ACT, lambda e, pb2=pb2, g8=g8: e.copy(out=XB[:, g8 * 8:(g8 + 1) * 8, :], in_=pb2[:].rearrange("p (g s) -> p g s", g=8)),
                     reads=[pb2], writes=[XB])
            Wz.sw = XB
            return Wz, Tt, YC

        def ssm_blocks(h1T, ncols_prompt, Wz, Tt, YC, full, sample):
            RW.reset()
            NJ = 32
            Zt = RW.alloc("Zt", [64, NJ], BF16)
            Zs = RW.alloc("Zs", [64, NJ], BF16)
            Sall = RW.alloc("Sall", [64, NJ], BF16)
            UJ2 = [RW.alloc(f"UJ{i}", [8, 128], BF16) for i in range(2)]
            U8all = RW.alloc("U8", [64, NJ], BF16)
            U8m = [T(U8all.ap[:, m * 8:(m + 1) * 8, :], f"U8m{m}") for m in range(8)]
            GJ2 = [RW.alloc(f"GJ{i}", [8, 128], BF16) for i in range(2)]
            t1 = RW.alloc("t1", [64], F32)
            t2 = RW.alloc("t2", [64], F32)
            m1 = RW.alloc("m1", [64], F32)
            m2 = RW.alloc("m2", [64], F32)
            stS = RW.alloc("stS", [4, 64], F32)
            stW = RW.alloc("stW", [4, 64], F32)
            so = RW.alloc("so", [64, 4], F32)
            WzS = Wz.sw

            def stageA(c0, nj, m):
                UJ = UJ2[m % 2]
                pb_ = psb[0]
                for tp in range(8):
                    P.op(PE, lambda e, pb_=pb_, tp=tp: e.transpose(
                        out=pb_[0:nj, tp * 128:(tp + 1) * 128],
                        in_=h1T[:, m, c0:c0 + nj * 8].rearrange("p (j t) -> p t j", t=8)[:, tp, :], identity=identb[:]),
                        reads=[h1T, identb], writes=[pb_])
                P.op(ACT, lambda e, pb_=pb_, UJ=UJ: e.copy(out=UJ[0:nj].rearrange("p g (t c) -> p t g c", t=8),
                                                          in_=pb_[0:nj, :].rearrange("p (t g c) -> p t g c", t=8, g=8)), reads=[pb_], writes=[UJ])

            def stageB(nj, m):
                UJ = UJ2[m % 2]
                pb2 = psb[1]
                for gl in range(8):
                    P.op(PE, lambda e, pb2=pb2, gl=gl, UJ=UJ: e.transpose(out=pb2[:, gl * NJ:gl * NJ + nj], in_=UJ[0:nj, gl, :],
                                                                          identity=identb[0:nj, 0:nj]), reads=[UJ, identb], writes=[pb2])
                P.op(DVE, lambda e, pb2=pb2, m=m: e.tensor_copy(out=U8m[m][:, :, 0:nj], in_=pb2[:, 0:8 * NJ].rearrange("p (g j) -> p g j", g=8)[:, :, 0:nj]),
                     reads=[pb2], writes=[U8m[m]])

            def stageC(nj, m):
                ps = pf()
                for gl in range(8):
                    g = m * 8 + gl
                    P.op(PE, lambda e, ps=ps, g=g, gl=gl, m=m: e.matmul(ps[:, gl * NJ:gl * NJ + nj], lhsT=Wz[:, g, :], rhs=U8m[m][:, gl, 0:nj], start=True, stop=True),
                         reads=[Wz, U8m[m]], writes=[ps])
                    P.op(PE, lambda e, ps=ps, g=g, gl=gl, m=m: e.matmul(ps[:, 256 + gl * NJ:256 + gl * NJ + nj], lhsT=WzS[:, g, :], rhs=U8m[m][:, gl, 0:nj],
                                                                        start=True, stop=True), reads=[WzS, U8m[m]], writes=[ps])
                P.op(ACT, lambda e, ps=ps, m=m: e.copy(out=Zt[:, m * 8:(m + 1) * 8, 0:nj],
                                                       in_=ps[:, 0:256].rearrange("p (g j) -> p g j", g=8)[:, :, 0:nj]), reads=[ps], writes=[Zt])
                P.op(ACT, lambda e, ps=ps, m=m: e.copy(out=Zs[:, m * 8:(m + 1) * 8, 0:nj],
                                                       in_=ps[:, 256:512].rearrange("p (g j) -> p g j", g=8)[:, :, 0:nj]), reads=[ps], writes=[Zs])

            def to_state(c0, nj):
                for step in range(10):
                    if step < 8:
                        stageA(c0, nj, step)
                    if 0 <= step - 1 < 8:
                        stageB(nj, step - 1)
                    if 0 <= step - 2 < 8:
                        stageC(nj, step - 2)

            def stageD(nj, m):
                GJ = GJ2[m % 2]
                for hb in range(2):
                    ps = pf()
                    for gi in range(4):
                        gl = hb * 4 + gi
                        g = m * 8 + gl
                        P.op(PE, lambda e, ps=ps, g=g, gl=gl, gi=gi, m=m: e.matmul(ps[0:nj, gi * 128:(gi + 1) * 128], lhsT=U8m[m][:, gl, 0:nj], rhs=Tt[:, g, :],
                                                                                  start=True, stop=False), reads=[U8m[m], Tt], writes=[ps])
                        P.op(PE, lambda e, ps=ps, g=g, gi=gi: e.matmul(ps[0:nj, gi * 128:(gi + 1) * 128], lhsT=Sall[:, g, 0:nj], rhs=YC[:, g, :],
                                                                      start=False, stop=True), reads=[Sall, YC], writes=[ps])
                    P.op(ACT, lambda e, ps=ps, hb=hb, GJ=GJ: e.activation(out=GJ[0:nj].rearrange("p t (g c) -> p g t c", g=8)[:, hb * 4:hb * 4 + 4],
                                                                          in_=ps[0:nj, :].rearrange("p (g t c) -> p g t c", g=4, t=8),
                                                                          func=AF.Gelu_apprx_tanh), reads=[ps], writes=[GJ])

            def stageE(c0, nj, m):
                GJ = GJ2[m % 2]
                pb_ = pbk()
                for tp in range(8):
                    P.op(PE, lambda e, pb_=pb_, tp=tp, GJ=GJ: e.transpose(out=pb_[:, tp * NJ:tp * NJ + nj], in_=GJ[0:nj, tp, :],
                                                                          identity=identb[0:nj, 0:nj]), reads=[GJ, identb], writes=[pb_])
                P.op(DVE, lambda e, pb_=pb_, m=m: e.tensor_copy(out=h1T[:, m, c0:c0 + nj * 8].rearrange("p (j t) -> p t j", t=8),
                                                                in_=pb_[:, 0:8 * NJ].rearrange("p (t j) -> p t j", t=8)[:, :, 0:nj]),
                     reads=[pb_], writes=[h1T])

            def out_stage(c0, nj):
                for step in range(9):
                    if step < 8:
                        stageD(nj, step)
                    if 0 <= step - 1 < 8:
                        stageE(c0, nj, step - 1)

            for c0 in range(0, ncols_prompt, NJ * 8):
                to_state(c0, NJ)
                for j in range(NJ):
                    if full:
                        P.op(POOL, lambda e, j=j: e.tensor_copy(out=Sall[:, :, j], in_=Sst[:]), reads=[Sst], writes=[Sall])
                    P.op(DVE, lambda e, j=j: e.tensor_tensor(out=t1[:], in0=Sst[:], in1=Zt[:, :, j], op=ALU.add), reads=[Sst, Zt], writes=[t1])
                    P.op(DVE, lambda e, j=j: e.tensor_tensor(out=t2[:], in0=Ssw[:], in1=Zs[:, :, j], op=ALU.add), reads=[Ssw, Zs], writes=[t2])
                    P.op(DVE, lambda e: e.tensor_tensor(out=m1[:], in0=A1[:], in1=t1[:], op=ALU.mult), reads=[A1, t1], writes=[m1])
                    P.op(DVE, lambda e: e.tensor_tensor(out=m2[:], in0=A2[:], in1=t2[:], op=ALU.mult), reads=[A2, t2], writes=[m2])
                    P.op(DVE, lambda e: e.tensor_tensor(out=Sst[:], in0=m1[:], in1=m2[:], op=ALU.add), reads=[m1, m2], writes=[Sst])
                    P.op(DVE, lambda e: e.tensor_tensor(out=m1[:], in0=A1[:], in1=t2[:], op=ALU.mult), reads=[A1, t2], writes=[m1])
                    P.op(DVE, lambda e: e.tensor_tensor(out=m2[:], in0=A2[:], in1=t1[:], op=ALU.mult), reads=[A2, t1], writes=[m2])
                    P.op(DVE, lambda e: e.tensor_tensor(out=Ssw[:], in0=m1[:], in1=m2[:], op=ALU.subtract), reads=[m1, m2], writes=[Ssw])
                if full:
                    out_stage(c0, NJ)
            if full:
                P.dma(SP, lambda e: e.dma_start(out=so_p, in_=Sst[:]), reads=[Sst])
            if sample:
                c0 = NT
                P.dma(SP, lambda e: e.dma_start(out=stS[:], in_=st_s), writes=[stS])
                P.dma(SP, lambda e: e.dma_start(out=stW[:], in_=st_sw), writes=[stW])
                to_state(c0, 4)
                P.op(POOL, lambda e: e.tensor_copy(out=Sall[:, :, 0:4], in_=stS[:].rearrange("p s g -> p g s")), reads=[stS], writes=[Sall])
                out_stage(c0, 4)
                t1v = so[:]
                P.op(DVE, lambda e: e.tensor_tensor(out=so[:], in0=stS[:].rearrange("p s g -> p g s"), in1=Zt[:, :, 0:4], op=ALU.add), reads=[stS, Zt], writes=[so])
                sw = RW.alloc("sw", [64, 4], F32)
                P.op(DVE, lambda e: e.tensor_tensor(out=sw[:], in0=stW[:].rearrange("p s g -> p g s"), in1=Zs[:, :, 0:4], op=ALU.add), reads=[stW, Zs], writes=[sw])
                P.op(DVE, lambda e: e.tensor_tensor(out=so[:], in0=so[:], in1=A1[:].unsqueeze(2).broadcast_to([128, 64, 4]), op=ALU.mult), reads=[so, A1], writes=[so])
                P.op(DVE, lambda e: e.tensor_tensor(out=sw[:], in0=sw[:], in1=A2[:].unsqueeze(2).broadcast_to([128, 64, 4]), op=ALU.mult), reads=[sw, A2], writes=[sw])
                P.op(DVE, lambda e: e.tensor_tensor(out=stS[:].rearrange("p s g -> p g s"), in0=so[:], in1=sw[:], op=ALU.add), reads=[so, sw], writes=[stS])
                P.dma(SP, lambda e: e.dma_start(out=so_s, in_=stS[:]), reads=[stS])

        def layer1_norm(ncols):
            P.barrier_all()
            R3.reset()
            RW.reset()
            h1T = R3.alloc("h1T", [8, NTOT], BF16)
            sq = RW.alloc("sq", [512], BF16)
            rstd = RW.alloc("rstd", [512], F32)
            for c0 in range(0, ncols, 512):
                n = min(512, ncols - c0)
                rmsnorm_cols(xT, c0, n, gmix[:, 1, :], h1T, c0, sq, rstd)
            return h1T

        def ssm_pred():
            P.mark('ssm_pred start')
            h1T = layer1_norm(NT)
            P.mark('pred norm done')
            P.barrier_all()
            P.op(DVE, lambda e: e.memset(Sst[:], 0.0), writes=[Sst])
            P.op(DVE, lambda e: e.memset(Ssw[:], 0.0), writes=[Ssw])
            Wz, Tt, YC = ssm_tables(False)
            P.mark('pred tables done')
            ssm_blocks(h1T, NT, Wz, Tt, YC, False, False)
            P.mark('pred blocks done')

        def ssm_own():
            h1T = layer1_norm(NTOT)
            P.barrier_all()
            if not FLAGS["pred"]:
                P.op(DVE, lambda e: e.memset(Sst[:], 0.0), writes=[Sst])
                P.op(DVE, lambda e: e.memset(Ssw[:], 0.0), writes=[Ssw])
            P.mark('own ssm tables start')
            Wz, Tt, YC = ssm_tables(True)
            P.mark('own tables done')
            ssm_blocks(h1T, NT, Wz, Tt, YC, True, True)
            P.mark('own blocks done')
            return h1T

        def glu_mix(GT):
            RW.reset()
            R2.reset()
            wab = [R2.alloc(f"wab{i}", [8, 512], BF16) for i in range(4)]
            sig = [RW.alloc(f"sig{i}", [512], F32) for i in range(2)]
            tmp = [RW.alloc(f"gtmp{i}", [512], F32) for i in range(2)]
            wv = w_glu.rearrange("(k p) c -> p k c", p=128)
            blocks = [(c, min(512, NTOT - c)) for c in range(0, NTOT, 512)]

            def load(ch):
                wat, wbt = wab[2 * (ch % 2)], wab[2 * (ch % 2) + 1]
                for hh in range(2):
                    P.dma(POOL, lambda e, wat=wat, hh=hh, ch=ch: e.dma_start(
                        out=wat[:, hh * 4:hh * 4 + 4, :], in_=wv[:, hh * 4:hh * 4 + 4, ch * 512:(ch + 1) * 512]), writes=[wat])
                    P.dma(POOL, lambda e, wbt=wbt, hh=hh, ch=ch: e.dma_start(
                        out=wbt[:, hh * 4:hh * 4 + 4, :], in_=wv[:, hh * 4:hh * 4 + 4, D + ch * 512:D + (ch + 1) * 512]), writes=[wbt])

            load(0)
            load(1)
            P.pe_free = FLAGS.get('pe_free', True)
            for cc in range(8):
                wat, wbt = wab[2 * ((cc // 4) % 2)], wab[2 * ((cc // 4) % 2) + 1]
                cq = cc % 4
                for bi, (c0, n) in enumerate(blocks):
                    pa, pb2 = pf(), pf()
                    for k in range(8):
                        P.op(PE, lambda e, pa=pa, wat=wat, k=k, c0=c0, n=n, cq=cq: e.matmul(
                            pa[:, 0:n], lhsT=wat[:, k, cq * 128:(cq + 1) * 128], rhs=GT[:, k, c0:c0 + n],
                            start=(k == 0), stop=(k == 7)), reads=[wat, GT], writes=[pa])
                    for k in range(8):
                        P.op(PE, lambda e, pb2=pb2, wbt=wbt, k=k, c0=c0, n=n, cq=cq: e.matmul(
                            pb2[:, 0:n], lhsT=wbt[:, k, cq * 128:(cq + 1) * 128], rhs=GT[:, k, c0:c0 + n],
                            start=(k == 0), stop=(k == 7)), reads=[wbt, GT], writes=[pb2])
                    sg_, tp_ = sig[bi % 2], tmp[bi % 2]
                    P.op(ACT, lambda e, pb2=pb2, sg_=sg_, cc=cc, n=n: e.activation(out=sg_[:, 0:n], in_=pb2[:, 0:n], func=AF.Sigmoid,
                                                                                   bias=bglu[:, 8 + cc:9 + cc]), reads=[pb2, bglu], writes=[sg_])
                    P.op(DVE, lambda e, pa=pa, sg_=sg_, tp_=tp_, cc=cc, n=n: e.scalar_tensor_tensor(
                        out=tp_[:, 0:n], in0=pa[:, 0:n], scalar=bglu[:, cc:cc + 1], in1=sg_[:, 0:n], op0=ALU.add, op1=ALU.mult),
                        reads=[pa, sg_, bglu], writes=[tp_])
                    P.op(POOL, lambda e, tp_=tp_, cc=cc, c0=c0, n=n: e.tensor_tensor(out=xT[:, cc, c0:c0 + n], in0=xT[:, cc, c0:c0 + n], in1=tp_[:, 0:n],
                                                                                     op=ALU.add), reads=[tp_, xT], writes=[xT])
            P.pe_free = False

        def final_out():
            P.barrier_all()
            R3.reset()
            RW.reset()
            sq = RW.alloc("sq", [512], BF16)
            rstd = RW.alloc("rstd", [512], F32)
            yT = RW.alloc("yT", [8, 512], F32)
            ys = [RW.alloc(f"ys{i}", [1024], F32) for i in range(2)]
            ps_ss = psf[4]
            cnt_ = 0
            for c0 in range(0, NTOT, 512):
                n = min(512, NTOT - c0)
                for k in range(8):
                    P.op(ACT, lambda e, k=k, c0=c0, n=n: e.activation(out=sq[:, 0:n], in_=xT[:, k, c0:c0 + n], func=AF.Square), reads=[xT], writes=[sq])
                    P.op(PE, lambda e, k=k, n=n: e.matmul(ps_ss[:, 0:n], lhsT=onesb[:], rhs=sq[:, 0:n], start=(k == 0), stop=(k == 7)),
                         reads=[sq, onesb], writes=[ps_ss])
                P.op(ACT, lambda e, n=n: e.activation(out=rstd[:, 0:n], in_=ps_ss[:, 0:n], func=AF.Sqrt, bias=epsT[:, 0:1], scale=1.0 / D),
                     reads=[ps_ss, epsT], writes=[rstd])
                P.op(DVE, lambda e, n=n: e.reciprocal(out=rstd[:, 0:n], in_=rstd[:, 0:n]), reads=[rstd], writes=[rstd])
                for k in range(8):
                    P.op(DVE, lambda e, k=k, c0=c0, n=n: e.scalar_tensor_tensor(out=yT[:, k, 0:n], in0=xT[:, k, c0:c0 + n], scalar=gfin[:, k:k + 1],
                                                                                in1=rstd[:, 0:n], op0=ALU.mult, op1=ALU.mult), reads=[xT, rstd, gfin], writes=[yT])
                for tt in range((n + 127) // 128):
                    r = min(128, n - tt * 128)
                    yst = ys[cnt_ % 2]
                    cnt_ += 1
                    for kk in range(2):
                        ps = pf()
                        for k4 in range(4):
                            k = kk * 4 + k4
                            P.op(PE, lambda e, ps=ps, k=k, k4=k4, tt=tt, r=r: e.transpose(out=ps[0:r, k4 * 128:(k4 + 1) * 128],
                                                                                         in_=yT[:, k, tt * 128:tt * 128 + r], identity=identf[:]),
                                 reads=[yT, identf], writes=[ps])
                        P.op(ACT, lambda e, ps=ps, kk=kk, r=r, yst=yst: e.copy(out=yst[0:r, kk * 512:(kk + 1) * 512], in_=ps[0:r, :]), reads=[ps], writes=[yst])
                    dst = y_s if c0 == NT else y_p[c0 + tt * 128:c0 + tt * 128 + r, :]
                    P.dma(SP, lambda e, yst=yst, r=r, dst=dst: e.dma_start(out=dst, in_=yst[0:r, :]), reads=[yst])

        def act_view(off_bytes, nblk, ncols):
            v = R2.base[:, off_bytes // 4:(off_bytes + nblk * ncols * 2) // 4].bitcast(BF16)
            return T(v.rearrange("p (a b) -> p a b", a=nblk), "actT")

        if FLAGS["pred"]:
            P.mark('pred layer0')
            layer0_mixer("pred")
            if FLAGS.get("ffn"):
                P.mark('pred ffn0')
                ffn(0, NT, act_view(33024, 4, NT), 4, 33024 + 4 * NT * 2)
            if FLAGS.get("ssm"):
                ssm_pred()
        P.mark('own layer0 start')
        layer0_mixer("own")
        P.mark('own layer0 done')
        if FLAGS.get("ffn"):
            P.mark('own ffn0')
            ffn(0, NTOT, act_view(0, 8, NTOT), 8, 8 * NTOT * 2)
            P.mark('own l1norm')
        if FLAGS.get("ssm"):
            GT = ssm_own()
            if FLAGS.get('dbg') == 'GT':
                P.barrier_all()
                for k_ in range(8):
                    P.op(DVE, lambda e, k_=k_: e.tensor_copy(out=xT[:, k_, :], in_=GT[:, k_, :]), reads=[GT], writes=[xT])
            if FLAGS.get("glu"):
                P.barrier_all()
                glu_mix(GT)
                P.mark('glu done')
                if FLAGS.get('stop') != 'x3':
                    ffn(1, NTOT, act_view(0, 8, NTOT), 8, 8 * NTOT * 2)
                    P.mark('ffn1 done')
                    final_out()
        if FLAGS.get('dbg'):
            P.dma(SP, lambda e: e.dma_start(out=dbg_x, in_=xT[:]), reads=[xT])
        P.barrier_all()
        print('recorded ops', P.n)
        P.emit()
    return nc


_NC_CACHE = {}


def _prep(x_prompt, x_sample, cache_k, cache_v, page_table, state_ssm_re, state_ssm_im,
          norm_mix, norm_ffn, norm_final, w_in_even, w_out_even, sgu_norm, sgu_w, sgu_b,
          lambda_q1, lambda_k1, lambda_q2, lambda_k2, attn_subln,
          ssm_a_re, ssm_a_im, ssm_log_dt, ssm_b_re, ssm_b_im, ssm_c_re, ssm_c_im, ssm_d,
          w_glu, b_glu, w_ffn_in, w_ffn_out):
    f = lambda a: np.ascontiguousarray(np.asarray(a))
    x_prompt, x_sample = f(x_prompt), f(x_sample)
    ck = f(cache_k).reshape(-1, 512)
    cv = f(cache_v).reshape(-1, 512)

    def gam(a):
        a = f(a)
        return np.ascontiguousarray(a.reshape(a.shape[0], 8, 128).transpose(0, 2, 1))

    sgu_w = f(sgu_w)[0]
    sgu_wT = np.ascontiguousarray(sgu_w.transpose(0, 2, 1))
    sgu_wTs = np.zeros((4, 32, 32), np.float32)
    for s_ in range(4):
        sgu_wTs[:, s_ * 8:(s_ + 1) * 8, s_ * 8:(s_ + 1) * 8] = sgu_wT[:, :8, :8]
    sgu_b0 = f(sgu_b)[0]
    lam = np.stack([f(lambda_q1)[0], f(lambda_k1)[0], f(lambda_q2)[0], f(lambda_k2)[0]])
    dsk = f(ssm_d)[0].reshape(64, 16)
    d_skip = np.ascontiguousarray(np.tile(dsk.T[None, :, :], (8, 1, 1)).reshape(128, 64))
    bre_ = np.ascontiguousarray(f(ssm_b_re)[0].transpose(1, 0, 2))
    bim_ = np.ascontiguousarray(f(ssm_b_im)[0].transpose(1, 0, 2))
    cre_ = np.ascontiguousarray(f(ssm_c_re)[0].transpose(2, 0, 1))
    cim_ = np.ascontiguousarray(f(ssm_c_im)[0].transpose(2, 0, 1))
    common = {
        "cache_k": ck, "cache_v": cv,
        "gam_mix": gam(norm_mix), "gam_ffn": gam(norm_ffn), "gam_fin": gam(f(norm_final)[None])[0],
        "w_in": f(w_in_even)[0], "w_out": f(w_out_even)[0],
        "sgu_norm": f(sgu_norm)[0][None, :], "sgu_wT": sgu_wT, "sgu_wTs": sgu_wTs,
        "sgu_b": sgu_b0, "sgu_bs": np.ascontiguousarray(np.tile(sgu_b0[:, :8], (1, 4))),
        "lam": lam, "subln": f(attn_subln)[0][None, :],
        "a_re": np.ascontiguousarray(np.tile(f(ssm_a_re)[0].T, (2, 1))), "a_im": np.ascontiguousarray(np.tile(f(ssm_a_im)[0].T, (2, 1))),
        "logdt": np.ascontiguousarray(np.tile(f(ssm_log_dt)[0][None, :], (128, 1))),
        "b_A": np.concatenate([bre_, bim_], 0), "b_B": np.concatenate([bim_, bre_], 0),
        "c_A": np.concatenate([cre_, cre_], 0), "c_B": np.concatenate([cim_, cim_], 0),
        "d_skip": d_skip,
        "w_glu": f(w_glu)[0], "b_glu": np.ascontiguousarray(f(b_glu)[0].reshape(16, 128).T),
        "w_f1": f(w_ffn_in), "w_f2": f(w_ffn_out),
    }
    in_maps = []
    for c in range(8):
        b, half = c // 2, c % 2
        m = dict(common)
        m["x_own"] = x_prompt[b, half * NT:(half + 1) * NT]
        m["x_pred"] = x_prompt[b, 0:NT] if half == 1 else np.zeros((NT, D), np.float32)
        m["x_smp"] = x_sample[4 * c:4 * c + 4].reshape(NS, D)
        m["pbias"] = np.full((128, 1), 0.0 if half == 1 else -30000.0, np.float32)
        m["ptab"] = f(page_table)[4 * c:4 * c + 4].astype(np.int32)
        sre_ = np.ascontiguousarray(f(state_ssm_re)[0, 4 * c:4 * c + 4].transpose(2, 0, 1))
        sim_ = np.ascontiguousarray(f(state_ssm_im)[0, 4 * c:4 * c + 4].transpose(2, 0, 1))
        m["st_s"] = np.concatenate([sre_, sim_], 0)
        m["st_sw"] = np.concatenate([sim_, sre_], 0)
        in_maps.append(m)
    return in_maps


def kernel(**inputs):
    in_maps = _prep(**inputs)
    n_rows = in_maps[0]["cache_k"].shape[0]
    if n_rows not in _NC_CACHE:
        _NC_CACHE[n_rows] = build_program(n_rows)
    nc = _NC_CACHE[n_rows]
    res = run_bass_kernel_spmd(nc, in_maps, core_ids=list(range(8))).results

    B, S = 4, 4096
    y_prompt = np.zeros((B, S, D), np.float32)
    y_sample = np.zeros((32, 8, D), np.float32)
    k_p = np.zeros((1, B, S, 4, 128), np.float32)
    v_p = np.zeros((1, B, S, 4, 128), np.float32)
    k_s = np.zeros((1, 32, 8, 4, 128), np.float32)
    v_s = np.zeros((1, 32, 8, 4, 128), np.float32)
    cvs = np.zeros((1, 32, 8, 512), np.float32)
    sp_re = np.zeros((1, B, 64, 64), np.float32)
    sp_im = np.zeros((1, B, 64, 64), np.float32)
    ss_re = np.zeros((1, 32, 64, 64), np.float32)
    ss_im = np.zeros((1, 32, 64, 64), np.float32)
    for c in range(8):
        b, half = c // 2, c % 2
        r = res[c]
        sl = slice(half * NT, (half + 1) * NT)
        y_prompt[b, sl] = r["y_p"]
        y_sample[4 * c:4 * c + 4] = r["y_s"].reshape(4, 8, D)
        k_p[0, b, sl] = r["kr_p"].reshape(NT, 4, 128)
        v_p[0, b, sl] = r["vr_p"].reshape(NT, 4, 128)
        k_s[0, 4 * c:4 * c + 4] = r["kr_s"].reshape(4, 8, 4, 128)
        v_s[0, 4 * c:4 * c + 4] = r["vr_s"].reshape(4, 8, 4, 128)
        cvs[0, 4 * c:4 * c + 4] = r["cv_s"].reshape(4, 8, 512)
        if half == 1:
            sp_re[0, b] = r["so_p"][0:64].T
            sp_im[0, b] = r["so_p"][64:128].T
        ss_re[0, 4 * c:4 * c + 4] = r["so_s"][0:64].transpose(1, 2, 0)
        ss_im[0, 4 * c:4 * c + 4] = r["so_s"][64:128].transpose(1, 2, 0)
    return (y_prompt, y_sample, k_p, v_p, k_s, v_s, cvs, sp_re, sp_im, ss_re, ss_im)
```

```python
import contextlib
import math
import numpy as np
import concourse.bass as bass
import concourse.mybir as mybir
from concourse.bass_utils import run_bass_kernel_spmd

F32 = mybir.dt.float32
BF16 = mybir.dt.bfloat16
I32 = mybir.dt.int32
AF = mybir.ActivationFunctionType
ALU = mybir.AluOpType

PE, ACT, DVE, POOL, SP = "pe", "act", "dve", "pool", "sp"
COMPUTE = (PE, ACT, DVE, POOL)
NRING = {SP: 12, POOL: 8}

D = 1024
NT = 2048
NS = 32
NTOT = NT + NS
NPH = 2560
DFF = 2816
EPS = 1e-6
LAM_INIT = 0.8 - 0.6 * math.exp(-0.3 * 0)
SCALE = 64 ** -0.5
NPAGE = 64
FLAGS = dict(sgu=True, attn=True, outp=True, smp_attn=True, pred=True, ffn=True, ssm=True, glu=True)


class T:
    __slots__ = ("ap", "lw", "rd", "name", "sw")

    def __init__(self, ap, name=""):
        self.ap = ap
        self.lw = None
        self.rd = []
        self.name = name

    def __getitem__(self, idx):
        return self.ap[idx]


class Prog:
    def __init__(self, nc):
        self.nc = nc
        self.streams = {e: [] for e in (PE, ACT, DVE, POOL, SP)}
        self.cnt = {e: 0 for e in COMPUTE}
        self.seen = {e: {} for e in (PE, ACT, DVE, POOL, SP)}
        self.ring_pos = {q: 0 for q in NRING}
        self.ring_cnt = {q: [0] * NRING[q] for q in NRING}
        self.n = 0
        self.limit = FLAGS.get('limit', 10 ** 9)
        self.scope = 'setup'
        self.pe_free = False

    def _need(self, eng, events):
        seen = self.seen[eng]
        best = {}
        for ev in events:
            if ev is None:
                continue
            k, v = ev
            if best.get(k, 0) < v:
                best[k] = v
        out = []
        for k, v in best.items():
            if seen.get(k, 0) >= v:
                continue
            seen[k] = v
            out.append((k, v))
        return out

    @staticmethod
    def _deps(reads, writes):
        evs = []
        for t in reads:
            evs.append(t.lw)
        for t in writes:
            evs.append(t.lw)
            evs.extend(t.rd)
        return evs

    @staticmethod
    def _commit(ev, reads, writes):
        for t in reads:
            t.rd.append(ev)
            if len(t.rd) > 16:
                best = {}
                for k, v in t.rd:
                    if best.get(k, 0) < v:
                        best[k] = v
                t.rd = list(best.items())
        for t in writes:
            t.lw = ev
            t.rd = []

    def op(self, eng, fn, reads=(), writes=()):
        self.n += 1
        if self.n > self.limit:
            return None
        waits = self._need(eng, [ev for ev in self._deps(reads, writes) if ev is not None and not (self.pe_free and eng == PE and ev[0] == PE)])
        self.cnt[eng] += 1
        ev = (eng, self.cnt[eng])
        self.streams[eng].append((waits, fn, ev, 1, self.scope))
        self._commit(ev, reads, writes)
        return ev

    def dma(self, q, fn, reads=(), writes=()):
        self.n += 1
        if self.n > self.limit:
            return None
        n = NRING[q]
        slot = self.ring_pos[q] % n
        self.ring_pos[q] += 1
        key = ("ring", q, slot)
        prev = self.ring_cnt[q][slot]
        evs = self._deps(reads, writes)
        if prev > 0:
            evs.append((key, prev))
        waits = self._need(q, evs)
        self.ring_cnt[q][slot] = prev + 16
        ev = (key, prev + 16)
        self.streams[q].append((waits, fn, ev, 16, self.scope))
        self._commit(ev, reads, writes)
        return ev

    def mark(self, name):
        self.scope = name.replace(' ', '_')
        if FLAGS.get('marks'):
            print('MARK', name, self.n)

    def barrier_all(self):
        evs = [(e, self.cnt[e]) for e in COMPUTE if self.cnt[e] > 0]
        for q in NRING:
            for s in range(NRING[q]):
                if self.ring_cnt[q][s] > 0:
                    evs.append((("ring", q, s), self.ring_cnt[q][s]))
        for e in (PE, ACT, DVE, POOL, SP):
            waits = self._need(e, evs)
            if waits:
                self.streams[e].append((waits, None, None, 0, self.scope))

    def emit(self):
        nc = self.nc
        keys = set()
        for e in self.streams:
            for waits, fn, ev, inc, _sc in self.streams[e]:
                for k, v in waits:
                    keys.add(k)
                if ev is not None:
                    keys.add(ev[0])
        keys = sorted(keys, key=str)
        needed = {e: set() for e in COMPUTE}
        for e in self.streams:
            for waits, fn, ev, inc, _sc in self.streams[e]:
                for k, v in waits:
                    if k in needed:
                        needed[k].add(v)
        rank = {e: {v: i + 1 for i, v in enumerate(sorted(needed[e]))} for e in COMPUTE}
        if FLAGS.get('allsem'):
            rank = {}
        with contextlib.ExitStack() as st:
            sems = {}
            for k in keys:
                nm = "s_" + "_".join(str(x) for x in (k if isinstance(k, tuple) else (k,)))
                sems[k] = st.enter_context(nc.semaphore(nm))
            block = st.enter_context(nc.Block())

            def run(stream, eh):
                use_scopes = FLAGS.get('scopes')
                for waits, fn, ev, inc, sc in stream:
                    for k, v in waits:
                        eh.wait_ge(sems[k], rank[k][v] if k in rank else v)
                    if fn is None:
                        continue
                    if use_scopes:
                        with nc.named_scope(sc):
                            ins = fn(eh)
                    else:
                        ins = fn(eh)
                    if ev[0] in rank:
                        if ev[1] in rank[ev[0]]:
                            ins.then_inc(sems[ev[0]], 1)
                    else:
                        ins.then_inc(sems[ev[0]], inc)

            @block.tensor
            def _(e):
                run(self.streams[PE], e)

            @block.scalar
            def _(e):
                run(self.streams[ACT], e)

            @block.vector
            def _(e):
                run(self.streams[DVE], e)

            @block.gpsimd
            def _(e):
                run(self.streams[POOL], e)

            @block.sync
            def _(e):
                run(self.streams[SP], e)


class Arena:
    def __init__(self, nc, st, name, nbytes):
        self.base = st.enter_context(nc.sbuf_tensor(name, [128, nbytes // 4], F32))
        self.cap = nbytes
        self.off = 0
        self.name = name

    def reset(self):
        self.off = 0

    def alloc(self, name, free_shape, dt):
        esz = 2 if dt == BF16 else 4
        n = 1
        for s in free_shape:
            n *= s
        nb = (n * esz + 31) // 32 * 32
        assert self.off + nb <= self.cap, f"arena {self.name} overflow at {name}: {self.off}+{nb}>{self.cap}"
        ap = self.base[:, self.off // 4:(self.off + nb) // 4]
        if dt == BF16:
            ap = ap.bitcast(BF16)
        elif dt == I32:
            ap = ap.bitcast(I32)
        ap = ap[:, 0:n]
        if len(free_shape) == 2:
            ap = ap.rearrange("p (a b) -> p a b", a=free_shape[0])
        elif len(free_shape) == 3:
            ap = ap.rearrange("p (a b c) -> p a b c", a=free_shape[0], b=free_shape[1])
        self.off += nb
        return T(ap, name)


def build_program(n_rows_cache):
    nc = bass.Bass("TRN2", target_bir_lowering=False)

    def din(name, shape, dt=F32):
        return nc.dram_tensor(name, list(shape), dt, kind="ExternalInput").ap()

    def dout(name, shape, dt=F32):
        return nc.dram_tensor(name, list(shape), dt, kind="ExternalOutput").ap()

    x_own = din("x_own", [NT, D])
    x_pred = din("x_pred", [NT, D])
    x_smp = din("x_smp", [NS, D])
    pbias_d = din("pbias", [128, 1])
    cache_k = din("cache_k", [n_rows_cache, 512])
    cache_v = din("cache_v", [n_rows_cache, 512])
    ptab = din("ptab", [4, NPAGE], I32)
    st_s = din("st_s", [128, 4, 64])
    st_sw = din("st_sw", [128, 4, 64])
    gam_mix = din("gam_mix", [2, 128, 8])
    gam_ffn = din("gam_ffn", [2, 128, 8])
    gam_fin = din("gam_fin", [128, 8])
    w_in = din("w_in", [D, NPH])
    w_out = din("w_out", [D, D])
    sgu_norm_b = din("sgu_norm", [1, 512])
    sgu_wT = din("sgu_wT", [4, 128, 128])
    sgu_wTs = din("sgu_wTs", [4, 32, 32])
    sgu_b = din("sgu_b", [4, 128])
    sgu_bs = din("sgu_bs", [4, 32])
    lam_d = din("lam", [4, 64])
    subln_d = din("subln", [1, 128])
    a_re = din("a_re", [128, 64])
    a_im = din("a_im", [128, 64])
    logdt = din("logdt", [128, 64])
    b_A = din("b_A", [128, 64, 16])
    b_B = din("b_B", [128, 64, 16])
    c_A = din("c_A", [128, 64, 16])
    c_B = din("c_B", [128, 64, 16])
    d_skip = din("d_skip", [128, 64])
    w_glu = din("w_glu", [D, 2 * D])
    b_glu = din("b_glu", [128, 16])
    w_f1 = din("w_f1", [2, D, 2 * DFF])
    w_f2 = din("w_f2", [2, DFF, D])

    y_p = dout("y_p", [NT, D])
    y_s = dout("y_s", [NS, D])
    kr_p = dout("kr_p", [NT, 512])
    vr_p = dout("vr_p", [NT, 512])
    kr_s = dout("kr_s", [NS, 512])
    vr_s = dout("vr_s", [NS, 512])
    cv_s = dout("cv_s", [NS, 512])
    so_p = dout("so_p", [128, 64])
    so_s = dout("so_s", [128, 4, 64])
    dbg_x = dout("dbg_x", [128, 8, NTOT]) if FLAGS.get("dbg") else None

    P = Prog(nc)
    with contextlib.ExitStack() as st:
        R1 = Arena(nc, st, "R1", 8 * NTOT * 4)
        R2 = Arena(nc, st, "R2", 66560)
        R3 = Arena(nc, st, "R3", 8 * NTOT * 2)
        RW = Arena(nc, st, "RW", 30720)
        RC = Arena(nc, st, "RC", 14336)
        psf = [T(st.enter_context(nc.psum_tensor(f"psf{i}", [128, 512], F32)), f"psf{i}") for i in range(6)]
        psb = [T(st.enter_context(nc.psum_tensor(f"psb{i}", [128, 1024], BF16)), f"psb{i}") for i in range(2)]
        rot = {"f": 0, "b": 0, "n": 4}

        def pf():
            rot["f"] = (rot["f"] + 1) % rot["n"]
            return psf[rot["f"]]

        def pbk():
            rot["b"] = (rot["b"] + 1) % 2
            return psb[rot["b"]]

        identf = RC.alloc("identf", [128], F32)
        identb = RC.alloc("identb", [128], BF16)
        trimask = RC.alloc("trimask", [128], F32)
        trimb = RC.alloc("trimb", [128], BF16)
        identsw = RC.alloc("identsw", [128], BF16)
        onesb = RC.alloc("onesb", [128], BF16)
        epsT = RC.alloc("epsT", [1], F32)
        gmix = RC.alloc("gmix", [2, 8], F32)
        gffn = RC.alloc("gffn", [2, 8], F32)
        gfin = RC.alloc("gfin", [8], F32)
        pbias = RC.alloc("pbias", [1], F32)
        zbias = RC.alloc("zbias", [1], F32)
        lamneg = RC.alloc("lamneg", [1], F32)
        sublnb = RC.alloc("sublnb", [128], F32)
        sgunb = RC.alloc("sgunb", [512], F32)
        bglu = RC.alloc("bglu", [16], F32)
        sgub = RC.alloc("sgub", [4, 128], BF16)
        sgubs = RC.alloc("sgubs", [4, 32], BF16)
        wmT = RC.alloc("wmT", [4, 128], BF16)
        wmTs = RC.alloc("wmTs", [4, 32], BF16)
        tmpc = RC.alloc("tmpc", [512], F32)
        tmpc2 = RC.alloc("tmpc2", [512], F32)

        P.op(POOL, lambda e: e.iota(tmpc[:, 0:128], pattern=[[1, 128]], base=0, channel_multiplier=-1,
                                    allow_small_or_imprecise_dtypes=True), writes=[tmpc])
        P.op(DVE, lambda e: e.tensor_single_scalar(out=identf[:], in_=tmpc[:, 0:128], scalar=0.0, op=ALU.is_equal),
             reads=[tmpc], writes=[identf])
        P.op(DVE, lambda e: e.tensor_single_scalar(out=trimask[:], in_=tmpc[:, 0:128], scalar=0.0, op=ALU.is_ge),
             reads=[tmpc], writes=[trimask])
        P.op(DVE, lambda e: e.tensor_copy(out=identb[:], in_=identf[:]), reads=[identf], writes=[identb])
        P.op(DVE, lambda e: e.tensor_copy(out=trimb[:], in_=trimask[:]), reads=[trimask], writes=[trimb])
        P.op(DVE, lambda e: e.tensor_tensor(out=tmpc2[:, 0:128], in0=tmpc[:, 0:128], in1=tmpc[:, 0:128], op=ALU.mult), reads=[tmpc], writes=[tmpc2])
        P.op(DVE, lambda e: e.tensor_single_scalar(out=identsw[:], in_=tmpc2[:, 0:128], scalar=4096.0, op=ALU.is_equal),
             reads=[tmpc2], writes=[identsw])
        P.op(DVE, lambda e: e.memset(onesb[:], 1.0), writes=[onesb])
        P.op(DVE, lambda e: e.memset(epsT[:], EPS), writes=[epsT])
        P.op(DVE, lambda e: e.memset(zbias[:], 0.0), writes=[zbias])
        P.dma(SP, lambda e: e.dma_start(out=gmix[:], in_=gam_mix.rearrange("l p k -> p l k")), writes=[gmix])
        P.dma(SP, lambda e: e.dma_start(out=gffn[:], in_=gam_ffn.rearrange("l p k -> p l k")), writes=[gffn])
        P.dma(SP, lambda e: e.dma_start(out=gfin[:], in_=gam_fin), writes=[gfin])
        P.dma(SP, lambda e: e.dma_start(out=pbias[:], in_=pbias_d), writes=[pbias])
        P.dma(SP, lambda e: e.dma_start(out=bglu[:], in_=b_glu), writes=[bglu])
        P.dma(SP, lambda e: e.dma_start(out=sublnb[:], in_=subln_d.partition_broadcast(128).rearrange("p a f -> p (a f)")),
              writes=[sublnb])
        P.dma(SP, lambda e: e.dma_start(out=sgunb[:], in_=sgu_norm_b.partition_broadcast(128).rearrange("p a f -> p (a f)")),
              writes=[sgunb])
        P.op(DVE, lambda e: e.tensor_scalar_mul(out=sublnb[:], in0=sublnb[:], scalar1=1.0 - LAM_INIT),
             reads=[sublnb], writes=[sublnb])
        P.dma(SP, lambda e: e.dma_start(out=tmpc[:, 0:256], in_=lam_d.rearrange("(o a) f -> o (a f)", o=1)
                                        .partition_broadcast(128).rearrange("p a f -> p (a f)")), writes=[tmpc])
        P.op(DVE, lambda e: e.tensor_tensor(out=tmpc2[:, 0:64], in0=tmpc[:, 0:64], in1=tmpc[:, 64:128], op=ALU.mult),
             reads=[tmpc], writes=[tmpc2])
        P.op(DVE, lambda e: e.tensor_tensor(out=tmpc2[:, 64:128], in0=tmpc[:, 128:192], in1=tmpc[:, 192:256], op=ALU.mult),
             reads=[tmpc], writes=[tmpc2])
        P.op(DVE, lambda e: e.tensor_reduce(out=tmpc2[:, 128:130], in_=tmpc2[:, 0:128].rearrange("p (a b) -> p a b", a=2),
                                            axis=mybir.AxisListType.X, op=ALU.add), reads=[tmpc2], writes=[tmpc2])
        P.op(ACT, lambda e: e.activation(out=tmpc2[:, 130:132], in_=tmpc2[:, 128:130], func=AF.Exp),
             reads=[tmpc2], writes=[tmpc2])
        P.op(DVE, lambda e: e.tensor_tensor(out=lamneg[:], in0=tmpc2[:, 131:132], in1=tmpc2[:, 130:131], op=ALU.subtract),
             reads=[tmpc2], writes=[lamneg])
        P.op(DVE, lambda e: e.tensor_scalar_add(out=lamneg[:], in0=lamneg[:], scalar1=-LAM_INIT),
             reads=[lamneg], writes=[lamneg])
        for g in range(4):
            P.dma(SP, lambda e, g=g: e.dma_start(out=tmpc[:, 0:128], in_=sgu_wT[g]), writes=[tmpc])
            P.op(DVE, lambda e, g=g: e.tensor_tensor(out=wmT[:, g, :], in0=tmpc[:, 0:128], in1=trimask[:], op=ALU.mult),
                 reads=[tmpc, trimask], writes=[wmT])
            P.dma(SP, lambda e, g=g: e.dma_start(out=tmpc2[0:32, 0:32], in_=sgu_wTs[g]), writes=[tmpc2])
            P.op(DVE, lambda e, g=g: e.tensor_tensor(out=wmTs[0:32, g, :], in0=tmpc2[0:32, 0:32], in1=trimask[0:32, 0:32],
                                                     op=ALU.mult), reads=[tmpc2, trimask], writes=[wmTs])
        P.dma(SP, lambda e: e.dma_start(out=tmpc[0:1, 0:512], in_=sgu_b.rearrange("(o g) t -> o (g t)", o=1)), writes=[tmpc])
        P.op(DVE, lambda e: e.tensor_copy(out=sgub[0:1], in_=tmpc[0:1, 0:512].rearrange("p (g t) -> p g t", g=4)),
             reads=[tmpc], writes=[sgub])
        P.dma(SP, lambda e: e.dma_start(out=tmpc2[0:1, 0:128], in_=sgu_bs.rearrange("(o g) t -> o (g t)", o=1)), writes=[tmpc2])
        P.op(DVE, lambda e: e.tensor_copy(out=sgubs[0:1], in_=tmpc2[0:1, 0:128].rearrange("p (g t) -> p g t", g=4)),
             reads=[tmpc2], writes=[sgubs])

        def rmsnorm_cols(xT, c0, n, gam_ap, hT, h0, sq, rstd):
            ps = pf()
            for k in range(8):
                P.op(ACT, lambda e, k=k: e.activation(out=sq[:, 0:n], in_=xT[:, k, c0:c0 + n], func=AF.Square),
                     reads=[xT], writes=[sq])
                P.op(PE, lambda e, k=k: e.matmul(ps[:, 0:n], lhsT=onesb[:], rhs=sq[:, 0:n], start=(k == 0), stop=(k == 7)),
                     reads=[sq, onesb], writes=[ps])
            P.op(ACT, lambda e: e.activation(out=rstd[:, 0:n], in_=ps[:, 0:n], func=AF.Sqrt, bias=epsT[:, 0:1], scale=1.0 / D),
                 reads=[ps, epsT], writes=[rstd])
            P.op(DVE, lambda e: e.reciprocal(out=rstd[:, 0:n], in_=rstd[:, 0:n]), reads=[rstd], writes=[rstd])
            for k in range(8):
                P.op(DVE, lambda e, k=k: e.scalar_tensor_tensor(out=hT[:, k, h0:h0 + n], in0=xT[:, k, c0:c0 + n],
                                                                scalar=gam_ap[:, k:k + 1], in1=rstd[:, 0:n],
                                                                op0=ALU.mult, op1=ALU.mult),
                     reads=[xT, rstd], writes=[hT])

        def load_w(dst, src_ap, q=POOL):
            P.dma(q, lambda e: e.dma_start(out=dst.ap if isinstance(dst, T) else dst, in_=src_ap), writes=[dst] if isinstance(dst, T) else [])

        KTp = R2.alloc("KTp", [4, 2048], BF16)
        V1p = R2.alloc("V1p", [16, 4, 130], BF16)
        KTo = R2.alloc("KTo", [4, 2048], BF16)
        V1o = R2.alloc("V1o", [16, 4, 130], BF16)
        KTt = [T((KTp if i < 16 else KTo).ap, f"KTt{i}") for i in range(32)]
        V1t = [T((V1p if i < 16 else V1o).ap, f"V1t{i}") for i in range(32)]

        def ktile(j, half, h):
            return KTt[j][half * 64:(half + 1) * 64, h, (j % 16) * 128:(j % 16 + 1) * 128]

        def vtile(j, h):
            return V1t[j][:, j % 16, h, 0:129]
        xT = R1.alloc("xT", [8, NTOT], F32)

        def layer0_mixer(pas):
            own = pas == "own"
            xsrc = x_own if own else x_pred
            kbase = 2048 if own else 0
            P.barrier_all()
            R3.reset()
            RW.reset()
            vt_ = V1o if own else V1p
            P.op(POOL, lambda e: e.memset(vt_[:, :, :, 128:130], 1.0), writes=[V1t[i + (16 if own else 0)] for i in range(16)])
            hT = R3.alloc("hT", [8, 512], BF16)
            uT = R3.alloc("uT", [4, 512], BF16)
            qT = R3.alloc("qT", [4, 512], BF16)
            aT = R3.alloc("aT", [4, 512], BF16)
            bT = R3.alloc("bT", [4, 512], BF16)
            vn = [R3.alloc(f"vn{i}", [512], BF16) for i in range(4)]
            sq = R3.alloc("sq", [512], BF16)
            rstd = R3.alloc("rstd", [512], F32)
            xs = [RW.alloc(f"xs{i}", [1024], F32) for i in range(2)]
            wtm = [RW.alloc(f"wtm{i}", [8, 512], BF16) for i in range(1)]
            rows = [RW.alloc(f"rows{i}", [512], F32) for i in range(2)]
            gl = RW.alloc("gl", [512], F32)
            wfm = [RW.alloc(f"wfm{i}", [8, 128], BF16) for i in range(2)]
            PT = [RW.alloc(f"PT{i}", [512], BF16) for i in range(2)]
            osb = RW.alloc("osb", [128], F32)
            onb = RW.alloc("onb", [128], BF16)
            sm = RW.alloc("sm", [8], F32)
            cnt = {"xs": 0, "wfm": 0, "rows": 0, "pt": 0}
            w_in_v = w_in.rearrange("(k p) c -> p k c", p=128)
            w_out_v = w_out.rearrange("(k p) c -> p k c", p=128)

            blocks = [(b * 512, 512) for b in range(4)] + ([(NT, NS)] if own else [])
            def do_block(c0, n):
                P.pe_free = False
                smp = c0 == NT
                ntt = (n + 127) // 128
                for tt in range(ntt):
                    r = min(128, n - tt * 128)
                    xs_t = xs[cnt["xs"] % 2]
                    cnt["xs"] += 1
                    src = x_smp if smp else xsrc[c0 + tt * 128:c0 + tt * 128 + r, :]
                    P.dma(SP, lambda e, xs_t=xs_t, src=src, r=r: e.dma_start(out=xs_t[0:r, :], in_=src), writes=[xs_t])
                    for kk in range(2):
                        ps = pf()
                        for k4 in range(4):
                            k = kk * 4 + k4
                            P.op(PE, lambda e, ps=ps, k=k, k4=k4, xs_t=xs_t, r=r: e.transpose(
                                out=ps[:, k4 * 128:k4 * 128 + r], in_=xs_t[0:r, k * 128:(k + 1) * 128], identity=identf[0:r, 0:r]),
                                reads=[xs_t, identf], writes=[ps])
                        P.op(ACT, lambda e, ps=ps, kk=kk, tt=tt, r=r: e.copy(
                            out=xT[:, kk * 4:kk * 4 + 4, c0 + tt * 128:c0 + tt * 128 + r],
                            in_=ps[:].rearrange("p (a b) -> p a b", a=4)[:, :, 0:r]), reads=[ps], writes=[xT])
                rmsnorm_cols(xT, c0, n, gmix[:, 0, :], hT, 0, sq, rstd)
                P.pe_free = FLAGS.get('pe_free_l0', True)
                for cc in [0, 1, 2, 3, 8, 9, 10, 11, 12, 13, 14, 15]:
                    if cc in (8, 9, 10, 11) and not own:
                        pass
                    wt = wfm[cnt["wfm"] % 2]
                    cnt["wfm"] += 1
                    P.dma(POOL, lambda e, wt=wt, cc=cc: e.dma_start(out=wt[:], in_=w_in_v[:, :, cc * 128:(cc + 1) * 128]), writes=[wt])
                    ps = pf()
                    for k in range(8):
                        P.op(PE, lambda e, ps=ps, wt=wt, k=k: e.matmul(ps[:, 0:n], lhsT=wt[:, k, :], rhs=hT[:, k, 0:n],
                                                                      start=(k == 0), stop=(k == 7)),
                             reads=[wt, hT], writes=[ps])
                    if cc < 4:
                        P.op(ACT, lambda e, ps=ps, cc=cc: e.activation(out=uT[:, cc, 0:n], in_=ps[:, 0:n], func=AF.Gelu_apprx_tanh),
                             reads=[ps], writes=[uT])
                    elif cc < 12:
                        P.op(ACT, lambda e, ps=ps, cc=cc: e.copy(out=qT[:, cc - 8, 0:n], in_=ps[:, 0:n]), reads=[ps], writes=[qT])
                    else:
                        h = cc - 12
                        if smp:
                            P.op(DVE, lambda e, ps=ps, h=h: e.tensor_copy(out=KTs[:, h, 0:n], in_=ps[:, 0:n]), reads=[ps], writes=[KTs_t])
                        else:
                            for tt in range(4):
                                ti = (kbase + c0) // 128 + tt
                                P.op(DVE, lambda e, ps=ps, h=h, tt=tt, ti=ti: e.tensor_copy(
                                    out=KTt[ti][:, h, (ti % 16) * 128:(ti % 16 + 1) * 128], in_=ps[:, tt * 128:(tt + 1) * 128]),
                                    reads=[ps], writes=[KTt[ti]])
                for gi, col0 in enumerate((512, 1536, 2048)):
                    if gi == 1 and not own:
                        continue
                    wt = wtm[0]
                    for hh in range(2):
                        P.dma(POOL, lambda e, wt=wt, col0=col0, hh=hh: e.dma_start(
                            out=wt[:, hh * 4:hh * 4 + 4, :], in_=w_in_v[:, hh * 4:hh * 4 + 4, col0:col0 + 512]), writes=[wt])
                    for tt in range(ntt):
                        r = min(128, n - tt * 128)
                        ps = pf()
                        for k in range(8):
                            P.op(PE, lambda e, ps=ps, wt=wt, k=k, tt=tt, r=r: e.matmul(
                                ps[0:r, :], lhsT=hT[:, k, tt * 128:tt * 128 + r], rhs=wt[:, k, :], start=(k == 0), stop=(k == 7)),
                                reads=[wt, hT], writes=[ps])
                        if gi == 0:
                            P.op(ACT, lambda e, ps=ps, r=r: e.activation(out=gl[0:r, :], in_=ps[0:r, :], func=AF.Gelu_apprx_tanh),
                                 reads=[ps], writes=[gl])
                            rw = rows[cnt["rows"] % 2]
                            cnt["rows"] += 1
                            P.op(ACT, lambda e, r=r, rw=rw: e.activation(out=rw[0:r, :], in_=gl[0:r, :], func=AF.Square,
                                                                         accum_out=sm[0:r, 0:1]), reads=[gl], writes=[rw, sm])
                            P.op(ACT, lambda e, r=r: e.activation(out=sm[0:r, 1:2], in_=sm[0:r, 0:1], func=AF.Sqrt,
                                                                  bias=epsT[0:r, 0:1], scale=1.0 / 512), reads=[sm, epsT], writes=[sm])
                            P.op(DVE, lambda e, r=r: e.reciprocal(out=sm[0:r, 2:3], in_=sm[0:r, 1:2]), reads=[sm], writes=[sm])
                            if smp:
                                P.op(DVE, lambda e, r=r, rw=rw: e.scalar_tensor_tensor(
                                    out=rw[0:r, :], in0=gl[0:r, :], scalar=sm[0:r, 2:3], in1=sgunb[0:r, :], op0=ALU.mult, op1=ALU.mult),
                                    reads=[gl, sm, sgunb], writes=[rw])
                                P.dma(SP, lambda e, rw=rw, r=r: e.dma_start(out=cv_s, in_=rw[0:r, :]), reads=[rw])
                                P.op(DVE, lambda e, r=r, rw=rw, tt=tt: e.tensor_copy(out=vn[tt][0:r, :], in_=rw[0:r, :]),
                                     reads=[rw], writes=[vn[tt]])
                            else:
                                P.op(DVE, lambda e, r=r, tt=tt: e.scalar_tensor_tensor(
                                    out=vn[tt][0:r, :], in0=gl[0:r, :], scalar=sm[0:r, 2:3], in1=sgunb[0:r, :], op0=ALU.mult, op1=ALU.mult),
                                    reads=[gl, sm, sgunb], writes=[vn[tt]])
                        else:
                            rw = rows[cnt["rows"] % 2]
                            cnt["rows"] += 1
                            P.op(ACT, lambda e, ps=ps, r=r, rw=rw: e.copy(out=rw[0:r, :], in_=ps[0:r, :]), reads=[ps], writes=[rw])
                            if own:
                                if smp:
                                    dst = kr_s if gi == 1 else vr_s
                                else:
                                    dst = (kr_p if gi == 1 else vr_p)[c0 + tt * 128:c0 + tt * 128 + r, :]
                                P.dma(SP, lambda e, rw=rw, r=r, dst=dst: e.dma_start(out=dst, in_=rw[0:r, :]), reads=[rw])
                            if gi == 2:
                                if smp:
                                    for h_ in range(4):
                                        P.op(POOL, lambda e, rw=rw, r=r, h_=h_: e.tensor_copy(
                                            out=V1s[0:r, h_, 0:128], in_=rw[0:r, h_ * 128:(h_ + 1) * 128]),
                                            reads=[rw], writes=[V1s_t])
                                else:
                                    ti = (kbase + c0) // 128 + tt
                                    for h_ in range(4):
                                        P.op(POOL, lambda e, rw=rw, ti=ti, h_=h_: e.tensor_copy(
                                            out=V1t[ti][:, ti % 16, h_, 0:128], in_=rw[:, h_ * 128:(h_ + 1) * 128]),
                                            reads=[rw], writes=[V1t[ti]])
                P.pe_free = False
                if not FLAGS['sgu']:
                    pass
                elif not smp:
                    for tt in range(4):
                        ps = pf()
                        for g in range(4):
                            P.op(PE, lambda e, ps=ps, g=g, tt=tt: e.matmul(ps[:, g * 128:(g + 1) * 128], lhsT=vn[tt][:, g * 128:(g + 1) * 128],
                                                                           rhs=wmT[:, g, :], start=True, stop=False),
                                 reads=[vn[tt], wmT], writes=[ps])
                            P.op(PE, lambda e, ps=ps, g=g: e.matmul(ps[:, g * 128:(g + 1) * 128], lhsT=onesb[0:1, :],
                                                                    rhs=sgub[0:1, g, :], start=False, stop=True),
                                 reads=[sgub, onesb], writes=[ps])
                        P.op(DVE, lambda e, ps=ps, tt=tt: e.tensor_tensor(
                            out=aT[:, :, tt * 128:(tt + 1) * 128], in0=ps[:].rearrange("p (g t) -> p g t", g=4),
                            in1=uT[:, :, tt * 128:(tt + 1) * 128], op=ALU.mult), reads=[ps, uT], writes=[aT])
                else:
                    ps = pf()
                    for g in range(4):
                        P.op(PE, lambda e, ps=ps, g=g: e.matmul(ps[:, g * 32:(g + 1) * 32], lhsT=vn[0][0:32, g * 128:(g + 1) * 128],
                                                                rhs=wmTs[0:32, g, :], start=True, stop=False),
                             reads=[vn[0], wmTs], writes=[ps])
                        P.op(PE, lambda e, ps=ps, g=g: e.matmul(ps[:, g * 32:(g + 1) * 32], lhsT=onesb[0:1, :],
                                                                rhs=sgubs[0:1, g, :], start=False, stop=True),
                             reads=[sgubs, onesb], writes=[ps])
                    P.op(DVE, lambda e, ps=ps: e.tensor_tensor(
                        out=aT[:, :, 0:32], in0=ps[:, 0:128].rearrange("p (g t) -> p g t", g=4),
                        in1=uT[:, :, 0:32], op=ALU.mult), reads=[ps, uT], writes=[aT])
                if not FLAGS['attn']:
                    pass
                elif not smp:
                    attention_prompt(own, c0, kbase, qT, bT, PT, osb, onb, sm, cnt)
                elif FLAGS['smp_attn']:
                    attention_sample(qT, bT, osb, onb, sm)
                if FLAGS.get('stop') == 'ab' and smp:
                    P.op(DVE, lambda e: e.tensor_copy(out=xT[:, 0:4, 0:32], in_=aT[:, :, 0:32]), reads=[aT], writes=[xT])
                    P.op(DVE, lambda e: e.tensor_copy(out=xT[:, 4:8, 0:32], in_=bT[:, :, 0:32]), reads=[bT], writes=[xT])
                    return
                if FLAGS.get('dbg') == 'bT' and own and c0 == 1536:
                    P.op(DVE, lambda e: e.tensor_copy(out=xT[:, 0:4, 0:512], in_=bT[:]), reads=[bT], writes=[xT])
                    return
                P.pe_free = FLAGS.get('pe_free_l0', True)
                for m in (range(8) if FLAGS['outp'] else []):
                    wt = wfm[cnt["wfm"] % 2]
                    cnt["wfm"] += 1
                    P.dma(POOL, lambda e, wt=wt, m=m: e.dma_start(out=wt[:], in_=w_out_v[:, :, m * 128:(m + 1) * 128]), writes=[wt])
                    ps = pf()
                    for kc in range(8):
                        srcT = aT if kc < 4 else bT
                        P.op(PE, lambda e, ps=ps, wt=wt, kc=kc, srcT=srcT: e.matmul(
                            ps[:, 0:n], lhsT=wt[:, kc, :], rhs=srcT[:, kc % 4, 0:n], start=(kc == 0), stop=(kc == 7)),
                            reads=[wt, srcT], writes=[ps])
                    P.op(DVE, lambda e, ps=ps, m=m: e.tensor_tensor(out=xT[:, m, c0:c0 + n], in0=xT[:, m, c0:c0 + n], in1=ps[:, 0:n],
                                                                   op=ALU.add), reads=[ps, xT], writes=[xT])

            for (c0_, n_) in blocks:
                do_block(c0_, n_)
            P.pe_free = False

        acc = [psf[4], psf[5]]

        def attention_prompt(own, c0, kbase, qT, bT, PT, osb, onb, sm, cnt):
            units = []
            for qb in range(2):
                q0 = qb * 256
                tA = (kbase + c0 + q0) // 128
                jlist = list(range(0, tA + 2))
                for h in range(4):
                    for idx, j in enumerate(jlist):
                        units.append((q0, tA, h, j, idx == 0, idx == len(jlist) - 1))

            def front(u):
                q0, tA, h, j, is_first, is_last = u
                ps = pf()
                for half in range(2):
                    P.op(PE, lambda e, ps=ps, half=half, h=h, j=j, q0=q0: e.matmul(
                        ps[:, half * 256:(half + 1) * 256], lhsT=ktile(j, half, h),
                        rhs=qT[half * 64:(half + 1) * 64, h, q0:q0 + 256], start=True, stop=True),
                        reads=[KTt[j], qT], writes=[ps])
                pt = PT[cnt["pt"] % 2]
                cnt["pt"] += 1
                bias = pbias if (own and j < 16) else zbias
                P.op(ACT, lambda e, ps=ps, pt=pt, bias=bias: e.activation(out=pt[:], in_=ps[:], func=AF.Exp,
                                                                          bias=bias[:, 0:1], scale=SCALE),
                     reads=[ps, bias], writes=[pt])
                for qt in range(2):
                    if j == tA + qt:
                        for half in range(2):
                            o_ = half * 256 + qt * 128
                            P.op(POOL, lambda e, pt=pt, o_=o_: e.tensor_tensor(
                                out=pt[:, o_:o_ + 128], in0=pt[:, o_:o_ + 128], in1=trimb[:], op=ALU.mult),
                                reads=[pt, trimb], writes=[pt])
                return pt

            first = {0: True, 1: True}

            def back(u, pt):
                q0, tA, h, j, is_first, is_last = u
                if is_first:
                    first[0] = first[1] = True
                for qt in range(2):
                    tq = tA + qt
                    if j > tq:
                        continue
                    for half in range(2):
                        a = acc[qt]
                        st_flag = first[qt]
                        first[qt] = False
                        P.op(PE, lambda e, a=a, pt=pt, half=half, qt=qt, h=h, j=j, st_flag=st_flag: e.matmul(
                            a[:, half * 129:(half + 1) * 129], lhsT=pt[:, half * 256 + qt * 128:half * 256 + (qt + 1) * 128],
                            rhs=vtile(j, h), start=st_flag, stop=False, skip_group_check=True),
                            reads=[pt, V1t[j]], writes=[a])
                if is_last:
                    for qt in range(2):
                        attn_epilogue(acc[qt], 128, bT, h, q0 + qt * 128, osb, onb, sm)

            nxt = front(units[0])
            for ui, u in enumerate(units):
                cur = nxt
                if ui + 1 < len(units):
                    nxt = front(units[ui + 1])
                back(u, cur)

        def attn_epilogue(a, r, bT, h, col0, osb, onb, sm):
            P.op(DVE, lambda e: e.reciprocal(out=sm[0:r, 0:1], in_=a[0:r, 128:129]), reads=[a], writes=[sm])
            P.op(DVE, lambda e: e.reciprocal(out=sm[0:r, 1:2], in_=a[0:r, 257:258]), reads=[a], writes=[sm])
            P.op(DVE, lambda e: e.tensor_tensor(out=sm[0:r, 1:2], in0=sm[0:r, 1:2], in1=lamneg[0:r, 0:1], op=ALU.mult),
                 reads=[sm, lamneg], writes=[sm])
            P.op(DVE, lambda e: e.tensor_scalar_mul(out=osb[0:r, :], in0=a[0:r, 0:128], scalar1=sm[0:r, 0:1]),
                 reads=[a, sm], writes=[osb])
            P.op(DVE, lambda e: e.scalar_tensor_tensor(out=osb[0:r, :], in0=a[0:r, 129:257], scalar=sm[0:r, 1:2], in1=osb[0:r, :],
                                                       op0=ALU.mult, op1=ALU.add), reads=[a, sm, osb], writes=[osb])
            P.op(ACT, lambda e: e.activation(out=onb[0:r, :], in_=osb[0:r, :], func=AF.Square, accum_out=sm[0:r, 2:3]),
                 reads=[osb], writes=[onb, sm])
            P.op(ACT, lambda e: e.activation(out=sm[0:r, 3:4], in_=sm[0:r, 2:3], func=AF.Sqrt, bias=epsT[0:r, 0:1], scale=1.0 / 128),
                 reads=[sm, epsT], writes=[sm])
            P.op(DVE, lambda e: e.reciprocal(out=sm[0:r, 4:5], in_=sm[0:r, 3:4]), reads=[sm], writes=[sm])
            P.op(DVE, lambda e: e.scalar_tensor_tensor(out=onb[0:r, :], in0=osb[0:r, :], scalar=sm[0:r, 4:5], in1=sublnb[0:r, :],
                                                       op0=ALU.mult, op1=ALU.mult), reads=[osb, sm, sublnb], writes=[onb])
            pb_ = pbk()
            P.op(PE, lambda e: e.transpose(out=pb_[:, 0:r], in_=onb[0:r, :], identity=identb[0:r, 0:r]),
                 reads=[onb, identb], writes=[pb_])
            P.op(ACT, lambda e: e.copy(out=bT[:, h, col0:col0 + r], in_=pb_[:, 0:r]), reads=[pb_], writes=[bT])

        KTs = RC.alloc("KTs", [4, 32], BF16)
        KTs_t = KTs
        V1s = RC.alloc("V1s", [4, 130], BF16)
        V1s_t = V1s
        P.op(POOL, lambda e: e.memset(V1s[:, :, 128:129], 1.0), writes=[V1s])

        def attention_sample(qT, bT, osb, onb, sm):
            P.mark('sample attn')
            P.barrier_all()
            R2.reset()
            pidx_i = R2.alloc("pidx_i", [4, NPAGE], I32)
            pidx_f = R2.alloc("pidx_f", [4, NPAGE], F32)
            iot = R2.alloc("iot", [NPAGE], F32)
            pidx = R2.alloc("pidx", [4, NPAGE], I32)
            NKB, NVB = 8, 16
            kpg = [R2.alloc(f"kpg{i}", [512], F32) for i in range(NKB)]
            vpg = [R2.alloc(f"vpg{i}", [512], F32) for i in range(NVB)]
            kpb2 = [R2.alloc(f"kpb{i}", [512], BF16) for i in range(2)]
            kTp2 = [R2.alloc(f"kTp{i}", [4, 128], BF16) for i in range(2)]
            PTn = R2.alloc("PTn", [64], BF16)
            vpb = [R2.alloc(f"vpb{i}", [4, 130], BF16) for i in range(2)]
            PTs = [R2.alloc(f"PTs{i}", [512], BF16) for i in range(2)]
            qbd = R2.alloc("qbd", [4, 4, 16], BF16)
            smask = R2.alloc("smask", [4, 8], BF16)
            est = R2.alloc("est", [258], F32)
            mk = R2.alloc("mk", [96], F32)
            mk2 = R2.alloc("mk2", [96], F32)
            for vb_ in vpb:
                P.op(POOL, lambda e, vb_=vb_: e.memset(vb_[:, :, 128:129], 1.0), writes=[vb_])
            P.dma(SP, lambda e: e.dma_start(out=pidx_i[:], in_=ptab.rearrange("(o s) n -> o s n", o=1).partition_broadcast(128)
                                            .rearrange("p o s n -> p (o s) n")), writes=[pidx_i])
            P.op(POOL, lambda e: e.iota(iot[:], pattern=[[0, NPAGE]], base=0, channel_multiplier=1, allow_small_or_imprecise_dtypes=True),
                 writes=[iot])
            P.op(DVE, lambda e: e.tensor_copy(out=pidx_f[:], in_=pidx_i[:]), reads=[pidx_i], writes=[pidx_f])
            for s_ in range(4):
                P.op(DVE, lambda e, s_=s_: e.scalar_tensor_tensor(out=pidx_f[:, s_, :], in0=pidx_f[:, s_, :], scalar=128.0, in1=iot[:],
                                                                  op0=ALU.mult, op1=ALU.add), reads=[pidx_f, iot], writes=[pidx_f])
            P.op(DVE, lambda e: e.tensor_copy(out=pidx[:], in_=pidx_f[:]), reads=[pidx_f], writes=[pidx])
            P.op(POOL, lambda e: e.iota(mk[0:32, 0:32], pattern=[[8, 4], [1, 8]], base=0, channel_multiplier=-1,
                                        allow_small_or_imprecise_dtypes=True), writes=[mk])
            P.op(POOL, lambda e: e.iota(mk[0:32, 32:64], pattern=[[8, 4], [0, 8]], base=7, channel_multiplier=-1,
                                        allow_small_or_imprecise_dtypes=True), writes=[mk])
            P.op(POOL, lambda e: e.iota(mk[0:32, 64:96], pattern=[[-8, 4], [0, 8]], base=0, channel_multiplier=1,
                                        allow_small_or_imprecise_dtypes=True), writes=[mk])
            P.op(DVE, lambda e: e.tensor_single_scalar(out=mk2[0:32, :], in_=mk[0:32, :], scalar=0.0, op=ALU.is_ge),
                 reads=[mk], writes=[mk2])
            P.op(DVE, lambda e: e.tensor_tensor(out=mk2[0:32, 0:32], in0=mk2[0:32, 0:32], in1=mk2[0:32, 32:64], op=ALU.mult),
                 reads=[mk2], writes=[mk2])
            P.op(DVE, lambda e: e.tensor_tensor(out=smask[0:32].rearrange("p s q -> p (s q)"), in0=mk2[0:32, 0:32], in1=mk2[0:32, 64:96],
                                                op=ALU.mult), reads=[mk2], writes=[smask])
            P.op(POOL, lambda e: e.memset(qbd[:], 0.0), writes=[qbd])
            for h in range(4):
                P.op(DVE, lambda e, h=h: e.tensor_copy(out=qbd[0:64, h, :, 0:8], in_=qT[0:64, h, 0:32].rearrange("p (s q) -> p s q", s=4)),
                     reads=[qT], writes=[qbd])
                P.op(DVE, lambda e, h=h: e.tensor_copy(out=qbd[64:128, h, :, 8:16], in_=qT[64:128, h, 0:32].rearrange("p (s q) -> p s q", s=4)),
                     reads=[qT], writes=[qbd])
            a, a2, a3 = acc[0], acc[1], psf[3]
            rot["n"] = 3

            def region(h, half):
                if h < 3:
                    return (a if half == 0 else a2), (a if half == 0 else a2)[0:8, h * 129:(h + 1) * 129]
                return a3, a3[0:8, half * 129:(half + 1) * 129]

            steps = [(s_, grp) for s_ in range(4) for grp in range(NPAGE // 8)]

            def gathers(si, which):
                s_, grp = steps[si]
                for jj in range(8):
                    j = grp * 8 + jj
                    pi = si * 8 + jj
                    kp, vp = kpg[pi % NKB], vpg[pi % NVB]
                    if which == "k":
                        P.dma(POOL, lambda e, kp=kp, s_=s_, j=j: e.indirect_dma_start(
                            out=kp[:], out_offset=None, in_=cache_k,
                            in_offset=bass.IndirectOffsetOnAxis(ap=pidx[:, s_, j:j + 1], axis=0)), reads=[pidx], writes=[kp])
                    else:
                        P.dma(POOL, lambda e, vp=vp, s_=s_, j=j: e.indirect_dma_start(
                            out=vp[:], out_offset=None, in_=cache_v,
                            in_offset=bass.IndirectOffsetOnAxis(ap=pidx[:, s_, j:j + 1], axis=0)), reads=[pidx], writes=[vp])

            def front(si):
                s_, grp = steps[si]
                sc = pf()
                for jj in range(8):
                    pi = si * 8 + jj
                    kp = kpg[pi % NKB]
                    kpb, kTp = kpb2[pi % 2], kTp2[pi % 2]
                    P.op(ACT, lambda e, kp=kp, kpb=kpb: e.copy(out=kpb[:], in_=kp[:]), reads=[kp], writes=[kpb])
                    pb_ = pbk()
                    for h in range(4):
                        P.op(PE, lambda e, pb_=pb_, h=h, kpb=kpb: e.transpose(out=pb_[:, h * 128:(h + 1) * 128], in_=kpb[:, h * 128:(h + 1) * 128],
                                                                              identity=identb[:]), reads=[kpb, identb], writes=[pb_])
                    P.op(DVE, lambda e, pb_=pb_, kTp=kTp: e.tensor_copy(out=kTp[:], in_=pb_[:, 0:512].rearrange("p (h t) -> p h t", h=4)),
                         reads=[pb_], writes=[kTp])
                    for h in range(4):
                        P.op(PE, lambda e, sc=sc, h=h, jj=jj, s_=s_, kTp=kTp: e.matmul(
                            sc[:, jj * 64 + h * 16:jj * 64 + (h + 1) * 16], lhsT=kTp[:, h, :], rhs=qbd[:, h, s_, :], start=True, stop=True),
                            reads=[kTp, qbd], writes=[sc])
                if si + 1 < len(steps):
                    gathers(si + 1, "k")
                pt = PTs[si % 2]
                P.op(ACT, lambda e, sc=sc, pt=pt: e.activation(out=pt[:], in_=sc[:], func=AF.Exp, scale=SCALE, bias=zbias[:, 0:1]),
                     reads=[sc, zbias], writes=[pt])
                return pt

            def back(si, pt):
                s_, grp = steps[si]
                for jj in range(8):
                    pi = si * 8 + jj
                    vp = vpg[pi % NVB]
                    vb_ = vpb[pi % 2]
                    P.op(POOL, lambda e, vp=vp, vb_=vb_: e.tensor_copy(out=vb_[:, :, 0:128], in_=vp[:].rearrange("p (h d) -> p h d", h=4)),
                         reads=[vp], writes=[vb_])
                    for h in range(4):
                        for half in range(2):
                            tT, tg = region(h, half)
                            st_flag = (grp == 0 and jj == 0) and ((h == 0) or (h == 3 and half == 0))
                            P.op(PE, lambda e, tg=tg, pt=pt, jj=jj, h=h, half=half, st_flag=st_flag, vb_=vb_: e.matmul(
                                tg, lhsT=pt[:, jj * 64 + h * 16 + half * 8:jj * 64 + h * 16 + half * 8 + 8],
                                rhs=vb_[:, h, 0:129], start=st_flag, stop=False, skip_group_check=True),
                                reads=[pt, vb_], writes=[tT])
                if grp < NPAGE // 8 - 1:
                    return
                sc = pf()
                for h in range(4):
                    P.op(PE, lambda e, sc=sc, h=h, s_=s_: e.matmul(sc[0:32, h * 16:(h + 1) * 16], lhsT=KTs[:, h, :], rhs=qbd[:, h, s_, :],
                                                                   start=True, stop=True), reads=[KTs_t, qbd], writes=[sc])
                P.op(ACT, lambda e, sc=sc: e.activation(out=PTn[0:32, 0:64], in_=sc[0:32, 0:64], func=AF.Exp, scale=SCALE,
                                                        bias=zbias[0:32, 0:1]), reads=[sc, zbias], writes=[PTn])
                P.op(DVE, lambda e, s_=s_: e.tensor_tensor(
                    out=PTn[0:32, 0:64].rearrange("p (a q) -> p a q", q=8), in0=PTn[0:32, 0:64].rearrange("p (a q) -> p a q", q=8),
                    in1=smask[0:32, s_, :].unsqueeze(1).to_broadcast([32, 8, 8]), op=ALU.mult), reads=[PTn, smask], writes=[PTn])
                for h in range(4):
                    for half in range(2):
                        tT, tg = region(h, half)
                        P.op(PE, lambda e, tg=tg, h=h, half=half: e.matmul(
                            tg, lhsT=PTn[0:32, h * 16 + half * 8:h * 16 + half * 8 + 8], rhs=V1s[0:32, h, 0:129], start=False, stop=True,
                            skip_group_check=True), reads=[PTn, V1s_t], writes=[tT])
                for h in range(3):
                    cs = h * 129
                    P.op(DVE, lambda e, cs=cs: e.tensor_copy(out=est[0:8, 0:129], in_=a[0:8, cs:cs + 129]), reads=[a], writes=[est])
                    P.op(DVE, lambda e, cs=cs: e.tensor_copy(out=est[0:8, 129:258], in_=a2[0:8, cs:cs + 129]), reads=[a2], writes=[est])
                    attn_epilogue(est, 8, bT, h, s_ * 8, osb, onb, sm)
                attn_epilogue(a3, 8, bT, 3, s_ * 8, osb, onb, sm)

            gathers(0, "k")
            gathers(0, "v")
            nxt = front(0)
            for si in range(len(steps)):
                cur = nxt
                if si + 1 < len(steps):
                    gathers(si + 1, "v")
                    nxt = front(si + 1)
                back(si, cur)
            rot["n"] = 4
            P.mark('sample outproj')

        def ffn(layer, ncols, actT, G, stage_base):
            P.barrier_all()
            R3.reset()
            RW.reset()
            hT = R3.alloc("hTall", [8, NTOT], BF16)
            sq = RW.alloc("sq", [512], BF16)
            rstd = RW.alloc("rstd", [512], F32)
            sg = [RW.alloc(f"sg{i}", [512], BF16) for i in range(2)]
            nr2 = min(4, (R2.cap - stage_base) // 8192)
            w1t = [T(R2.base[:, (stage_base + i * 8192) // 4:(stage_base + (i + 1) * 8192) // 4].bitcast(BF16)
                     .rearrange("p (k c) -> p k c", k=8), f"w1t{i}") for i in range(nr2)]
            w1t += [RW.alloc(f"w1t{i}", [8, 512], BF16) for i in range(nr2, 4)]
            w2 = [RW.alloc(f"w2{i}", [G, 512], BF16) for i in range(2)]
            blocks = [(c, min(512, ncols - c)) for c in range(0, ncols, 512)]
            for (c0, n) in blocks:
                rmsnorm_cols(xT, c0, n, gffn[:, layer, :], hT, c0, sq, rstd)
            w1v = w_f1[layer].rearrange("(k p) c -> p k c", p=128)
            w2v = w_f2[layer].rearrange("(j p) c -> p j c", p=128)
            groups = [(j0, min(G, 22 - j0)) for j0 in range(0, 22, G)]
            chunks = []
            for gi, (j0, gsz) in enumerate(groups):
                qs = list(range(0, gsz, 4))
                for q0 in qs:
                    chunks.append((gi, j0, gsz, q0, min(4, gsz - q0), q0 == qs[0], q0 == qs[-1]))

            def load_chunk(idx):
                if FLAGS.get('noffndma'):
                    return
                gi, j0, gsz, q0, nq, _, _ = chunks[idx]
                wgt, wut = w1t[2 * (idx % 2)], w1t[2 * (idx % 2) + 1]
                col, w = (j0 + q0) * 128, nq * 128
                for hh in range(2):
                    P.dma(POOL, lambda e, wgt=wgt, hh=hh, col=col, w=w: e.dma_start(
                        out=wgt[:, hh * 4:hh * 4 + 4, 0:w], in_=w1v[:, hh * 4:hh * 4 + 4, col:col + w]), writes=[wgt])
                    P.dma(POOL, lambda e, wut=wut, hh=hh, col=col, w=w: e.dma_start(
                        out=wut[:, hh * 4:hh * 4 + 4, 0:w], in_=w1v[:, hh * 4:hh * 4 + 4, DFF + col:DFF + col + w]), writes=[wut])

            def load_w2(j0, gsz):
                if FLAGS.get('noffndma'):
                    return
                for m4 in range(2):
                    w2t = w2[m4]
                    for a0 in range(0, gsz, 4):
                        a1 = min(gsz, a0 + 4)
                        P.dma(POOL, lambda e, w2t=w2t, m4=m4, a0=a0, a1=a1, j0=j0: e.dma_start(
                            out=w2t[:, a0:a1, :], in_=w2v[:, j0 + a0:j0 + a1, m4 * 512:(m4 + 1) * 512]), writes=[w2t])

            P.mark(f'ffn{layer}_{ncols}_w1')
            P.pe_free = FLAGS.get('pe_free', True)
            load_chunk(0)
            for idx, (gi, j0, gsz, q0, nq, g_first, g_last) in enumerate(chunks):
                if g_first:
                    load_w2(j0, gsz)
                if idx + 1 < len(chunks):
                    load_chunk(idx + 1)
                wgt, wut = w1t[2 * (idx % 2)], w1t[2 * (idx % 2) + 1]
                for jq in range(nq):
                    jj = q0 + jq
                    for bi, (c0, n) in enumerate(blocks):
                        pg_ = pf()
                        pu_ = pf()
                        for k in range(8):
                            P.op(PE, lambda e, pg_=pg_, wgt=wgt, k=k, c0=c0, n=n, jq=jq: e.matmul(
                                pg_[:, 0:n], lhsT=wgt[:, k, jq * 128:(jq + 1) * 128], rhs=hT[:, k, c0:c0 + n],
                                start=(k == 0), stop=(k == 7)), reads=[wgt, hT], writes=[pg_])
                        for k in range(8):
                            P.op(PE, lambda e, pu_=pu_, wut=wut, k=k, c0=c0, n=n, jq=jq: e.matmul(
                                pu_[:, 0:n], lhsT=wut[:, k, jq * 128:(jq + 1) * 128], rhs=hT[:, k, c0:c0 + n],
                                start=(k == 0), stop=(k == 7)), reads=[wut, hT], writes=[pu_])
                        sgt = sg[bi % 2]
                        P.op(ACT, lambda e, pg_=pg_, sgt=sgt, n=n: e.activation(out=sgt[:, 0:n], in_=pg_[:, 0:n], func=AF.Silu),
                             reads=[pg_], writes=[sgt])
                        P.op(DVE, lambda e, pu_=pu_, sgt=sgt, jj=jj, c0=c0, n=n: e.tensor_tensor(
                            out=actT[:, jj, c0:c0 + n], in0=sgt[:, 0:n], in1=pu_[:, 0:n], op=ALU.mult), reads=[pu_, sgt], writes=[actT])
                if not g_last:
                    continue
                P.mark(f'ffn{layer}_{ncols}_w2_{gi}')
                for m in range(8):
                    w2t = w2[m // 4]
                    mm = m % 4
                    for (c0, n) in blocks:
                        ps = pf()
                        for jj in range(gsz):
                            P.op(PE, lambda e, ps=ps, w2t=w2t, jj=jj, c0=c0, n=n, gsz=gsz, mm=mm: e.matmul(
                                ps[:, 0:n], lhsT=w2t[:, jj, mm * 128:(mm + 1) * 128], rhs=actT[:, jj, c0:c0 + n],
                                start=(jj == 0), stop=(jj == gsz - 1)), reads=[w2t, actT], writes=[ps])
                        P.op(DVE, lambda e, ps=ps, m=m, c0=c0, n=n: e.tensor_tensor(out=xT[:, m, c0:c0 + n], in0=xT[:, m, c0:c0 + n], in1=ps[:, 0:n],
                                                                                 op=ALU.add), reads=[ps, xT], writes=[xT])
                P.mark(f'ffn{layer}_{ncols}_w1b_{gi}')
            P.pe_free = False


        Sst = RC.alloc("Sst", [64], F32)
        Ssw = RC.alloc("Ssw", [64], F32)
        A1 = RC.alloc("A1", [64], F32)
        A2 = RC.alloc("A2", [64], F32)
        sgn = RC.alloc("sgn", [1], F32)
        mtop = RC.alloc("mtop", [1], F32)
        nmbot = RC.alloc("nmbot", [1], F32)
        negpi = RC.alloc("negpi", [1], F32)
        P.op(POOL, lambda e: e.iota(tmpc[:, 0:1], pattern=[[0, 1]], base=-64, channel_multiplier=1,
                                    allow_small_or_imprecise_dtypes=True), writes=[tmpc])
        P.op(DVE, lambda e: e.tensor_single_scalar(out=nmbot[:], in_=tmpc[:, 0:1], scalar=0.0, op=ALU.is_ge), reads=[tmpc], writes=[nmbot])
        P.op(DVE, lambda e: e.tensor_scalar(out=sgn[:], in0=nmbot[:], scalar1=2.0, scalar2=-1.0, op0=ALU.mult, op1=ALU.add),
             reads=[nmbot], writes=[sgn])
        P.op(DVE, lambda e: e.tensor_scalar(out=mtop[:], in0=nmbot[:], scalar1=-1.0, scalar2=1.0, op0=ALU.mult, op1=ALU.add),
             reads=[nmbot], writes=[mtop])
        P.op(DVE, lambda e: e.tensor_scalar_mul(out=nmbot[:], in0=nmbot[:], scalar1=-1.0), reads=[nmbot], writes=[nmbot])
        P.op(DVE, lambda e: e.memset(negpi[:], -math.pi), writes=[negpi])

        TWO_PI = 2.0 * math.pi

        def ssm_tables(full):
            R2.reset()
            RW.reset()
            if not full:
                R2.off = 33024
            XB = R2.alloc("XB", [64, 128], BF16)
            Wz = R2.alloc("Wz", [64, 128], BF16)
            if full:
                Tt = R2.alloc("Tt", [64, 128], BF16)
                YC = R2.alloc("YC", [64, 128], BF16)
                tflat = R2.base[:, 32768 // 4:(32768 + 16384) // 4]
                inA = T(tflat[:, 0:1024].rearrange("p (g c) -> p g c", g=64), "inA")
                inB = T(tflat[:, 1024:2048].rearrange("p (g c) -> p g c", g=64), "inB")
                parts = [(0, 64)]
            else:
                Tt = YC = None
                inA = RW.alloc("inA", [32, 16], F32)
                inB = RW.alloc("inB", [32, 16], F32)
                parts = [(0, 32), (32, 64)]
            tb = [RW.alloc(f"tb{i}", [64, 8], F32) for i in range(10)]
            sm_ = [RW.alloc(f"ts{i}", [64], F32) for i in range(8)]
            kidx = RW.alloc("kidx", [8], F32)
            if full:
                tmask4 = RW.alloc("tmask4", [4, 128], F32)
                dsk = RW.alloc("dsk", [64], F32)
                tmpf = RW.alloc("tmpf", [512], F32)
            aR, aI = sm_[6], sm_[7]
            P.dma(SP, lambda e: e.dma_start(out=aR[:], in_=a_re), writes=[aR])
            P.dma(SP, lambda e: e.dma_start(out=aI[:], in_=a_im), writes=[aI])
            P.dma(SP, lambda e: e.dma_start(out=sm_[0][:], in_=logdt), writes=[sm_[0]])
            if full:
                P.dma(SP, lambda e: e.dma_start(out=dsk[:], in_=d_skip), writes=[dsk])
            P.op(ACT, lambda e: e.activation(out=sm_[0][:], in_=sm_[0][:], func=AF.Exp), reads=[sm_[0]], writes=[sm_[0]])
            P.op(DVE, lambda e: e.tensor_tensor(out=sm_[1][:], in0=aR[:], in1=sm_[0][:], op=ALU.mult), reads=[aR, sm_[0]], writes=[sm_[1]])
            P.op(DVE, lambda e: e.tensor_tensor(out=sm_[2][:], in0=aI[:], in1=sm_[0][:], op=ALU.mult), reads=[aI, sm_[0]], writes=[sm_[2]])
            P.op(POOL, lambda e: e.iota(kidx[:], pattern=[[1, 8]], base=1, channel_multiplier=0, allow_small_or_imprecise_dtypes=True), writes=[kidx])
            kb = lambda: kidx[:].unsqueeze(1).broadcast_to([128, 64, 8])
            gb = lambda t: t[:].unsqueeze(2).broadcast_to([128, 64, 8])
            P.op(DVE, lambda e: e.tensor_tensor(out=tb[0][:], in0=gb(sm_[1]), in1=kb(), op=ALU.mult), reads=[sm_[1], kidx], writes=[tb[0]])
            P.op(ACT, lambda e: e.activation(out=tb[1][:], in_=tb[0][:], func=AF.Exp), reads=[tb[0]], writes=[tb[1]])
            P.op(ACT, lambda e: e.activation(out=tb[2][:], in_=tb[0][:], func=AF.Exp, scale=-1.0), reads=[tb[0]], writes=[tb[2]])
            P.op(DVE, lambda e: e.tensor_tensor(out=tb[3][:], in0=gb(sm_[2]), in1=kb(), op=ALU.mult), reads=[sm_[2], kidx], writes=[tb[3]])
            def sincos(dst, off):
                yi = T(tb[9].ap.bitcast(I32), "yi")
                P.op(DVE, lambda e: e.tensor_scalar(out=tb[0][:], in0=tb[3][:], scalar1=1.0 / TWO_PI, scalar2=off, op0=ALU.mult, op1=ALU.add),
                     reads=[tb[3]], writes=[tb[0]])
                P.op(DVE, lambda e: e.tensor_copy(out=yi[:], in_=tb[0][:]), reads=[tb[0]], writes=[yi, tb[9]])
                P.op(DVE, lambda e: e.tensor_copy(out=tb[8][:], in_=yi[:]), reads=[yi, tb[9]], writes=[tb[8]])
                P.op(DVE, lambda e: e.tensor_tensor(out=tb[0][:], in0=tb[0][:], in1=tb[8][:], op=ALU.subtract), reads=[tb[0], tb[8]], writes=[tb[0]])
                P.op(DVE, lambda e: e.tensor_single_scalar(out=tb[8][:], in_=tb[0][:], scalar=0.5, op=ALU.is_gt), reads=[tb[0]], writes=[tb[8]])
                P.op(DVE, lambda e: e.tensor_tensor(out=tb[0][:], in0=tb[0][:], in1=tb[8][:], op=ALU.subtract), reads=[tb[0], tb[8]], writes=[tb[0]])
                P.op(DVE, lambda e: e.tensor_single_scalar(out=tb[8][:], in_=tb[0][:], scalar=-0.5, op=ALU.is_lt), reads=[tb[0]], writes=[tb[8]])
                P.op(DVE, lambda e: e.tensor_tensor(out=tb[0][:], in0=tb[0][:], in1=tb[8][:], op=ALU.add), reads=[tb[0], tb[8]], writes=[tb[0]])
                P.op(ACT, lambda e: e.activation(out=dst[:], in_=tb[0][:], func=AF.Sin, scale=TWO_PI), reads=[tb[0]], writes=[dst])

            sincos(tb[4], 0.0)
            sincos(tb[5], 0.25)
            PWre, PWim, MWre, MWim = tb[6], tb[7], tb[8], tb[9]
            P.op(DVE, lambda e: e.tensor_tensor(out=PWre[:], in0=tb[1][:], in1=tb[5][:], op=ALU.mult), reads=[tb[1], tb[5]], writes=[PWre])
            P.op(DVE, lambda e: e.tensor_tensor(out=PWim[:], in0=tb[1][:], in1=tb[4][:], op=ALU.mult), reads=[tb[1], tb[4]], writes=[PWim])
            P.op(DVE, lambda e: e.tensor_tensor(out=MWre[:], in0=tb[2][:], in1=tb[5][:], op=ALU.mult), reads=[tb[2], tb[5]], writes=[MWre])
            P.op(DVE, lambda e: e.scalar_tensor_tensor(out=MWim[:], in0=tb[2][:], scalar=-1.0, in1=tb[4][:], op0=ALU.mult, op1=ALU.mult), reads=[tb[2], tb[4]], writes=[MWim])
            P.op(DVE, lambda e: e.tensor_copy(out=A1[:], in_=PWre[:, :, 7]), reads=[PWre], writes=[A1])
            P.op(DVE, lambda e: e.tensor_scalar_mul(out=A2[:], in0=PWim[:, :, 7], scalar1=sgn[:, 0:1]), reads=[PWim, sgn], writes=[A2])
            nr, den, qre, qim, t_ = sm_[3], sm_[4], sm_[5], sm_[0], sm_[1]
            P.op(DVE, lambda e: e.tensor_scalar_add(out=nr[:], in0=PWre[:, :, 0], scalar1=-1.0), reads=[PWre], writes=[nr])
            P.op(DVE, lambda e: e.tensor_tensor(out=den[:], in0=aR[:], in1=aR[:], op=ALU.mult), reads=[aR], writes=[den])
            P.op(DVE, lambda e: e.tensor_tensor(out=t_[:], in0=aI[:], in1=aI[:], op=ALU.mult), reads=[aI], writes=[t_])
            P.op(DVE, lambda e: e.tensor_tensor(out=den[:], in0=den[:], in1=t_[:], op=ALU.add), reads=[den, t_], writes=[den])
            P.op(DVE, lambda e: e.reciprocal(out=den[:], in_=den[:]), reads=[den], writes=[den])
            P.op(DVE, lambda e: e.tensor_tensor(out=qre[:], in0=nr[:], in1=aR[:], op=ALU.mult), reads=[nr, aR], writes=[qre])
            P.op(DVE, lambda e: e.tensor_tensor(out=t_[:], in0=PWim[:, :, 0], in1=aI[:], op=ALU.mult), reads=[PWim, aI], writes=[t_])
            P.op(DVE, lambda e: e.tensor_tensor(out=qre[:], in0=qre[:], in1=t_[:], op=ALU.add), reads=[qre, t_], writes=[qre])
            P.op(DVE, lambda e: e.tensor_tensor(out=qre[:], in0=qre[:], in1=den[:], op=ALU.mult), reads=[qre, den], writes=[qre])
            P.op(DVE, lambda e: e.tensor_tensor(out=qim[:], in0=PWim[:, :, 0], in1=aR[:], op=ALU.mult), reads=[PWim, aR], writes=[qim])
            P.op(DVE, lambda e: e.tensor_tensor(out=t_[:], in0=nr[:], in1=aI[:], op=ALU.mult), reads=[nr, aI], writes=[t_])
            P.op(DVE, lambda e: e.tensor_tensor(out=qim[:], in0=qim[:], in1=t_[:], op=ALU.subtract), reads=[qim, t_], writes=[qim])
            P.op(DVE, lambda e: e.tensor_tensor(out=qim[:], in0=qim[:], in1=den[:], op=ALU.mult), reads=[qim, den], writes=[qim])
            W1re, W1im = tb[0], tb[1]
            P.op(DVE, lambda e: e.tensor_tensor(out=W1re[:], in0=MWre[:], in1=gb(qre), op=ALU.mult), reads=[MWre, qre], writes=[W1re])
            P.op(DVE, lambda e: e.tensor_tensor(out=tb[2][:], in0=MWim[:], in1=gb(qim), op=ALU.mult), reads=[MWim, qim], writes=[tb[2]])
            P.op(DVE, lambda e: e.tensor_tensor(out=W1re[:], in0=W1re[:], in1=tb[2][:], op=ALU.subtract), reads=[W1re, tb[2]], writes=[W1re])
            P.op(DVE, lambda e: e.tensor_tensor(out=W1im[:], in0=MWim[:], in1=gb(qre), op=ALU.mult), reads=[MWim, qre], writes=[W1im])
            P.op(DVE, lambda e: e.tensor_tensor(out=tb[2][:], in0=MWre[:], in1=gb(qim), op=ALU.mult), reads=[MWre, qim], writes=[tb[2]])
            P.op(DVE, lambda e: e.tensor_tensor(out=W1im[:], in0=W1im[:], in1=tb[2][:], op=ALU.add), reads=[W1im, tb[2]], writes=[W1im])
            P.op(DVE, lambda e: e.tensor_scalar_mul(out=W1im[:], in0=W1im[:], scalar1=sgn[:, 0:1]), reads=[W1im, sgn], writes=[W1im])
            for (g0, g1) in parts:
                ng = g1 - g0
                P.dma(SP, lambda e, g0=g0, g1=g1: e.dma_start(out=inA[:], in_=b_A[:, g0:g1, :]), writes=[inA])
                P.dma(SP, lambda e, g0=g0, g1=g1: e.dma_start(out=inB[:], in_=b_B[:, g0:g1, :]), writes=[inB])
                xb4 = lambda t, g0=g0, g1=g1: t[:, g0:g1, :].rearrange("p g (k c) -> p g k c", k=8)
                wb4 = lambda t, g0=g0, g1=g1, ng=ng: t[:, g0:g1, :].unsqueeze(3).broadcast_to([128, ng, 8, 16])
                ib4 = lambda t, ng=ng: t[:].unsqueeze(2).broadcast_to([128, ng, 8, 16])
                P.op(DVE, lambda e, xb4=xb4, wb4=wb4, ib4=ib4: e.tensor_tensor(out=xb4(XB), in0=wb4(W1re), in1=ib4(inA), op=ALU.mult), reads=[W1re, inA], writes=[XB])
                P.op(POOL, lambda e, xb4=xb4, wb4=wb4, ib4=ib4: e.tensor_tensor(out=xb4(Wz), in0=wb4(W1im), in1=ib4(inB), op=ALU.mult), reads=[W1im, inB], writes=[Wz])
            P.op(DVE, lambda e: e.tensor_tensor(out=XB[:], in0=XB[:], in1=Wz[:], op=ALU.add), reads=[XB, Wz], writes=[XB])
            xb4 = lambda t: t[:].rearrange("p g (k c) -> p g k c", k=8)
            wb4 = lambda t: t[:].unsqueeze(3).broadcast_to([128, 64, 8, 16])
            ib4 = lambda t: t[:].unsqueeze(2).broadcast_to([128, 64, 8, 16])
            if full:
                PA, PB = tb[2], tb[3]
                P.op(DVE, lambda e: e.tensor_scalar_mul(out=PA[:], in0=PWre[:], scalar1=mtop[:, 0:1]), reads=[PWre, mtop], writes=[PA])
                P.op(DVE, lambda e: e.scalar_tensor_tensor(out=PA[:], in0=PWim[:], scalar=nmbot[:, 0:1], in1=PA[:], op0=ALU.mult, op1=ALU.add),
                     reads=[PWim, nmbot, PA], writes=[PA])
                P.op(DVE, lambda e: e.tensor_scalar_mul(out=PB[:], in0=PWre[:], scalar1=nmbot[:, 0:1]), reads=[PWre, nmbot], writes=[PB])
                P.op(DVE, lambda e: e.tensor_scalar(out=tb[4][:], in0=PWim[:], scalar1=mtop[:, 0:1], scalar2=-1.0, op0=ALU.mult, op1=ALU.mult),
                     reads=[PWim, mtop], writes=[tb[4]])
                P.op(DVE, lambda e: e.tensor_tensor(out=PB[:], in0=PB[:], in1=tb[4][:], op=ALU.add), reads=[PB, tb[4]], writes=[PB])
                P.dma(SP, lambda e: e.dma_start(out=inA[:], in_=c_A), writes=[inA])
                P.dma(SP, lambda e: e.dma_start(out=inB[:], in_=c_B), writes=[inB])
                P.op(DVE, lambda e: e.tensor_tensor(out=xb4(YC), in0=wb4(PA), in1=ib4(inA), op=ALU.mult), reads=[PA, inA], writes=[YC])
                P.op(POOL, lambda e: e.tensor_tensor(out=xb4(Wz), in0=wb4(PB), in1=ib4(inB), op=ALU.mult), reads=[PB, inB], writes=[Wz])
                P.op(DVE, lambda e: e.tensor_tensor(out=YC[:], in0=YC[:], in1=Wz[:], op=ALU.add), reads=[YC, Wz], writes=[YC])
                for q4 in range(4):
                    P.op(POOL, lambda e, q4=q4: e.iota(tmask4[:, q4, :], pattern=[[16, 8], [0, 16]], base=15, channel_multiplier=-1,
                                                      allow_small_or_imprecise_dtypes=True), writes=[tmask4])
                P.op(DVE, lambda e: e.tensor_single_scalar(out=tmask4[:], in_=tmask4[:], scalar=0.0, op=ALU.is_ge), reads=[tmask4], writes=[tmask4])
                for g4 in range(16):
                    ps = pf()
                    for gi in range(4):
                        g = g4 * 4 + gi
                        P.op(PE, lambda e, ps=ps, g=g, gi=gi: e.matmul(ps[:, gi * 128:(gi + 1) * 128], lhsT=XB[:, g, :], rhs=YC[:, g, :], start=True, stop=True),
                             reads=[XB, YC], writes=[ps])
                    P.op(DVE, lambda e, ps=ps: e.tensor_tensor(out=tmpf[:], in0=ps[:], in1=tmask4[:].rearrange("p a b -> p (a b)"), op=ALU.mult),
                         reads=[ps, tmask4], writes=[tmpf])
                    for gi in range(4):
                        g = g4 * 4 + gi
                        P.op(DVE, lambda e, g=g, gi=gi: e.scalar_tensor_tensor(out=Tt[:, g, :], in0=identf[:], scalar=dsk[:, g:g + 1],
                                                                              in1=tmpf[:, gi * 128:(gi + 1) * 128], op0=ALU.mult, op1=ALU.add),
                             reads=[identf, dsk, tmpf], writes=[Tt])
            for g8 in range(8):
                pb_ = pbk()
                for gi in range(8):
                    g = g8 * 8 + gi
                    P.op(PE, lambda e, pb_=pb_, g=g, gi=gi: e.transpose(out=pb_[:, gi * 128:(gi + 1) * 128], in_=XB[:, g, :], identity=identb[:]),
                         reads=[XB, identb], writes=[pb_])
                pb2 = pbk()
                for gi in range(8):
                    g = g8 * 8 + gi
                    P.op(PE, lambda e, pb2=pb2, g=g, gi=gi: e.transpose(out=pb2[:, gi * 128:(gi + 1) * 128], in_=XB[:, g, :], identity=identsw[:]),
                         reads=[XB, identsw], writes=[pb2])
                P.op(ACT, lambda e, pb_=pb_, g8=g8: e.copy(out=Wz[:, g8 * 8:(g8 + 1) * 8, :], in_=pb_[:].rearrange("p (g s) -> p g s", g=8)),
                     reads=[pb_], writes=[Wz])
                P.op(ACT, lambda e, pb2=pb2, g8=g8: e.copy(out=XB[:, g8 * 8:(g8 + 1) * 8, :], in_=pb2[:].rearrange("p (g s) -> p g s", g=8)),
                     reads=[pb2], writes=[XB])
            Wz.sw = XB
            return Wz, Tt, YC

        def ssm_blocks(h1T, ncols_prompt, Wz, Tt, YC, full, sample):
            RW.reset()
            NJ = 32
            Zt = RW.alloc("Zt", [64, NJ], BF16)
            Zs = RW.alloc("Zs", [64, NJ], BF16)
            Sall = RW.alloc("Sall", [64, NJ], BF16)
            UJ2 = [RW.alloc(f"UJ{i}", [8, 128], BF16) for i in range(2)]
            U8all = RW.alloc("U8", [64, NJ], BF16)
            U8m = [T(U8all.ap[:, m * 8:(m + 1) * 8, :], f"U8m{m}") for m in range(8)]
            GJ2 = [RW.alloc(f"GJ{i}", [8, 128], BF16) for i in range(2)]
            t1 = RW.alloc("t1", [64], F32)
            t2 = RW.alloc("t2", [64], F32)
            m1 = RW.alloc("m1", [64], F32)
            m2 = RW.alloc("m2", [64], F32)
            stS = RW.alloc("stS", [4, 64], F32)
            stW = RW.alloc("stW", [4, 64], F32)
            so = RW.alloc("so", [64, 4], F32)
            WzS = Wz.sw

            def stageA(c0, nj, m):
                UJ = UJ2[m % 2]
                pb_ = psb[0]
                for tp in range(8):
                    P.op(PE, lambda e, pb_=pb_, tp=tp: e.transpose(
                        out=pb_[0:nj, tp * 128:(tp + 1) * 128],
                        in_=h1T[:, m, c0:c0 + nj * 8].rearrange("p (j t) -> p t j", t=8)[:, tp, :], identity=identb[:]),
                        reads=[h1T, identb], writes=[pb_])
                P.op(ACT, lambda e, pb_=pb_, UJ=UJ: e.copy(out=UJ[0:nj].rearrange("p g (t c) -> p t g c", t=8),
                                                          in_=pb_[0:nj, :].rearrange("p (t g c) -> p t g c", t=8, g=8)), reads=[pb_], writes=[UJ])

            def stageB(nj, m):
                UJ = UJ2[m % 2]
                pb2 = psb[1]
                for gl in range(8):
                    P.op(PE, lambda e, pb2=pb2, gl=gl, UJ=UJ: e.transpose(out=pb2[:, gl * NJ:gl * NJ + nj], in_=UJ[0:nj, gl, :],
                                                                          identity=identb[0:nj, 0:nj]), reads=[UJ, identb], writes=[pb2])
                P.op(DVE, lambda e, pb2=pb2, m=m: e.tensor_copy(out=U8m[m][:, :, 0:nj], in_=pb2[:, 0:8 * NJ].rearrange("p (g j) -> p g j", g=8)[:, :, 0:nj]),
                     reads=[pb2], writes=[U8m[m]])

            def stageC(nj, m):
                ps = pf()
                for gl in range(8):
                    g = m * 8 + gl
                    P.op(PE, lambda e, ps=ps, g=g, gl=gl, m=m: e.matmul(ps[:, gl * NJ:gl * NJ + nj], lhsT=Wz[:, g, :], rhs=U8m[m][:, gl, 0:nj], start=True, stop=True),
                         reads=[Wz, U8m[m]], writes=[ps])
                    P.op(PE, lambda e, ps=ps, g=g, gl=gl, m=m: e.matmul(ps[:, 256 + gl * NJ:256 + gl * NJ + nj], lhsT=WzS[:, g, :], rhs=U8m[m][:, gl, 0:nj],
                                                                        start=True, stop=True), reads=[WzS, U8m[m]], writes=[ps])
                P.op(ACT, lambda e, ps=ps, m=m: e.copy(out=Zt[:, m * 8:(m + 1) * 8, 0:nj],
                                                       in_=ps[:, 0:256].rearrange("p (g j) -> p g j", g=8)[:, :, 0:nj]), reads=[ps], writes=[Zt])
                P.op(ACT, lambda e, ps=ps, m=m: e.copy(out=Zs[:, m * 8:(m + 1) * 8, 0:nj],
                                                       in_=ps[:, 256:512].rearrange("p (g j) -> p g j", g=8)[:, :, 0:nj]), reads=[ps], writes=[Zs])

            def to_state(c0, nj):
                for step in range(10):
                    if step < 8:
                        stageA(c0, nj, step)
                    if 0 <= step - 1 < 8:
                        stageB(nj, step - 1)
                    if 0 <= step - 2 < 8:
                        stageC(nj, step - 2)

            def stageD(nj, m):
                GJ = GJ2[m % 2]
                for hb in range(2):
                    ps = pf()
                    for gi in range(4):
                        gl = hb * 4 + gi
                        g = m * 8 + gl
                        P.op(PE, lambda e, ps=ps, g=g, gl=gl, gi=gi, m=m: e.matmul(ps[0:nj, gi * 128:(gi + 1) * 128], lhsT=U8m[m][:, gl, 0:nj], rhs=Tt[:, g, :],
                                                                                  start=True, stop=False), reads=[U8m[m], Tt], writes=[ps])
                        P.op(PE, lambda e, ps=ps, g=g, gi=gi: e.matmul(ps[0:nj, gi * 128:(gi + 1) * 128], lhsT=Sall[:, g, 0:nj], rhs=YC[:, g, :],
                                                                      start=False, stop=True), reads=[Sall, YC], writes=[ps])
                    P.op(ACT, lambda e, ps=ps, hb=hb, GJ=GJ: e.activation(out=GJ[0:nj].rearrange("p t (g c) -> p g t c", g=8)[:, hb * 4:hb * 4 + 4],
                                                                          in_=ps[0:nj, :].rearrange("p (g t c) -> p g t c", g=4, t=8),
                                                                          func=AF.Gelu_apprx_tanh), reads=[ps], writes=[GJ])

            def stageE(c0, nj, m):
                GJ = GJ2[m % 2]
                pb_ = pbk()
                for tp in range(8):
                    P.op(PE, lambda e, pb_=pb_, tp=tp, GJ=GJ: e.transpose(out=pb_[:, tp * NJ:tp * NJ + nj], in_=GJ[0:nj, tp, :],
                                                                          identity=identb[0:nj, 0:nj]), reads=[GJ, identb], writes=[pb_])
                P.op(DVE, lambda e, pb_=pb_, m=m: e.tensor_copy(out=h1T[:, m, c0:c0 + nj * 8].rearrange("p (j t) -> p t j", t=8),
                                                                in_=pb_[:, 0:8 * NJ].rearrange("p (t j) -> p t j", t=8)[:, :, 0:nj]),
                     reads=[pb_], writes=[h1T])

            def out_stage(c0, nj):
                for step in range(9):
                    if step < 8:
                        stageD(nj, step)
                    if 0 <= step - 1 < 8:
                        stageE(c0, nj, step - 1)

            P.pe_free = FLAGS.get('pe_free_ssm', True)
            for c0 in range(0, ncols_prompt, NJ * 8):
                to_state(c0, NJ)
                for j in range(NJ):
                    if full:
                        P.op(POOL, lambda e, j=j: e.tensor_copy(out=Sall[:, :, j], in_=Sst[:]), reads=[Sst], writes=[Sall])
                    P.op(DVE, lambda e, j=j: e.tensor_tensor(out=t1[:], in0=Sst[:], in1=Zt[:, :, j], op=ALU.add), reads=[Sst, Zt], writes=[t1])
                    P.op(DVE, lambda e, j=j: e.tensor_tensor(out=t2[:], in0=Ssw[:], in1=Zs[:, :, j], op=ALU.add), reads=[Ssw, Zs], writes=[t2])
                    P.op(DVE, lambda e: e.tensor_tensor(out=m1[:], in0=A1[:], in1=t1[:], op=ALU.mult), reads=[A1, t1], writes=[m1])
                    P.op(DVE, lambda e: e.tensor_tensor(out=m2[:], in0=A2[:], in1=t2[:], op=ALU.mult), reads=[A2, t2], writes=[m2])
                    P.op(DVE, lambda e: e.tensor_tensor(out=Sst[:], in0=m1[:], in1=m2[:], op=ALU.add), reads=[m1, m2], writes=[Sst])
                    P.op(DVE, lambda e: e.tensor_tensor(out=m1[:], in0=A1[:], in1=t2[:], op=ALU.mult), reads=[A1, t2], writes=[m1])
                    P.op(DVE, lambda e: e.tensor_tensor(out=m2[:], in0=A2[:], in1=t1[:], op=ALU.mult), reads=[A2, t1], writes=[m2])
                    P.op(DVE, lambda e: e.tensor_tensor(out=Ssw[:], in0=m1[:], in1=m2[:], op=ALU.subtract), reads=[m1, m2], writes=[Ssw])
                if full:
                    out_stage(c0, NJ)
            if full:
                P.dma(SP, lambda e: e.dma_start(out=so_p, in_=Sst[:]), reads=[Sst])
            if sample:
                c0 = NT
                P.dma(SP, lambda e: e.dma_start(out=stS[:], in_=st_s), writes=[stS])
                P.dma(SP, lambda e: e.dma_start(out=stW[:], in_=st_sw), writes=[stW])
                to_state(c0, 4)
                P.op(POOL, lambda e: e.tensor_copy(out=Sall[:, :, 0:4], in_=stS[:].rearrange("p s g -> p g s")), reads=[stS], writes=[Sall])
                out_stage(c0, 4)
                t1v = so[:]
                P.op(DVE, lambda e: e.tensor_tensor(out=so[:], in0=stS[:].rearrange("p s g -> p g s"), in1=Zt[:, :, 0:4], op=ALU.add), reads=[stS, Zt], writes=[so])
                sw = RW.alloc("sw", [64, 4], F32)
                P.op(DVE, lambda e: e.tensor_tensor(out=sw[:], in0=stW[:].rearrange("p s g -> p g s"), in1=Zs[:, :, 0:4], op=ALU.add), reads=[stW, Zs], writes=[sw])
                P.op(DVE, lambda e: e.tensor_tensor(out=so[:], in0=so[:], in1=A1[:].unsqueeze(2).broadcast_to([128, 64, 4]), op=ALU.mult), reads=[so, A1], writes=[so])
                P.op(DVE, lambda e: e.tensor_tensor(out=sw[:], in0=sw[:], in1=A2[:].unsqueeze(2).broadcast_to([128, 64, 4]), op=ALU.mult), reads=[sw, A2], writes=[sw])
                P.op(DVE, lambda e: e.tensor_tensor(out=stS[:].rearrange("p s g -> p g s"), in0=so[:], in1=sw[:], op=ALU.add), reads=[so, sw], writes=[stS])
                P.dma(SP, lambda e: e.dma_start(out=so_s, in_=stS[:]), reads=[stS])
            P.pe_free = False

        def layer1_norm(ncols):
            P.barrier_all()
            R3.reset()
            RW.reset()
            h1T = R3.alloc("h1T", [8, NTOT], BF16)
            sq = RW.alloc("sq", [512], BF16)
            rstd = RW.alloc("rstd", [512], F32)
            for c0 in range(0, ncols, 512):
                n = min(512, ncols - c0)
                rmsnorm_cols(xT, c0, n, gmix[:, 1, :], h1T, c0, sq, rstd)
            return h1T

        def ssm_pred():
            P.mark('ssm_pred start')
            h1T = layer1_norm(NT)
            P.mark('pred norm done')
            P.barrier_all()
            P.op(DVE, lambda e: e.memset(Sst[:], 0.0), writes=[Sst])
            P.op(DVE, lambda e: e.memset(Ssw[:], 0.0), writes=[Ssw])
            Wz, Tt, YC = ssm_tables(False)
            P.mark('pred tables done')
            ssm_blocks(h1T, NT, Wz, Tt, YC, False, False)
            P.mark('pred blocks done')

        def ssm_own():
            h1T = layer1_norm(NTOT)
            P.barrier_all()
            if not FLAGS["pred"]:
                P.op(DVE, lambda e: e.memset(Sst[:], 0.0), writes=[Sst])
                P.op(DVE, lambda e: e.memset(Ssw[:], 0.0), writes=[Ssw])
            P.mark('own ssm tables start')
            Wz, Tt, YC = ssm_tables(True)
            P.mark('own tables done')
            ssm_blocks(h1T, NT, Wz, Tt, YC, True, True)
            P.mark('own blocks done')
            return h1T

        def glu_mix(GT):
            RW.reset()
            R2.reset()
            wab = [R2.alloc(f"wab{i}", [8, 512], BF16) for i in range(4)]
            sig = [RW.alloc(f"sig{i}", [512], F32) for i in range(2)]
            tmp = [RW.alloc(f"gtmp{i}", [512], F32) for i in range(2)]
            wv = w_glu.rearrange("(k p) c -> p k c", p=128)
            blocks = [(c, min(512, NTOT - c)) for c in range(0, NTOT, 512)]

            def load(ch):
                wat, wbt = wab[2 * (ch % 2)], wab[2 * (ch % 2) + 1]
                for hh in range(2):
                    P.dma(POOL, lambda e, wat=wat, hh=hh, ch=ch: e.dma_start(
                        out=wat[:, hh * 4:hh * 4 + 4, :], in_=wv[:, hh * 4:hh * 4 + 4, ch * 512:(ch + 1) * 512]), writes=[wat])
                    P.dma(POOL, lambda e, wbt=wbt, hh=hh, ch=ch: e.dma_start(
                        out=wbt[:, hh * 4:hh * 4 + 4, :], in_=wv[:, hh * 4:hh * 4 + 4, D + ch * 512:D + (ch + 1) * 512]), writes=[wbt])

            load(0)
            load(1)
            P.pe_free = FLAGS.get('pe_free', True)
            for cc in range(8):
                wat, wbt = wab[2 * ((cc // 4) % 2)], wab[2 * ((cc // 4) % 2) + 1]
                cq = cc % 4
                for bi, (c0, n) in enumerate(blocks):
                    pa, pb2 = pf(), pf()
                    for k in range(8):
                        P.op(PE, lambda e, pa=pa, wat=wat, k=k, c0=c0, n=n, cq=cq: e.matmul(
                            pa[:, 0:n], lhsT=wat[:, k, cq * 128:(cq + 1) * 128], rhs=GT[:, k, c0:c0 + n],
                            start=(k == 0), stop=(k == 7)), reads=[wat, GT], writes=[pa])
                    for k in range(8):
                        P.op(PE, lambda e, pb2=pb2, wbt=wbt, k=k, c0=c0, n=n, cq=cq: e.matmul(
                            pb2[:, 0:n], lhsT=wbt[:, k, cq * 128:(cq + 1) * 128], rhs=GT[:, k, c0:c0 + n],
                            start=(k == 0), stop=(k == 7)), reads=[wbt, GT], writes=[pb2])
                    sg_, tp_ = sig[bi % 2], tmp[bi % 2]
                    P.op(ACT, lambda e, pb2=pb2, sg_=sg_, cc=cc, n=n: e.activation(out=sg_[:, 0:n], in_=pb2[:, 0:n], func=AF.Sigmoid,
                                                                                   bias=bglu[:, 8 + cc:9 + cc]), reads=[pb2, bglu], writes=[sg_])
                    P.op(DVE, lambda e, pa=pa, sg_=sg_, tp_=tp_, cc=cc, n=n: e.scalar_tensor_tensor(
                        out=tp_[:, 0:n], in0=pa[:, 0:n], scalar=bglu[:, cc:cc + 1], in1=sg_[:, 0:n], op0=ALU.add, op1=ALU.mult),
                        reads=[pa, sg_, bglu], writes=[tp_])
                    P.op(POOL, lambda e, tp_=tp_, cc=cc, c0=c0, n=n: e.tensor_tensor(out=xT[:, cc, c0:c0 + n], in0=xT[:, cc, c0:c0 + n], in1=tp_[:, 0:n],
                                                                                     op=ALU.add), reads=[tp_, xT], writes=[xT])
            P.pe_free = False

        def final_out():
            P.barrier_all()
            R3.reset()
            RW.reset()
            sq = RW.alloc("sq", [512], BF16)
            rstd = RW.alloc("rstd", [512], F32)
            yT = RW.alloc("yT", [8, 512], F32)
            ys = [RW.alloc(f"ys{i}", [1024], F32) for i in range(2)]
            ps_ss = psf[4]
            cnt_ = 0
            for c0 in range(0, NTOT, 512):
                n = min(512, NTOT - c0)
                for k in range(8):
                    P.op(ACT, lambda e, k=k, c0=c0, n=n: e.activation(out=sq[:, 0:n], in_=xT[:, k, c0:c0 + n], func=AF.Square), reads=[xT], writes=[sq])
                    P.op(PE, lambda e, k=k, n=n: e.matmul(ps_ss[:, 0:n], lhsT=onesb[:], rhs=sq[:, 0:n], start=(k == 0), stop=(k == 7)),
                         reads=[sq, onesb], writes=[ps_ss])
                P.op(ACT, lambda e, n=n: e.activation(out=rstd[:, 0:n], in_=ps_ss[:, 0:n], func=AF.Sqrt, bias=epsT[:, 0:1], scale=1.0 / D),
                     reads=[ps_ss, epsT], writes=[rstd])
                P.op(DVE, lambda e, n=n: e.reciprocal(out=rstd[:, 0:n], in_=rstd[:, 0:n]), reads=[rstd], writes=[rstd])
                for k in range(8):
                    P.op(DVE, lambda e, k=k, c0=c0, n=n: e.scalar_tensor_tensor(out=yT[:, k, 0:n], in0=xT[:, k, c0:c0 + n], scalar=gfin[:, k:k + 1],
                                                                                in1=rstd[:, 0:n], op0=ALU.mult, op1=ALU.mult), reads=[xT, rstd, gfin], writes=[yT])
                for tt in range((n + 127) // 128):
                    r = min(128, n - tt * 128)
                    yst = ys[cnt_ % 2]
                    cnt_ += 1
                    for kk in range(2):
                        ps = pf()
                        for k4 in range(4):
                            k = kk * 4 + k4
                            P.op(PE, lambda e, ps=ps, k=k, k4=k4, tt=tt, r=r: e.transpose(out=ps[0:r, k4 * 128:(k4 + 1) * 128],
                                                                                         in_=yT[:, k, tt * 128:tt * 128 + r], identity=identf[:]),
                                 reads=[yT, identf], writes=[ps])
                        P.op(ACT, lambda e, ps=ps, kk=kk, r=r, yst=yst: e.copy(out=yst[0:r, kk * 512:(kk + 1) * 512], in_=ps[0:r, :]), reads=[ps], writes=[yst])
                    dst = y_s if c0 == NT else y_p[c0 + tt * 128:c0 + tt * 128 + r, :]
                    P.dma(SP, lambda e, yst=yst, r=r, dst=dst: e.dma_start(out=dst, in_=yst[0:r, :]), reads=[yst])

        def act_view(off_bytes, nblk, ncols):
            v = R2.base[:, off_bytes // 4:(off_bytes + nblk * ncols * 2) // 4].bitcast(BF16)
            return T(v.rearrange("p (a b) -> p a b", a=nblk), "actT")

        if FLAGS["pred"]:
            P.mark('pred layer0')
            layer0_mixer("pred")
            if FLAGS.get("ffn"):
                P.mark('pred ffn0')
                ffn(0, NT, act_view(33024, 4, NT), 4, 33024 + 4 * NT * 2)
            if FLAGS.get("ssm"):
                ssm_pred()
        P.mark('own layer0 start')
        layer0_mixer("own")
        P.mark('own layer0 done')
        if FLAGS.get("ffn"):
            P.mark('own ffn0')
            ffn(0, NTOT, act_view(0, 8, NTOT), 8, 8 * NTOT * 2)
            P.mark('own l1norm')
        if FLAGS.get("ssm"):
            GT = ssm_own()
            if FLAGS.get('dbg') == 'GT':
                P.barrier_all()
                for k_ in range(8):
                    P.op(DVE, lambda e, k_=k_: e.tensor_copy(out=xT[:, k_, :], in_=GT[:, k_, :]), reads=[GT], writes=[xT])
            if FLAGS.get("glu"):
                P.barrier_all()
                glu_mix(GT)
                P.mark('glu done')
                if FLAGS.get('stop') != 'x3':
                    ffn(1, NTOT, act_view(0, 8, NTOT), 8, 8 * NTOT * 2)
                    P.mark('ffn1 done')
                    final_out()
        if FLAGS.get('dbg'):
            P.dma(SP, lambda e: e.dma_start(out=dbg_x, in_=xT[:]), reads=[xT])
        P.barrier_all()
        print('recorded ops', P.n)
        P.emit()
    return nc


_NC_CACHE = {}


def _prep(x_prompt, x_sample, cache_k, cache_v, page_table, state_ssm_re, state_ssm_im,
          norm_mix, norm_ffn, norm_final, w_in_even, w_out_even, sgu_norm, sgu_w, sgu_b,
          lambda_q1, lambda_k1, lambda_q2, lambda_k2, attn_subln,
          ssm_a_re, ssm_a_im, ssm_log_dt, ssm_b_re, ssm_b_im, ssm_c_re, ssm_c_im, ssm_d,
          w_glu, b_glu, w_ffn_in, w_ffn_out):
    f = lambda a: np.ascontiguousarray(np.asarray(a))
    x_prompt, x_sample = f(x_prompt), f(x_sample)
    ck = f(cache_k).reshape(-1, 512)
    cv = f(cache_v).reshape(-1, 512)

    def gam(a):
        a = f(a)
        return np.ascontiguousarray(a.reshape(a.shape[0], 8, 128).transpose(0, 2, 1))

    sgu_w = f(sgu_w)[0]
    sgu_wT = np.ascontiguousarray(sgu_w.transpose(0, 2, 1))
    sgu_wTs = np.zeros((4, 32, 32), np.float32)
    for s_ in range(4):
        sgu_wTs[:, s_ * 8:(s_ + 1) * 8, s_ * 8:(s_ + 1) * 8] = sgu_wT[:, :8, :8]
    sgu_b0 = f(sgu_b)[0]
    lam = np.stack([f(lambda_q1)[0], f(lambda_k1)[0], f(lambda_q2)[0], f(lambda_k2)[0]])
    dsk = f(ssm_d)[0].reshape(64, 16)
    d_skip = np.ascontiguousarray(np.tile(dsk.T[None, :, :], (8, 1, 1)).reshape(128, 64))
    bre_ = np.ascontiguousarray(f(ssm_b_re)[0].transpose(1, 0, 2))
    bim_ = np.ascontiguousarray(f(ssm_b_im)[0].transpose(1, 0, 2))
    cre_ = np.ascontiguousarray(f(ssm_c_re)[0].transpose(2, 0, 1))
    cim_ = np.ascontiguousarray(f(ssm_c_im)[0].transpose(2, 0, 1))
    common = {
        "cache_k": ck, "cache_v": cv,
        "gam_mix": gam(norm_mix), "gam_ffn": gam(norm_ffn), "gam_fin": gam(f(norm_final)[None])[0],
        "w_in": f(w_in_even)[0], "w_out": f(w_out_even)[0],
        "sgu_norm": f(sgu_norm)[0][None, :], "sgu_wT": sgu_wT, "sgu_wTs": sgu_wTs,
        "sgu_b": sgu_b0, "sgu_bs": np.ascontiguousarray(np.tile(sgu_b0[:, :8], (1, 4))),
        "lam": lam, "subln": f(attn_subln)[0][None, :],
        "a_re": np.ascontiguousarray(np.tile(f(ssm_a_re)[0].T, (2, 1))), "a_im": np.ascontiguousarray(np.tile(f(ssm_a_im)[0].T, (2, 1))),
        "logdt": np.ascontiguousarray(np.tile(f(ssm_log_dt)[0][None, :], (128, 1))),
        "b_A": np.concatenate([bre_, bim_], 0), "b_B": np.concatenate([bim_, bre_], 0),
        "c_A": np.concatenate([cre_, cre_], 0), "c_B": np.concatenate([cim_, cim_], 0),
        "d_skip": d_skip,
        "w_glu": f(w_glu)[0], "b_glu": np.ascontiguousarray(f(b_glu)[0].reshape(16, 128).T),
        "w_f1": f(w_ffn_in), "w_f2": f(w_ffn_out),
    }
    in_maps = []
    for c in range(8):
        b, half = c // 2, c % 2
        m = dict(common)
        m["x_own"] = x_prompt[b, half * NT:(half + 1) * NT]
        m["x_pred"] = x_prompt[b, 0:NT] if half == 1 else np.zeros((NT, D), np.float32)
        m["x_smp"] = x_sample[4 * c:4 * c + 4].reshape(NS, D)
        m["pbias"] = np.full((128, 1), 0.0 if half == 1 else -30000.0, np.float32)
        m["ptab"] = f(page_table)[4 * c:4 * c + 4].astype(np.int32)
        sre_ = np.ascontiguousarray(f(state_ssm_re)[0, 4 * c:4 * c + 4].transpose(2, 0, 1))
        sim_ = np.ascontiguousarray(f(state_ssm_im)[0, 4 * c:4 * c + 4].transpose(2, 0, 1))
        m["st_s"] = np.concatenate([sre_, sim_], 0)
        m["st_sw"] = np.concatenate([sim_, sre_], 0)
        in_maps.append(m)
    return in_maps


def kernel(**inputs):
    in_maps = _prep(**inputs)
    n_rows = in_maps[0]["cache_k"].shape[0]
    if n_rows not in _NC_CACHE:
        _NC_CACHE[n_rows] = build_program(n_rows)
    nc = _NC_CACHE[n_rows]
    res = run_bass_kernel_spmd(nc, in_maps, core_ids=list(range(8))).results

    B, S = 4, 4096
    y_prompt = np.zeros((B, S, D), np.float32)
    y_sample = np.zeros((32, 8, D), np.float32)
    k_p = np.zeros((1, B, S, 4, 128), np.float32)
    v_p = np.zeros((1, B, S, 4, 128), np.float32)
    k_s = np.zeros((1, 32, 8, 4, 128), np.float32)
    v_s = np.zeros((1, 32, 8, 4, 128), np.float32)
    cvs = np.zeros((1, 32, 8, 512), np.float32)
    sp_re = np.zeros((1, B, 64, 64), np.float32)
    sp_im = np.zeros((1, B, 64, 64), np.float32)
    ss_re = np.zeros((1, 32, 64, 64), np.float32)
    ss_im = np.zeros((1, 32, 64, 64), np.float32)
    for c in range(8):
        b, half = c // 2, c % 2
        r = res[c]
        sl = slice(half * NT, (half + 1) * NT)
        y_prompt[b, sl] = r["y_p"]
        y_sample[4 * c:4 * c + 4] = r["y_s"].reshape(4, 8, D)
        k_p[0, b, sl] = r["kr_p"].reshape(NT, 4, 128)
        v_p[0, b, sl] = r["vr_p"].reshape(NT, 4, 128)
        k_s[0, 4 * c:4 * c + 4] = r["kr_s"].reshape(4, 8, 4, 128)
        v_s[0, 4 * c:4 * c + 4] = r["vr_s"].reshape(4, 8, 4, 128)
        cvs[0, 4 * c:4 * c + 4] = r["cv_s"].reshape(4, 8, 512)
        if half == 1:
            sp_re[0, b] = r["so_p"][0:64].T
            sp_im[0, b] = r["so_p"][64:128].T
        ss_re[0, 4 * c:4 * c + 4] = r["so_s"][0:64].transpose(1, 2, 0)
        ss_im[0, 4 * c:4 * c + 4] = r["so_s"][64:128].transpose(1, 2, 0)
    return (y_prompt, y_sample, k_p, v_p, k_s, v_s, cvs, sp_re, sp_im, ss_re, ss_im)
```
